# Optimizing a Trainium2 kernel written in Bass

```python
import math, functools
import jax, jax.numpy as jnp
from jax import lax
import numpy as np


D_MODEL = 1024
BATCH = 4
SEQ = 4096
DEPTH = 4
DEC_BATCH = 128
DEC_SEQ = 8
PAST_LEN = 8192
PAGE_SIZE = 128

MIX_DIM = D_MODEL
POOL_WINDOWS = (2, 4, 8, 16)
POOL_DIM = MIX_DIM // 4
POOL_GROUP = POOL_DIM // len(POOL_WINDOWS)
POOL_HIST = max(POOL_WINDOWS) - 1
SSD_HEADDIM = 64
SSD_HEADS = (MIX_DIM * 3 // 8) // SSD_HEADDIM
SSD_DIM = SSD_HEADS * SSD_HEADDIM
SSD_GROUPS = 2
SSD_STATE = 64
SSD_CONV = 4
SSD_CONV_DIM = SSD_DIM + 2 * SSD_GROUPS * SSD_STATE
SSD_CHUNK = 128
MLA_HEADS = 6
MLA_NOPE = 64
MLA_ROPE = 32
MLA_V = 64
MLA_Q_LORA = 256
MLA_KV_LORA = 128
MLA_DIM = MLA_HEADS * MLA_V
MLA_SCALE = (MLA_NOPE + MLA_ROPE) ** -0.5
ROPE_THETA = 10000.0
Q_BLOCK = 128
FFN_DIM = 4 * D_MODEL
NORM_EPS = 1e-6
IN_WIDTHS = (POOL_DIM, SSD_DIM, SSD_CONV_DIM, SSD_HEADS, MLA_Q_LORA, MLA_KV_LORA, MLA_ROPE)
IN_DIM = sum(IN_WIDTHS)

kernel_name = 'pool_ssd_mla_hybrid_step'


def rmsnorm(x, g):
    xf = x.astype(jnp.float32)
    xf = xf * lax.rsqrt(jnp.mean(xf * xf, axis=-1, keepdims=True) + NORM_EPS)
    return (xf * g.astype(jnp.float32)).astype(x.dtype)


def rope(x, pos):
    half = MLA_ROPE // 2
    inv = ROPE_THETA ** (-jnp.arange(half, dtype=jnp.float32) * (2.0 / MLA_ROPE))
    ang = pos.astype(jnp.float32)[:, None] * inv[None, :]
    shape = (1, pos.shape[0]) + (1,) * (x.ndim - 3) + (half,)
    cos = jnp.cos(ang).reshape(shape)
    sin = jnp.sin(ang).reshape(shape)
    xf = x.astype(jnp.float32)
    x1, x2 = xf[..., :half], xf[..., half:]
    return jnp.concatenate([x1 * cos - x2 * sin, x1 * sin + x2 * cos], axis=-1).astype(x.dtype)


def split_in_proj(proj):
    parts, start = [], 0
    for w in IN_WIDTHS:
        parts.append(proj[..., start:start + w])
        start += w
    return parts


def pool_mix(u, hist, pos, pool_w, pool_scale):
    L = u.shape[1]
    ext = jnp.concatenate([hist.astype(u.dtype), u], axis=1)
    extf = ext.astype(jnp.float32)
    cs = jnp.pad(jnp.cumsum(extf, axis=1), ((0, 0), (1, 0), (0, 0)))
    uf = extf[:, POOL_HIST:]
    outs = []
    for g, w in enumerate(POOL_WINDOWS):
        c0, c1 = g * POOL_GROUP, (g + 1) * POOL_GROUP
        hi = cs[:, POOL_HIST + 1:POOL_HIST + 1 + L, c0:c1]
        lo = cs[:, POOL_HIST + 1 - w:POOL_HIST + 1 - w + L, c0:c1]
        cnt = jnp.minimum(w, pos + 1).astype(jnp.float32)[None, :, None]
        m = (hi - lo) / cnt - uf[..., c0:c1]
        outs.append(m @ pool_w[g].astype(jnp.float32))
    out = jnp.concatenate(outs, axis=-1) * pool_scale.astype(jnp.float32)
    return out.astype(u.dtype), ext[:, -POOL_HIST:]


def causal_dwconv(x, hist, w, b):
    ext = jnp.concatenate([hist.astype(x.dtype), x], axis=1)
    y = lax.conv_general_dilated(ext, w[:, None, :].astype(x.dtype), (1,), 'VALID',
                                 dimension_numbers=('NWC', 'WIO', 'NWC'),
                                 feature_group_count=x.shape[-1])
    return y + b.astype(x.dtype), ext[:, -(SSD_CONV - 1):]


def ssd_scan(x, dt, A, Bm, Cm, h0):
    b, l, H, P = x.shape
    G, N = Bm.shape[2], Bm.shape[3]
    R = H // G
    q = min(SSD_CHUNK, l)
    c = l // q
    xc = x.reshape(b, c, q, G, R, P)
    dtc = dt.reshape(b, c, q, G, R)
    Bc = Bm.reshape(b, c, q, G, N)
    Cc = Cm.reshape(b, c, q, G, N)
    a = jnp.moveaxis(dtc * A.reshape(G, R), 2, -1)
    a_cs = jnp.cumsum(a, axis=-1)
    seg = a_cs[..., :, None] - a_cs[..., None, :]
    causal = jnp.tril(jnp.ones((q, q), dtype=bool))
    Lmat = jnp.exp(jnp.where(causal, seg, -jnp.inf))
    xdt = xc * dtc[..., None]
    cb = jnp.einsum('bclgn,bcsgn->bcgls', Cc, Bc)
    M = cb[:, :, :, None] * Lmat
    y_diag = jnp.einsum('bcgrls,bcsgrp->bclgrp', M, xdt)
    decay_states = jnp.exp(a_cs[..., -1:] - a_cs)
    states = jnp.einsum('bcsgn,bcgrs,bcsgrp->bcgrpn', Bc, decay_states, xdt)
    chunk_decay = jnp.exp(a_cs[..., -1])

    def step(h, inp):
        st, dec = inp
        return h * dec[..., None, None] + st, h

    hT, h_prev = lax.scan(step, h0.reshape(b, G, R, P, N),
                          (jnp.moveaxis(states, 1, 0), jnp.moveaxis(chunk_decay, 1, 0)))
    h_prev = jnp.moveaxis(h_prev, 0, 1)
    y_off = jnp.einsum('bclgn,bcgrpn,bcgrl->bclgrp', Cc, h_prev, jnp.exp(a_cs))
    y = (y_diag + y_off).reshape(b, l, H, P)
    return y, hT.reshape(b, H, P, N)


def ssd_mix(z, xbc, dt_raw, conv_hist, h0, p):
    b, L, _ = z.shape
    xbc, new_conv = causal_dwconv(xbc, conv_hist, p['conv_w'], p['conv_b'])
    xbc = jax.nn.silu(xbc.astype(jnp.float32))
    xs = xbc[..., :SSD_DIM].reshape(b, L, SSD_HEADS, SSD_HEADDIM)
    Bm = xbc[..., SSD_DIM:SSD_DIM + SSD_GROUPS * SSD_STATE].reshape(b, L, SSD_GROUPS, SSD_STATE)
    Cm = xbc[..., SSD_DIM + SSD_GROUPS * SSD_STATE:].reshape(b, L, SSD_GROUPS, SSD_STATE)
    dt = jax.nn.softplus(dt_raw.astype(jnp.float32) + p['dt_bias'].astype(jnp.float32))
    A = -jnp.exp(p['a_log'].astype(jnp.float32))
    y, hT = ssd_scan(xs, dt, A, Bm, Cm, h0.astype(jnp.float32))
    y = y + p['d_skip'].astype(jnp.float32)[:, None] * xs
    y = y.reshape(b, L, SSD_DIM) * jax.nn.silu(z.astype(jnp.float32))
    return rmsnorm(y, p['ssd_norm_g']).astype(z.dtype), new_conv, hT.astype(h0.dtype)


def mla_prompt_attn(q_lat, q_pe, ckv, kpe):
    b, S, H, C = q_lat.shape
    qb = min(Q_BLOCK, S)
    kpos = jnp.arange(S)

    def blk(i):
        s0 = i * qb
        ql = lax.dynamic_slice_in_dim(q_lat, s0, qb, axis=1)
        qp = lax.dynamic_slice_in_dim(q_pe, s0, qb, axis=1)
        sc = (jnp.einsum('bthc,bsc->bhts', ql, ckv) +
              jnp.einsum('bthr,bsr->bhts', qp, kpe)).astype(jnp.float32) * MLA_SCALE
        qpos = s0 + jnp.arange(qb)
        sc = jnp.where(kpos[None, :] <= qpos[:, None], sc, -jnp.inf)
        pr = jax.nn.softmax(sc, axis=-1).astype(ckv.dtype)
        return jnp.einsum('bhts,bsc->bthc', pr, ckv)

    o = lax.map(blk, jnp.arange(S // qb))
    return jnp.moveaxis(o, 0, 1).reshape(b, S, H, C)


def mla_sample_attn(q_lat, q_pe, ckv, kpe, ckv_past, kpe_past):
    t = q_lat.shape[1]
    sc_past = (jnp.einsum('bthc,bsc->bhts', q_lat, ckv_past) +
               jnp.einsum('bthr,bsr->bhts', q_pe, kpe_past)).astype(jnp.float32) * MLA_SCALE
    sc_new = (jnp.einsum('bthc,bsc->bhts', q_lat, ckv) +
              jnp.einsum('bthr,bsr->bhts', q_pe, kpe)).astype(jnp.float32) * MLA_SCALE
    sc_new = jnp.where(jnp.tril(jnp.ones((t, t), dtype=bool)), sc_new, -jnp.inf)
    pr = jax.nn.softmax(jnp.concatenate([sc_past, sc_new], axis=-1), axis=-1).astype(ckv.dtype)
    P = ckv_past.shape[1]
    return (jnp.einsum('bhts,bsc->bthc', pr[..., :P], ckv_past) +
            jnp.einsum('bhts,bsc->bthc', pr[..., P:], ckv))


def mla_mix(cq, ckv, kpe, pos, p, attend):
    b, L, _ = cq.shape
    q = (rmsnorm(cq, p['q_norm_g']) @ p['w_uq']).reshape(b, L, MLA_HEADS, MLA_NOPE + MLA_ROPE)
    q_nope = q[..., :MLA_NOPE]
    q_pe = rope(q[..., MLA_NOPE:], pos)
    ckv = rmsnorm(ckv, p['kv_norm_g'])
    kpe = rope(kpe, pos)
    q_lat = jnp.einsum('blhd,chd->blhc', q_nope, p['w_uk'])
    o_lat = attend(q_lat, q_pe, ckv, kpe)
    o = jnp.einsum('blhc,chd->blhd', o_lat, p['w_uv']).reshape(b, L, MLA_DIM)
    return rmsnorm(o, p['mla_out_g']), ckv, kpe


def mixer(h, p, pos0, pool_hist, conv_hist, ssm_h0, attend):
    L = h.shape[1]
    pos = pos0 + jnp.arange(L, dtype=jnp.int32)
    u, z, xbc, dt_raw, cq, ckv, kpe = split_in_proj(h @ p['w_in'])
    pool_out, new_pool = pool_mix(u, pool_hist, pos, p['pool_w'], p['pool_scale'])
    ssd_out, new_conv, new_ssm = ssd_mix(z, xbc, dt_raw, conv_hist, ssm_h0, p)
    mla_out, ckv_n, kpe_r = mla_mix(cq, ckv, kpe, pos, p, attend)
    y = jnp.concatenate([pool_out, ssd_out.astype(h.dtype), mla_out], axis=-1) @ p['w_out']
    return y, ckv_n, kpe_r, new_pool, new_conv, new_ssm


def ffn(h, w_up, w_down):
    a = jax.nn.relu(h @ w_up)
    return (a * a) @ w_down


def setup_inputs(seed: int = 0) -> dict:
    key = jax.random.key(seed)
    ks = iter(jax.random.split(key, 40))
    f32 = jnp.float32
    n_pages = PAST_LEN // PAGE_SIZE
    used = DEC_BATCH * n_pages
    n_phys = used + max(1, used // 4)

    def nrm(shape, scale):
        return jax.random.normal(next(ks), shape, f32) * scale

    def gain(shape):
        return 1.0 + 0.1 * jax.random.normal(next(ks), shape, f32)

    x_prompt = nrm((BATCH, SEQ, D_MODEL), 1.0)
    x_sample = nrm((DEC_BATCH, DEC_SEQ, D_MODEL), 1.0)
    cache_kv_latent = nrm((DEPTH, n_phys, PAGE_SIZE, MLA_KV_LORA), 1.0)
    cache_k_rope = nrm((DEPTH, n_phys, PAGE_SIZE, MLA_ROPE), 1.0)
    state_pool = nrm((DEPTH, DEC_BATCH, POOL_HIST, POOL_DIM), 1.0)
    state_conv = nrm((DEPTH, DEC_BATCH, SSD_CONV - 1, SSD_CONV_DIM), 1.0)
    state_ssm = nrm((DEPTH, DEC_BATCH, SSD_HEADS, SSD_HEADDIM, SSD_STATE), 0.3)
    page_table = jax.random.permutation(next(ks), n_phys)[:used].reshape(DEC_BATCH, n_pages).astype(jnp.int32)

    dt0 = jnp.exp(jax.random.uniform(next(ks), (DEPTH, SSD_HEADS), f32) *
                  (math.log(0.1) - math.log(1e-3)) + math.log(1e-3))
    dt_bias = dt0 + jnp.log(-jnp.expm1(-dt0))
    a_log = jnp.log(jax.random.uniform(next(ks), (DEPTH, SSD_HEADS), f32, 1.0, 16.0))

    return {
        'x_prompt': x_prompt,
        'x_sample': x_sample,
        'cache_kv_latent': cache_kv_latent,
        'cache_k_rope': cache_k_rope,
        'state_pool': state_pool,
        'state_conv': state_conv,
        'state_ssm': state_ssm,
        'page_table': page_table,
        'norm_mix_g': gain((DEPTH, D_MODEL)),
        'w_in': nrm((DEPTH, D_MODEL, IN_DIM), D_MODEL ** -0.5),
        'pool_w': nrm((DEPTH, len(POOL_WINDOWS), POOL_GROUP, POOL_GROUP), POOL_GROUP ** -0.5),
        'pool_scale': gain((DEPTH, POOL_DIM)),
        'conv_w': nrm((DEPTH, SSD_CONV, SSD_CONV_DIM), SSD_CONV ** -0.5),
        'conv_b': nrm((DEPTH, SSD_CONV_DIM), 0.02),
        'dt_bias': dt_bias,
        'a_log': a_log,
        'd_skip': gain((DEPTH, SSD_HEADS)),
        'ssd_norm_g': gain((DEPTH, SSD_DIM)),
        'q_norm_g': gain((DEPTH, MLA_Q_LORA)),
        'w_uq': nrm((DEPTH, MLA_Q_LORA, MLA_HEADS * (MLA_NOPE + MLA_ROPE)), MLA_Q_LORA ** -0.5),
        'kv_norm_g': gain((DEPTH, MLA_KV_LORA)),
        'w_uk': nrm((DEPTH, MLA_KV_LORA, MLA_HEADS, MLA_NOPE), MLA_KV_LORA ** -0.5),
        'w_uv': nrm((DEPTH, MLA_KV_LORA, MLA_HEADS, MLA_V), MLA_KV_LORA ** -0.5),
        'mla_out_g': gain((DEPTH, MLA_DIM)),
        'w_out': nrm((DEPTH, MIX_DIM, D_MODEL), MIX_DIM ** -0.5),
        'norm_ffn_g': gain((DEPTH, D_MODEL)),
        'w_up': nrm((DEPTH, D_MODEL, FFN_DIM), D_MODEL ** -0.5),
        'w_down': nrm((DEPTH, FFN_DIM, D_MODEL), FFN_DIM ** -0.5),
        'final_norm_g': gain((D_MODEL,)),
    }


def reference(x_prompt, x_sample, cache_kv_latent, cache_k_rope, state_pool, state_conv, state_ssm,
              page_table, norm_mix_g, w_in, pool_w, pool_scale, conv_w, conv_b, dt_bias, a_log,
              d_skip, ssd_norm_g, q_norm_g, w_uq, kv_norm_g, w_uk, w_uv, mla_out_g, w_out,
              norm_ffn_g, w_up, w_down, final_norm_g):
    past_len = page_table.shape[1] * PAGE_SIZE
    bp = x_prompt.shape[0]
    db = x_sample.shape[0]
    xp, xs = x_prompt, x_sample
    p_kv, p_kr, p_pool, p_conv, p_ssm = [], [], [], [], []
    s_kv, s_kr, s_pool, s_conv, s_ssm = [], [], [], [], []
    for l in range(DEPTH):
        p = {'w_in': w_in[l], 'pool_w': pool_w[l], 'pool_scale': pool_scale[l],
             'conv_w': conv_w[l], 'conv_b': conv_b[l], 'dt_bias': dt_bias[l], 'a_log': a_log[l],
             'd_skip': d_skip[l], 'ssd_norm_g': ssd_norm_g[l], 'q_norm_g': q_norm_g[l],
             'w_uq': w_uq[l], 'kv_norm_g': kv_norm_g[l], 'w_uk': w_uk[l], 'w_uv': w_uv[l],
             'mla_out_g': mla_out_g[l], 'w_out': w_out[l]}

        zp_pool = jnp.zeros((bp, POOL_HIST, POOL_DIM), xp.dtype)
        zp_conv = jnp.zeros((bp, SSD_CONV - 1, SSD_CONV_DIM), xp.dtype)
        zp_ssm = jnp.zeros((bp, SSD_HEADS, SSD_HEADDIM, SSD_STATE), xp.dtype)
        y, ckv, kpe, npool, nconv, nssm = mixer(rmsnorm(xp, norm_mix_g[l]), p, 0,
                                                zp_pool, zp_conv, zp_ssm, mla_prompt_attn)
        xp = xp + y
        xp = xp + ffn(rmsnorm(xp, norm_ffn_g[l]), w_up[l], w_down[l])
        p_kv.append(ckv); p_kr.append(kpe); p_pool.append(npool); p_conv.append(nconv); p_ssm.append(nssm)

        ckv_past = jnp.take(cache_kv_latent[l], page_table, axis=0).reshape(db, past_len, MLA_KV_LORA)
        kpe_past = jnp.take(cache_k_rope[l], page_table, axis=0).reshape(db, past_len, MLA_ROPE)
        attend = functools.partial(mla_sample_attn, ckv_past=ckv_past.astype(xs.dtype),
                                   kpe_past=kpe_past.astype(xs.dtype))
        y, ckv, kpe, npool, nconv, nssm = mixer(rmsnorm(xs, norm_mix_g[l]), p, past_len,
                                                state_pool[l], state_conv[l], state_ssm[l], attend)
        xs = xs + y
        xs = xs + ffn(rmsnorm(xs, norm_ffn_g[l]), w_up[l], w_down[l])
        s_kv.append(ckv); s_kr.append(kpe); s_pool.append(npool); s_conv.append(nconv); s_ssm.append(nssm)

    y_prompt = rmsnorm(xp, final_norm_g)
    y_sample = rmsnorm(xs, final_norm_g)
    return (y_prompt, y_sample,
            jnp.stack(p_kv), jnp.stack(p_kr), jnp.stack(p_pool), jnp.stack(p_conv), jnp.stack(p_ssm),
            jnp.stack(s_kv), jnp.stack(s_kr), jnp.stack(s_pool), jnp.stack(s_conv), jnp.stack(s_ssm))
```

```python
import numpy as np
import ml_dtypes
import concourse.bass as bass
import concourse.mybir as mybir
from concourse.bass_utils import run_bass_kernel_spmd
WITH_SAMPLE = True
F32 = mybir.dt.float32
BF16 = mybir.dt.bfloat16
I32 = mybir.dt.int32
AF = mybir.ActivationFunctionType
ALU = mybir.AluOpType

NDMA_SEM = 8


class _Op:
    __slots__ = ("eng", "fn", "deps", "dma", "sig", "signal", "idx", "dsem", "dval")

    def __init__(self, eng, fn, dma):
        self.eng = eng
        self.fn = fn
        self.dma = dma
        self.deps = ()
        self.sig = 0
        self.signal = False
        self.dsem = None
        self.dval = 0


class Prog:
    ENGS = ("pe", "act", "dve", "pool", "sp")

    def __init__(self, nc):
        self.nc = nc
        self.ops = {e: [] for e in self.ENGS}
        self.last_w = {}
        self.readers = {}
        self.ndma = {e: 0 for e in self.ENGS}
        self._cms = []
        self.nops = 0

    def sb(self, name, shape, dt):
        cm = self.nc.sbuf_tensor(name, list(shape), dt)
        t = cm.__enter__()
        self._cms.append(cm)
        return t

    def ps(self, name, shape, dt=F32):
        cm = self.nc.psum_tensor(name, list(shape), dt)
        t = cm.__enter__()
        self._cms.append(cm)
        return t

    @staticmethod
    def key(ap):
        return ap.tensor.name

    def rec(self, eng, fn, reads, writes, dma=False, rkeys=(), wkeys=()):
        op = _Op(eng, fn, dma)
        rk = set(self.key(a) for a in reads if a is not None)
        rk.update(rkeys)
        wk = set(self.key(a) for a in writes if a is not None)
        wk.update(wkeys)
        for k in list(rk):
            if k.startswith("pb"):
                wk.add(k)
        deps = set()
        for k in rk:
            w = self.last_w.get(k)
            if w is not None:
                deps.add(w)
        for k in wk:
            w = self.last_w.get(k)
            if w is not None:
                deps.add(w)
            for r in self.readers.get(k, ()):
                deps.add(r)
        deps.discard(op)
        if eng == "pe" and not dma:
            deps = set(d for d in deps if not (d.eng == "pe" and not d.dma))
        for d in deps:
            d.signal = True
        op.deps = tuple(deps)
        for k in wk:
            self.last_w[k] = op
            self.readers[k] = []
        for k in rk:
            if k in wk:
                continue
            lst = self.readers.setdefault(k, [])
            if not dma:
                lst[:] = [r for r in lst if r.dma or r.eng != eng]
            lst.append(op)
        op.idx = self.nops
        self.nops += 1
        self.ops[eng].append(op)
        return op

    def finish(self, final_wait_ops=()):
        nc = self.nc
        sem_cms = []

        def mksem(name):
            cm = nc.semaphore(name)
            s = cm.__enter__()
            sem_cms.append(cm)
            return s

        esem = {e: mksem("s_" + e) for e in ("pe", "act", "dve", "pool")}
        dsems = {e: [mksem("d_%s%d" % (e, i)) for i in range(NDMA_SEM)] for e in ("sp", "pool", "act")}
        for e in self.ENGS:
            c = 0
            nd = 0
            for op in self.ops[e]:
                if op.dma:
                    op.dsem = dsems[e][nd % NDMA_SEM]
                    op.dval = 16 * (nd // NDMA_SEM + 1)
                    nd += 1
                else:
                    if op.signal:
                        c += 1
                        op.sig = c
        self.stats = {e: len(self.ops[e]) for e in self.ENGS}

        def emit_engine(ename, eng):
            waited = {}

            def wait(sem, val):
                k = sem.name
                if waited.get(k, 0) >= val:
                    return
                waited[k] = val
                eng.wait_ge(sem, val)

            for op in self.ops[ename]:
                need = {}
                for d in op.deps:
                    if d.dma:
                        s, v = d.dsem, d.dval
                    else:
                        s, v = esem[d.eng], d.sig
                    if need.get(s.name, (None, 0))[1] < v:
                        need[s.name] = (s, v)
                if op.dma and op.dval > 16:
                    s, v = op.dsem, op.dval - 16
                    if need.get(s.name, (None, 0))[1] < v:
                        need[s.name] = (s, v)
                for s, v in need.values():
                    wait(s, v)
                inst = op.fn(eng)
                if op.dma:
                    inst.then_inc(op.dsem, 16)
                elif op.signal:
                    inst.then_inc(esem[ename], 1)
            for op in final_wait_ops:
                if op.eng == ename:
                    wait(op.dsem, op.dval)

        with nc.Block() as block:
            @block.tensor
            def _(e):
                emit_engine("pe", e)

            @block.scalar
            def _(e):
                emit_engine("act", e)

            @block.vector
            def _(e):
                emit_engine("dve", e)

            @block.gpsimd
            def _(e):
                emit_engine("pool", e)

            @block.sync
            def _(e):
                emit_engine("sp", e)

        for cm in reversed(sem_cms):
            cm.__exit__(None, None, None)
        for cm in reversed(self._cms):
            cm.__exit__(None, None, None)

    def dma(self, out, in_, q="sp", **kw):
        return self.rec(q, lambda e: e.dma_start(out=out, in_=in_, **kw), [in_], [out], dma=True)

    def mm(self, out, lhsT, rhs, start=True, stop=True, **kw):
        return self.rec("pe", lambda e: e.matmul(out, lhsT, rhs, start=start, stop=stop, **kw),
                        [lhsT, rhs], [out])

    def tr(self, out, in_, ident):
        return self.rec("pe", lambda e: e.transpose(out, in_, ident), [in_, ident], [out])

    def act(self, out, in_, func, bias=None, scale=None, accum_out=None, eng="act"):
        kw = {}
        rd = [in_]
        if bias is not None:
            kw["bias"] = bias
            if not isinstance(bias, (int, float)):
                rd.append(bias)
        if scale is not None:
            kw["scale"] = scale
            if not isinstance(scale, (int, float)):
                rd.append(scale)
        wr = [out]
        if accum_out is not None:
            kw["accum_out"] = accum_out
            wr.append(accum_out)
        return self.rec("act", lambda e: e.activation(out, in_, func, **kw), rd, wr)

    def tt(self, out, in0, in1, op, eng="dve"):
        return self.rec(eng, lambda e: e.tensor_tensor(out, in0, in1, op), [in0, in1], [out])

    def ts(self, out, in0, s1, s2, op0, op1=None, eng="dve", accum_out=None):
        rd = [in0]
        for s in (s1, s2):
            if s is not None and not isinstance(s, (int, float)):
                rd.append(s)
        kw = {}
        wr = [out]
        if accum_out is not None:
            kw["accum_out"] = accum_out
            wr.append(accum_out)
        if op1 is None:
            return self.rec(eng, lambda e: e.tensor_scalar(out, in0, s1, s2, op0, **kw), rd, wr)
        return self.rec(eng, lambda e: e.tensor_scalar(out, in0, s1, s2, op0, op1, **kw), rd, wr)

    def stt(self, out, in0, scalar, in1, op0, op1, accum_out=None):
        rd = [in0, in1]
        if not isinstance(scalar, (int, float)):
            rd.append(scalar)
        kw = {}
        wr = [out]
        if accum_out is not None:
            kw["accum_out"] = accum_out
            wr.append(accum_out)
        return self.rec("dve", lambda e: e.scalar_tensor_tensor(out, in0, scalar, in1, op0, op1, **kw), rd, wr)

    def copy(self, out, in_, eng="dve"):
        if eng == "act":
            return self.rec("act", lambda e: e.copy(out, in_), [in_], [out])
        return self.rec(eng, lambda e: e.tensor_copy(out, in_), [in_], [out])

    def memset(self, ap, val, eng="pool"):
        return self.rec(eng, lambda e: e.memset(ap, val), [], [ap])

    def recip(self, out, in_):
        return self.rec("dve", lambda e: e.reciprocal(out, in_), [in_], [out])
D = 1024
T = 512
SLOT = 5120
NP = 24
NRING = 4
NGEN = 30
GW = 640
EPS = 1e-6
MLA_SCALE = 96 ** -0.5
POOL_W = (2, 4, 8, 16)
NSEQ = 16
NTS = 8

CF_PER_LAYER = 57 + 140
O_GMIX, O_GFFN, O_PSC, O_CW, O_CB, O_DSK, O_SSDG, O_QG, O_MLAG = 0, 8, 16, 18, 38, 43, 46, 49, 51
O_KVG, O_DTB, O_ALOG = 57, 185, 191


def cf_globals(depth):
    base = depth * CF_PER_LAYER
    o = {}
    for name, n in (("FING", 8), ("INVW", 2), ("U", 128), ("LS", 128), ("UB", 128), ("LSB", 128),
                    ("ONES", 128), ("INVC", 30), ("SEQM", 16), ("PARS", 2), ("PAIRM", 8), ("LASTM", 128)):
        o[name] = base
        base += n
    o["NCF"] = base
    return o


CB_ID, CB_ONES, CB_U, CB_PMASK, CB_NMASK, CB_DUP0, CB_DUP1, CB_PARL, NCB = 0, 128, 256, 384, 480, 1248, 1376, 1504, 1632
W_UQ, W_UK, W_UV, W_PW = 0, 1536, 1920, 2304


class TilePool:
    def __init__(self, P, n, prefix, shape, dt, psum=False):
        alloc = P.ps if psum else P.sb
        self.tiles = [alloc("%s%d" % (prefix, i), shape, dt) for i in range(n)]
        self.free_list = list(self.tiles)

    def get(self):
        assert self.free_list, "tile pool exhausted"
        return self.free_list.pop(0)

    def put(self, *ts):
        for t in ts:
            assert all(t is not f for f in self.free_list)
            self.free_list.append(t)


def gf(t, n, p0=0, p1=128):
    return t[p0:p1, 0:n]


def gb(t, n, p0=0, p1=128):
    return t[p0:p1, 0:(n + 1) // 2].bitcast(BF16)[:, 0:n]


class StopBuild(Exception):
    pass


def build_program(S, depth, n_pages, n_phys, with_sample=True):
    import os
    KSTAGE = float(os.environ.get("KSTAGE", "99"))

    def stage(k):
        if KSTAGE <= k:
            raise StopBuild()
    NB = S // T
    NTOK = S + 128
    NTILE = S // 128 + 1
    CG = cf_globals(depth)
    NCF = CG["NCF"]
    nc = bass.Bass("TRN2", target_bir_lowering=False)

    def din(name, shape, dt=F32):
        return nc.dram_tensor(name, list(shape), dt, kind="ExternalInput")

    def dout(name, shape, dt=F32):
        return nc.dram_tensor(name, list(shape), dt, kind="ExternalOutput")

    xT_in = din("xT_in", [D, NTOK])
    wimg = din("wimg", [depth * NP * 128, SLOT])
    cf_d = din("cf", [128, NCF])
    cb_d = din("cb", [128, NCB], BF16)
    ropeq = din("ropeq", [192, NTOK])
    ropek = din("ropek", [128, 2 * NTILE * 16])
    yT = dout("yT", [D, NTOK])
    o_pkv = dout("o_pkv", [depth, S, 128])
    o_pkr = dout("o_pkr", [depth, S, 32])
    o_ppool = dout("o_ppool", [depth, 256, 15])
    o_pconv = dout("o_pconv", [depth, 640, 3])
    o_pssm = dout("o_pssm", [depth, 6, 64, 64])
    if with_sample:
        assert n_pages == 64
        ckvc = [din("ckvc%d" % l, [n_phys, 16384]) for l in range(depth)]
        ckrc = [din("ckrc%d" % l, [n_phys, 4096]) for l in range(depth)]
        pt_d = din("pt", [128, 8], I32)
        spool = din("spool", [depth, 256, 16, 15])
        sconv = din("sconv", [depth, 640, 16, 3])
        sssm = din("sssm", [depth, 8, 2, 6, 64, 64])
        o_skv = dout("o_skv", [depth, 128, 128])
        o_skr = dout("o_skr", [depth, 128, 32])
        o_spool = dout("o_spool", [depth, 256, 16, 15])
        o_sconv = dout("o_sconv", [depth, 640, 16, 3])
        o_sssm = dout("o_sssm", [depth, 8, 2, 6, 64, 64])
    XT = nc.dram_tensor("XT", [D, NTOK], F32, kind="Internal")
    wb = nc.dram_tensor("wb", [depth * NP * 128, SLOT], BF16, kind="Internal")

    P = Prog(nc)
    outs = []
    cf = P.sb("cf_sb", [128, NCF], F32)
    cb = P.sb("cb_sb", [128, NCB], BF16)
    ring = [P.sb("ring%d" % i, [128, SLOT], BF16) for i in range(NRING)]
    wsm = P.sb("wsm", [128, 2560], BF16)
    xT = P.sb("xT", [128, 8 * T], F32)
    hT = P.sb("hT", [128, 8 * T], BF16)
    ckvT = P.sb("ckvT", [128, S + 128], BF16)
    ckv_tok = P.sb("ckv_tok", [128, (S // 128) * 128], BF16)
    kpeT3 = P.sb("kpeT3", [96, S + 128], BF16)
    uhist = P.sb("uhist", [128, 2 * 15], F32)
    xhist = P.sb("xhist", [128, 5 * 3], F32)
    xdt_pad = [P.sb("xdtp%d" % i, [128, 6 * 128], BF16) for i in range(2)]
    hTp = P.sb("hTp", [128, 6 * 128], BF16)
    hTf = P.sb("hTf", [128, 6 * 64], F32)
    dt_tok = P.sb("dt_tok", [128, 4 * 6], F32)
    a_tok = P.sb("a_tok", [128, 4 * 6], F32)
    A_row = P.sb("A_row", [128, 6], F32)
    if with_sample:
        xg = [P.sb("xg%d" % i, [128, 1024], F32) for i in range(2)]
        pt_sb = P.sb("pt_sb", [128, 8], I32)
    GEN = TilePool(P, NGEN, "g", [128, GW], F32)
    BANK = TilePool(P, 8, "pb", [128, 512], F32, psum=True)

    ident = cb[:, CB_ID:CB_ID + 128]
    ones_b = cb[:, CB_ONES:CB_ONES + 128]
    U_b = cb[:, CB_U:CB_U + 128]
    U_f = cf[:, CG["U"]:CG["U"] + 128]
    LS_f = cf[:, CG["LS"]:CG["LS"] + 128]
    ones_f = cf[:, CG["ONES"]:CG["ONES"] + 128]
    UB_f = cf[:, CG["UB"]:CG["UB"] + 128]
    LSB_f = cf[:, CG["LSB"]:CG["LSB"] + 128]
    parS_f = cf[:, CG["PARS"]:CG["PARS"] + 2]
    lastm_f = cf[:, CG["LASTM"]:CG["LASTM"] + 128]
    pairm_f = cf[:, CG["PAIRM"]:CG["PAIRM"] + 8]
    pmask_b = cb[:, CB_PMASK:CB_PMASK + 96]
    nmask_b = cb[:, CB_NMASK:CB_NMASK + 768]
    dup_b = [cb[:, CB_DUP0:CB_DUP0 + 128], cb[:, CB_DUP1:CB_DUP1 + 128]]
    parL_b = cb[:, CB_PARL:CB_PARL + 128]

    def cfc(l, off, n=1, p0=0, p1=128):
        o = l * CF_PER_LAYER + off
        return cf[p0:p1, o:o + n]

    P.dma(cf[:], cf_d.ap())
    P.dma(cb[:], cb_d.ap())
    if with_sample:
        P.dma(pt_sb[:], pt_d.ap())
    for l in range(depth):
        for pc in range(NP):
            r0 = (l * NP + pc) * 128
            P.dma(wb.ap()[r0:r0 + 128, :], wimg.ap()[r0:r0 + 128, :], q="pool")
    for t_ in xdt_pad:
        P.memset(t_[:], 0.0)

    sched = []
    for l in range(depth):
        for b in range(NB + (1 if with_sample else 0)):
            for pc in range(23):
                sched.append((l, pc))
    rstate = {"issued": 0, "used": 0}

    def ring_issue():
        i = rstate["issued"]
        if i >= len(sched):
            return
        l, pc = sched[i]
        r0 = (l * NP + pc) * 128
        P.dma(ring[i % NRING][:], wb.ap()[r0:r0 + 128, :])
        rstate["issued"] += 1

    def ring_next(expect):
        i = rstate["used"]
        assert sched[i][1] == expect, (sched[i], expect)
        while rstate["issued"] < min(len(sched), i + NRING):
            ring_issue()
        if rstate["issued"] <= i:
            ring_issue()
        rstate["used"] += 1
        return ring[i % NRING]

    def rmsnorm_fm(srcs, gcols, dim, dsts, Pn, N):
        ps = BANK.get()
        n = len(srcs)
        for i, s in enumerate(srcs):
            sq = GEN.get()
            P.act(gb(sq, N, 0, Pn), s, AF.Square)
            P.mm(ps[0:Pn, 0:N], ones_b[0:Pn, 0:Pn], gb(sq, N, 0, Pn), start=(i == 0), stop=(i == n - 1))
            GEN.put(sq)
        rs = GEN.get()
        P.act(gf(rs, N, 0, Pn), ps[0:Pn, 0:N], AF.Sqrt, scale=1.0 / dim, bias=EPS)
        BANK.put(ps)
        P.recip(gf(rs, N, 0, Pn), gf(rs, N, 0, Pn))
        for i, s in enumerate(srcs):
            P.stt(dsts[i], s, gcols[i], gf(rs, N, 0, Pn), ALU.mult, ALU.mult)
        GEN.put(rs)

    def hTk(k, n0, n1):
        return hT[:, k * T + n0:k * T + n1]

    def xTk(k, n0, n1):
        return xT[:, k * T + n0:k * T + n1]

    def proj_fm(ps_out, w, wstride, c0, M, N):
        for k in range(8):
            P.mm(ps_out[0:M, 0:N], w[:, k * wstride + c0:k * wstride + c0 + M], hTk(k, 0, N),
                 start=(k == 0), stop=(k == 7))

    def ssd_chunk(l, ci, N0, xs_bf, BT, CT, yv, masks, sample=None):
        Uf, LSf = masks
        cc = slice(N0, N0 + 128)
        xp = xdt_pad[ci % 2]
        dtc = dt_tok[:, ci * 6:(ci + 1) * 6]
        ac = a_tok[:, ci * 6:(ci + 1) * 6]
        ptr = BANK.get()
        ptb = ptr[:, 0:256].bitcast(BF16)
        for j in range(3):
            P.tr(ptb[:, j * 128:(j + 1) * 128], gb(xs_bf[j], T)[:, cc], ident)
        P.tr(ptb[:, 384:512], gb(BT, T)[:, cc], ident)
        btok = GEN.get()
        P.copy(gb(btok, 128), ptb[:, 384:512], eng="act")
        xs3 = ptb[:, 0:384].rearrange("p (j e d) -> p j e d", j=3, e=2)
        xp4 = xp[:, :].rearrange("p (j e c) -> p j e c", j=3, e=2)
        dt3 = dtc.rearrange("p (j e) -> p j e", e=2)
        for e in range(2):
            P.tt(xp4[:, :, e, e * 64:e * 64 + 64], xs3[:, :, e, :],
                 dt3[:, :, e:e + 1].broadcast_to([128, 3, 64]), ALU.mult)
        stage(6.1)
        pcb = BANK.get()
        for g in range(2):
            P.mm(pcb[:, g * 128:(g + 1) * 128], gb(BT, T)[:, cc], gb(CT[g], T)[:, cc],
                 start=(g == 0), stop=(g == 1), skip_group_check=True)
        stage(6.15)
        cbm = GEN.get()
        P.tt(gf(cbm, 256).rearrange("p (g c) -> p g c", g=2), pcb[:, 0:256].rearrange("p (g c) -> p g c", g=2),
             Uf[:, None, :].broadcast_to([128, 2, 128]), ALU.mult)
        BANK.put(pcb)
        stage(6.2)
        aU = [GEN.get(), GEN.get()]
        for i in range(2):
            P.tt(gf(aU[i], 384).rearrange("p (h c) -> p h c", h=3),
                 Uf[:, None, :].broadcast_to([128, 3, 128]),
                 ac[:, 3 * i:3 * i + 3].unsqueeze(2).broadcast_to([128, 3, 128]), ALU.mult)
        stage(6.25)
        Lx = [GEN.get(), GEN.get()]
        Ex = [GEN.get(), GEN.get()]
        for i in range(2):
            pseg = BANK.get()
            pacs = BANK.get()
            for hh in range(3):
                rhs = gf(aU[i], 384)[:, hh * 128:(hh + 1) * 128]
                P.mm(pseg[:, hh * 128:(hh + 1) * 128], LSf, rhs, start=(hh == 0), stop=(hh == 2), skip_group_check=True)
            for hh in range(3):
                rhs = gf(aU[i], 384)[:, hh * 128:(hh + 1) * 128]
                P.mm(pacs[:, hh * 128:(hh + 1) * 128], ones_f, rhs, start=(hh == 0), stop=(hh == 2), skip_group_check=True)
            P.act(gf(Lx[i], 384), pseg[:, 0:384], AF.Exp)
            P.act(gf(Ex[i], 384), pacs[:, 0:384], AF.Exp)
            BANK.put(pseg, pacs)
        GEN.put(*aU)
        stage(6.3)
        MT = [GEN.get(), GEN.get()]
        CsT = GEN.get()
        for g in range(2):
            P.tt(gb(MT[g], 384).rearrange("p (h c) -> p h c", h=3),
                 gf(Lx[g], 384).rearrange("p (h c) -> p h c", h=3),
                 gf(cbm, 256)[:, g * 128:(g + 1) * 128].unsqueeze(1).broadcast_to([128, 3, 128]), ALU.mult, eng="pool")
            r0, r1 = 64 * g, 64 * g + 64
            P.tt(gb(CsT, 384, r0, r1).rearrange("p (h c) -> p h c", h=3),
                 gf(Ex[g], 384, r0, r1).rearrange("p (h c) -> p h c", h=3),
                 gb(CT[g], T, r0, r1)[:, cc].unsqueeze(1).broadcast_to([64, 3, 128]), ALU.mult, eng="pool")
        GEN.put(cbm)
        wdec = GEN.get()
        for g in range(2):
            if sample is None:
                P.tt(gf(wdec, 6)[:, 3 * g:3 * g + 3], dtc[:, 3 * g:3 * g + 3],
                     gf(Lx[g], 384).rearrange("p (h c) -> p h c", h=3)[:, :, 127], ALU.mult)
            else:
                tmpm = GEN.get()
                tv_ = gf(tmpm, 384).rearrange("p (h c) -> p h c", h=3)
                P.tt(tv_, gf(Lx[g], 384).rearrange("p (h c) -> p h c", h=3),
                     lastm_f.unsqueeze(1).broadcast_to([128, 3, 128]), ALU.mult, eng="pool")
                wl = gf(tmpm, 400)[:, 392:395]
                P.rec("dve", lambda e, o=wl, i=tv_: e.reduce_sum(o, i, mybir.AxisListType.X), [tv_], [wl])
                P.tt(gf(wdec, 6)[:, 3 * g:3 * g + 3], dtc[:, 3 * g:3 * g + 3], wl, ALU.mult)
                GEN.put(tmpm)
        xdtw = GEN.get()
        P.tt(gb(xdtw, 384).rearrange("p (h d) -> p h d", h=6), ptb[:, 0:384].rearrange("p (h d) -> p h d", h=6),
             gf(wdec, 6).unsqueeze(2).broadcast_to([128, 6, 64]), ALU.mult)
        GEN.put(wdec)
        BANK.put(ptr)
        stage(6.4)
        py = BANK.get()
        first = True
        for j in range(3):
            oc = slice(j * 128, (j + 1) * 128)
            for e in range(2):
                h = 2 * j + e
                g, hh = h // 3, h % 3
                P.mm(py[:, oc], xp[:, h * 128:(h + 1) * 128], gb(MT[g], 384)[:, hh * 128:(hh + 1) * 128],
                     start=first, stop=False, skip_group_check=True)
                first = False
            for e in range(2):
                h = 2 * j + e
                g, hh = h // 3, h % 3
                r0, r1 = 64 * g, 64 * g + 64
                if sample is None:
                    P.mm(py[:, oc], hTp[:, h * 128:(h + 1) * 128], gb(CsT, 384)[:, hh * 128:(hh + 1) * 128],
                         start=False, stop=(j == 2 and e == 1), skip_group_check=True)
        GEN.put(*MT)
        if sample is not None:
            sample["yoff"](py, CsT)
        for j in range(3):
            P.stt(gf(yv[j], T)[:, cc], gb(xs_bf[j], T)[:, cc], cfc(l, O_DSK + j), py[:, j * 128:(j + 1) * 128],
                  ALU.mult, ALU.add)
        BANK.put(py)
        GEN.put(CsT)
        stage(6.5)
        if sample is None:
            pst = BANK.get()
            for h in range(6):
                P.mm(pst[:, h * 64:(h + 1) * 64], gb(btok, 128), gb(xdtw, 384)[:, h * 64:(h + 1) * 64],
                     start=(h == 0), stop=(h == 5), skip_group_check=True)
            for h in range(6):
                g, hh = h // 3, h % 3
                r0, r1 = 64 * g, 64 * g + 64
                cd = gf(Ex[g], 384, r0, r1)[:, hh * 128 + 127:hh * 128 + 128]
                P.stt(hTf[r0:r1, h * 64:(h + 1) * 64], hTf[r0:r1, h * 64:(h + 1) * 64], cd,
                      pst[r0:r1, h * 64:(h + 1) * 64], ALU.mult, ALU.add)
                e = h % 2
                P.copy(hTp[r0:r1, h * 128 + e * 64:h * 128 + e * 64 + 64], hTf[r0:r1, h * 64:(h + 1) * 64], eng="pool")
            BANK.put(pst)
        else:
            sample["states"](btok, xdtw, Ex)
        GEN.put(btok, xdtw, *Lx, *Ex)

    def prompt_block(l, b):
        t0 = b * T
        last_layer = (l == depth - 1)
        src = xT_in if l == 0 else XT
        P.dma(xT[:].rearrange("p (k t) -> p k t", k=8),
              src.ap().rearrange("(k p) t -> p k t", p=128)[:, :, t0:t0 + T])
        if b == 0:
            P.dma(wsm[:], wb.ap()[(l * NP + 23) * 128:(l * NP + 24) * 128, 0:2560])
            P.act(A_row[:], cfc(l, O_ALOG, 6), AF.Exp)
            P.ts(A_row[:], A_row[:], -1.0, None, ALU.mult)
            P.memset(hTp[:], 0.0)
            P.memset(hTf[:], 0.0)
            P.memset(uhist[:], 0.0)
            P.memset(xhist[:], 0.0)
        stage(0)
        rmsnorm_fm([xTk(k, 0, T) for k in range(8)], [cfc(l, O_GMIX + k) for k in range(8)], D,
                   [hTk(k, 0, T) for k in range(8)], 128, T)
        stage(1)
        w1 = ring_next(0)
        uet = [GEN.get(), GEN.get()]
        for c in range(2):
            ps = BANK.get()
            proj_fm(ps, w1, 512, c * 128, 128, T)
            P.copy(gf(uet[c], 15 + T)[:, 15:15 + T], ps[:, 0:T], eng="act")
            P.copy(gf(uet[c], 15), uhist[:, c * 15:(c + 1) * 15], eng="pool")
            BANK.put(ps)
        psq = [BANK.get(), BANK.get()]
        for c in range(2):
            proj_fm(psq[c], w1, 512, 256 + c * 128, 128, T)
        cqn = [GEN.get(), GEN.get()]
        rmsnorm_fm([psq[c][:, 0:T] for c in range(2)], [cfc(l, O_QG + c) for c in range(2)], 256,
                   [gb(cqn[c], T) for c in range(2)], 128, T)
        BANK.put(*psq)
        E_ = 15 + T
        m_bf = [GEN.get(), GEN.get()]
        for c in range(2):
            ue = gf(uet[c], 15 + T)
            sel = GEN.get()
            t2 = GEN.get()
            se, t2f = gf(sel, E_), gf(t2, E_)
            if c == 0:
                P.tt(se[0:64, 1:E_], ue[0:64, 1:E_], ue[0:64, 0:E_ - 1], ALU.add, eng="pool")
                P.tt(t2f[64:128, 1:E_], ue[64:128, 1:E_], ue[64:128, 0:E_ - 1], ALU.add, eng="pool")
                P.tt(se[64:128, 3:E_], t2f[64:128, 3:E_], t2f[64:128, 1:E_ - 2], ALU.add, eng="pool")
            else:
                t4 = GEN.get()
                t4f = gf(t4, E_)
                P.tt(t2f[:, 1:E_], ue[:, 1:E_], ue[:, 0:E_ - 1], ALU.add, eng="pool")
                P.tt(t4f[:, 3:E_], t2f[:, 3:E_], t2f[:, 1:E_ - 2], ALU.add, eng="pool")
                P.tt(se[0:64, 7:E_], t4f[0:64, 7:E_], t4f[0:64, 3:E_ - 4], ALU.add, eng="pool")
                P.tt(t2f[64:128, 7:E_], t4f[64:128, 7:E_], t4f[64:128, 3:E_ - 4], ALU.add, eng="pool")
                P.tt(se[64:128, 15:E_], t2f[64:128, 15:E_], t2f[64:128, 7:E_ - 8], ALU.add, eng="pool")
                GEN.put(t4)
            P.stt(gb(m_bf[c], T), se[:, 15:E_], cf[:, CG["INVW"] + c:CG["INVW"] + c + 1], ue[:, 15:E_],
                  ALU.mult, ALU.subtract)
            if b == 0:
                tq = GEN.get()
                P.tt(gf(tq, 15), se[:, 15:30], cf[:, CG["INVC"] + c * 15:CG["INVC"] + c * 15 + 15], ALU.mult)
                P.tt(gb(m_bf[c], T)[:, 0:15], gf(tq, 15), ue[:, 15:30], ALU.subtract)
                GEN.put(tq)
            GEN.put(sel, t2)
            if b == NB - 1:
                outs.append(P.dma(o_ppool.ap()[l, c * 128:(c + 1) * 128, :], ue[:, T:T + 15]))
            P.copy(uhist[:, c * 15:(c + 1) * 15], ue[:, T:T + 15], eng="pool")
            GEN.put(uet[c])
        stage(2)
        w2 = ring_next(1)
        zs = [GEN.get() for _ in range(3)]
        for j in range(3):
            ps = BANK.get()
            proj_fm(ps, w2, 550, j * 128, 128, T)
            P.act(gb(zs[j], T), ps[:, 0:T], AF.Silu)
            BANK.put(ps)
        ptk = [BANK.get(), BANK.get()]
        for i in range(4):
            pso = ptk[i // 2][:, (i % 2) * 166:(i % 2) * 166 + 166]
            for k in range(8):
                P.mm(pso, hTk(k, i * 128, (i + 1) * 128), w2[:, k * 550 + 384:k * 550 + 550],
                     start=(k == 0 and i % 2 == 0), stop=(k == 7), skip_group_check=True)
        tokmajor_post(l, t0, S // 128, b * 4, 4, ptk, o_pkv.ap()[l, t0:t0 + T, :], o_pkr.ap()[l, t0:t0 + T, :])
        BANK.put(*ptk)
        stage(3)
        w3 = ring_next(2)
        xs_bf = [GEN.get() for _ in range(3)]
        BT = GEN.get()
        CT = [GEN.get(), GEN.get()]
        P.memset(gb(CT[0], T, 64, 128), 0.0)
        P.memset(gb(CT[1], T, 0, 64), 0.0)
        for j in range(5):
            ps = BANK.get()
            proj_fm(ps, w3, 640, j * 128, 128, T)
            xet = GEN.get()
            xe = gf(xet, 3 + T)
            P.copy(xe[:, 3:3 + T], ps[:, 0:T], eng="act")
            P.copy(xe[:, 0:3], xhist[:, j * 3:(j + 1) * 3], eng="pool")
            BANK.put(ps)
            acc = GEN.get()
            af_ = gf(acc, T)
            P.act(af_, xe[:, 0:T], AF.Identity, scale=cfc(l, O_CW + j * 4), bias=cfc(l, O_CB + j))
            for k in range(1, 4):
                P.stt(af_, xe[:, k:k + T], cfc(l, O_CW + j * 4 + k), af_, ALU.mult, ALU.add)
            if j < 4:
                dst = xs_bf[j] if j < 3 else BT
                P.act(gb(dst, T), af_, AF.Silu)
            else:
                P.act(gb(CT[0], T, 0, 64), af_[0:64, :], AF.Silu)
                P.act(gb(CT[1], T, 64, 128), af_[64:128, :], AF.Silu)
            GEN.put(acc)
            if b == NB - 1:
                outs.append(P.dma(o_pconv.ap()[l, j * 128:(j + 1) * 128, :], xe[:, T:T + 3]))
            P.copy(xhist[:, j * 3:(j + 1) * 3], xe[:, T:T + 3], eng="pool")
            GEN.put(xet)
        stage(4)
        qlat, qpe = q_path(l, t0, T, cqn)
        stage(5)
        GEN.put(*cqn)
        mla_tiles, mlan = attention_prompt(l, b, qlat, qpe)
        stage(6)
        yv = [GEN.get() for _ in range(3)]
        for ci in range(4):
            ssd_chunk(l, ci, ci * 128, xs_bf, BT, CT, yv, (U_f, LS_f))
        GEN.put(BT, *CT, *xs_bf)
        if b == NB - 1:
            for h in range(6):
                g = h // 3
                outs.append(P.dma(o_pssm.ap()[l, h, :, :], hTf[64 * g:64 * g + 64, h * 64:(h + 1) * 64]))
        stage(7)
        mixT = [GEN.get() for _ in range(5)]
        for j in range(3):
            P.tt(gf(yv[j], T), gf(yv[j], T), gb(zs[j], T), ALU.mult, eng="pool")
        GEN.put(*zs)
        rmsnorm_fm([gf(yv[j], T) for j in range(3)], [cfc(l, O_SSDG + j) for j in range(3)], 384,
                   [gb(mixT[2 + j], T) for j in range(3)], 128, T)
        GEN.put(*yv)
        for c in range(2):
            ps = BANK.get()
            P.mm(ps[:, 0:T], wsm[:, W_PW + c * 128:W_PW + (c + 1) * 128], gb(m_bf[c], T), start=True, stop=True)
            P.act(gb(mixT[c], T), ps[:, 0:T], AF.Copy, scale=cfc(l, O_PSC + c))
            BANK.put(ps)
        GEN.put(*m_bf)
        dense_tail(l, T, mixT, mlan, t0)
        GEN.put(*mixT, *mla_tiles)

    def tokmajor_post(l, t0, tile_cap, tg0, ntile, ptk, dkv, dkr, per_bank=2, keep_ckb=False):
        nb_ = (ntile + per_bank - 1) // per_bank
        ss = GEN.get()
        junk = GEN.get()
        for i in range(ntile):
            v = ptk[i // per_bank][:, (i % per_bank) * 166:(i % per_bank) * 166 + 128]
            P.act(gf(junk, 128), v, AF.Square, accum_out=gf(ss, ntile)[:, i:i + 1])
        GEN.put(junk)
        P.act(gf(ss, ntile), gf(ss, ntile), AF.Sqrt, scale=1.0 / 128, bias=EPS)
        P.recip(gf(ss, ntile), gf(ss, ntile))
        ckn = GEN.get()
        for i in range(ntile):
            v = ptk[i // per_bank][:, (i % per_bank) * 166:(i % per_bank) * 166 + 128]
            P.stt(gf(ckn, ntile * 128)[:, i * 128:(i + 1) * 128], v, gf(ss, ntile)[:, i:i + 1],
                  cfc(l, O_KVG, 128), ALU.mult, ALU.mult)
        GEN.put(ss)
        outs.append(P.dma(dkv.rearrange("(i p) c -> p i c", p=128),
                          gf(ckn, ntile * 128).rearrange("p (i c) -> p i c", i=ntile)))
        ckb = GEN.get()
        P.copy(gb(ckb, ntile * 128), gf(ckn, ntile * 128), eng="act")
        GEN.put(ckn)
        if tg0 + ntile <= tile_cap:
            P.copy(ckv_tok[:, tg0 * 128:(tg0 + ntile) * 128], gb(ckb, ntile * 128), eng="pool")
        pT_ = BANK.get()
        pTb = pT_[:, 0:256].bitcast(BF16)
        for i in range(ntile):
            P.tr(pTb[:, i * 128:(i + 1) * 128], gb(ckb, ntile * 128)[:, i * 128:(i + 1) * 128], ident)
        P.copy(ckvT[:, t0:t0 + ntile * 128], pTb[:, 0:ntile * 128], eng="act")
        BANK.put(pT_)
        kpr = GEN.get()
        kv = gf(kpr, ntile * 32).rearrange("p (i c) -> p i c", i=ntile)
        tm = [GEN.get() for _ in range(4)]
        rkt = GEN.get()
        P.dma(gf(rkt, ntile * 16), ropek.ap()[:, tg0 * 16:(tg0 + ntile) * 16])
        P.dma(gf(rkt, 2 * ntile * 16)[:, ntile * 16:2 * ntile * 16],
              ropek.ap()[:, (NTILE + tg0) * 16:(NTILE + tg0 + ntile) * 16])
        cosv = gf(rkt, ntile * 16).rearrange("p (i c) -> p i c", c=16)
        sinv = gf(rkt, 2 * ntile * 16)[:, ntile * 16:2 * ntile * 16].rearrange("p (i c) -> p i c", c=16)
        for bi in range(nb_):
            n_in = min(per_bank, ntile - bi * per_bank)
            bv = ptk[bi][:, 0:per_bank * 166].rearrange("p (i c) -> p i c", i=per_bank)
            x1 = bv[:, 0:n_in, 128:144]
            x2 = bv[:, 0:n_in, 144:160]
            tg = bi * per_bank
            cs_ = cosv[:, tg:tg + n_in, :]
            sn_ = sinv[:, tg:tg + n_in, :]
            tv = [gf(t_, n_in * 16).rearrange("p (i c) -> p i c", i=n_in) for t_ in tm]
            P.tt(tv[0], x1, cs_, ALU.mult)
            P.tt(tv[1], x2, sn_, ALU.mult)
            P.tt(tv[2], x1, sn_, ALU.mult)
            P.tt(tv[3], x2, cs_, ALU.mult)
            i0 = bi * per_bank
            P.tt(kv[:, i0:i0 + n_in, 0:16], tv[0], tv[1], ALU.subtract, eng="pool")
            P.tt(kv[:, i0:i0 + n_in, 16:32], tv[2], tv[3], ALU.add, eng="pool")
        for t_ in tm:
            GEN.put(t_)
        GEN.put(rkt)
        outs.append(P.dma(dkr.rearrange("(i p) c -> p i c", p=128), kv))
        kp3 = GEN.get()
        P.copy(gb(kp3, ntile * 96).rearrange("p (i r c) -> p i r c", i=ntile, r=3),
               kv.unsqueeze(2).broadcast_to([128, ntile, 3, 32]), eng="act")
        GEN.put(kpr)
        pT2 = BANK.get()
        pT2b = pT2[:, 0:256].bitcast(BF16)
        for i in range(ntile):
            P.tr(pT2b[0:96, i * 128:(i + 1) * 128], gb(kp3, ntile * 96)[:, i * 96:(i + 1) * 96], ident)
        P.copy(kpeT3[:, t0:t0 + ntile * 128], pT2b[0:96, 0:ntile * 128], eng="act")
        BANK.put(pT2)
        GEN.put(kp3)
        if not keep_ckb:
            GEN.put(ckb)
        tx = GEN.get()
        for bi in range(nb_):
            n_in = min(per_bank, ntile - bi * per_bank)
            bv = ptk[bi][:, 0:per_bank * 166].rearrange("p (i c) -> p i c", i=per_bank)
            i0 = bi * per_bank
            P.tt(gf(tx, ntile * 6).rearrange("p (i c) -> p i c", i=ntile)[:, i0:i0 + n_in, :], bv[:, 0:n_in, 160:166],
                 cfc(l, O_DTB, 6).unsqueeze(1).broadcast_to([128, n_in, 6]), ALU.add)
        P.act(gf(tx, ntile * 6), gf(tx, ntile * 6), AF.Exp)
        P.act(dt_tok[:, 0:ntile * 6], gf(tx, ntile * 6), AF.Ln, bias=1.0)
        GEN.put(tx)
        P.tt(a_tok[:, 0:ntile * 6].rearrange("p (i c) -> p i c", i=ntile),
             dt_tok[:, 0:ntile * 6].rearrange("p (i c) -> p i c", i=ntile),
             A_row[:, :].unsqueeze(1).broadcast_to([128, ntile, 6]), ALU.mult)
        return ckb if keep_ckb else None

    def q_path(l, t0, N, cqn):
        qn = [GEN.get() for _ in range(3)]
        for j in range(3):
            ps = BANK.get()
            for k in range(2):
                P.mm(ps[:, 0:N], wsm[:, W_UQ + k * 768 + j * 128:W_UQ + k * 768 + (j + 1) * 128], gb(cqn[k], N),
                     start=(k == 0), stop=(k == 1))
            P.copy(gb(qn[j], N), ps[:, 0:N], eng="act")
            BANK.put(ps)
        cosT = GEN.get()
        sinT = GEN.get()
        P.dma(gf(cosT, N, 0, 96), ropeq.ap()[0:96, t0:t0 + N])
        P.dma(gf(sinT, N, 0, 96), ropeq.ap()[96:192, t0:t0 + N])
        qpe = [GEN.get(), GEN.get()]
        for x in range(2):
            pp = BANK.get()
            psw = BANK.get()
            for k in range(2):
                c0 = W_UQ + k * 768 + 384 + x * 96
                P.mm(pp[0:96, 0:N], wsm[:, c0:c0 + 96], gb(cqn[k], N), start=(k == 0), stop=(k == 1))
            for k in range(2):
                c0 = W_UQ + k * 768 + 576 + x * 96
                P.mm(psw[0:96, 0:N], wsm[:, c0:c0 + 96], gb(cqn[k], N), start=(k == 0), stop=(k == 1))
            ta = GEN.get()
            tb_ = GEN.get()
            P.tt(gf(ta, N, 0, 96), pp[0:96, 0:N], gf(cosT, N, 0, 96), ALU.mult)
            P.tt(gf(tb_, N, 0, 96), psw[0:96, 0:N], gf(sinT, N, 0, 96), ALU.mult)
            P.tt(gb(qpe[x], N, 0, 96), gf(ta, N, 0, 96), gf(tb_, N, 0, 96), ALU.add, eng="pool")
            BANK.put(pp, psw)
            GEN.put(ta, tb_)
        GEN.put(cosT, sinT)
        qlat = [GEN.get() for _ in range(6)]
        for h in range(6):
            r0 = 64 * (h % 2)
            ps = BANK.get()
            P.mm(ps[:, 0:N], wsm[r0:r0 + 64, W_UK + (h // 2) * 128:W_UK + (h // 2 + 1) * 128],
                 gb(qn[h // 2], N, r0, r0 + 64), start=True, stop=True)
            P.copy(gb(qlat[h], N), ps[:, 0:N], eng="act")
            BANK.put(ps)
        GEN.put(*qn)
        return qlat, qpe

    def attention_prompt(l, b, qlat, qpe):
        nkt = 4 * b + 4
        ot = [GEN.get() for _ in range(3)]
        oTv = [gb(ot[h // 2], 2 * T, 0, 64)[:, (h % 2) * T:(h % 2 + 1) * T] for h in range(6)]
        for h in range(6):
            psO = BANK.get()
            psD = BANK.get()
            rr = 32 * (h % 3)
            for kt in range(nkt):
                i = kt - 4 * b
                q0 = max(0, i) * 128
                psS = BANK.get()
                P.mm(psS[:, q0:T], ckvT[:, kt * 128:(kt + 1) * 128], gb(qlat[h], T)[:, q0:T], start=True, stop=False)
                P.mm(psS[:, q0:T], kpeT3[rr:rr + 32, kt * 128:(kt + 1) * 128], gb(qpe[h // 3], T, rr, rr + 32)[:, q0:T],
                     start=False, stop=True)
                pT = GEN.get()
                P.act(gb(pT, T)[:, q0:T], psS[:, q0:T], AF.Exp, scale=MLA_SCALE)
                BANK.put(psS)
                if i >= 0:
                    P.tt(gb(pT, T)[:, q0:q0 + 128], gb(pT, T)[:, q0:q0 + 128], U_b, ALU.mult, eng="pool")
                P.mm(psO[:, q0:T], ckv_tok[:, kt * 128:(kt + 1) * 128], gb(pT, T)[:, q0:T],
                     start=(kt == 0), stop=(kt == nkt - 1), skip_group_check=True)
                P.mm(psD[:, q0:T], ones_b, gb(pT, T)[:, q0:T],
                     start=(kt == 0), stop=(kt == nkt - 1), skip_group_check=True)
                GEN.put(pT)
            rden = GEN.get()
            P.recip(gf(rden, T), psD[:, 0:T])
            olat = GEN.get()
            P.tt(gb(olat, T), psO[:, 0:T], gf(rden, T), ALU.mult)
            BANK.put(psO, psD)
            GEN.put(rden)
            ps = BANK.get()
            P.mm(ps[0:64, 0:T], wsm[:, W_UV + h * 64:W_UV + (h + 1) * 64], gb(olat, T), start=True, stop=True)
            P.copy(oTv[h], ps[0:64, 0:T], eng="act")
            BANK.put(ps)
            GEN.put(olat, qlat[h])
        GEN.put(*qpe)
        rmsnorm_fm(oTv, [cfc(l, O_MLAG + h, 1, 0, 64) for h in range(6)], 384, oTv, 64, T)
        return ot, oTv

    def dense_tail(l, N, mixT, mlan, t0):
        last_layer = (l == depth - 1)
        for jm in range(4):
            w = ring_next(3 + jm)
            for e in range(2):
                mc = 2 * jm + e
                ps = BANK.get()
                for kc in range(5):
                    P.mm(ps[:, 0:N], w[:, kc * 256 + e * 128:kc * 256 + (e + 1) * 128], gb(mixT[kc], N),
                         start=(kc == 0), stop=False)
                for h in range(6):
                    kc = 5 + h
                    P.mm(ps[:, 0:N], w[0:64, kc * 256 + e * 128:kc * 256 + (e + 1) * 128], mlan[h][:, 0:N],
                         start=False, stop=(h == 5))
                P.tt(xTk(mc, 0, N), ps[:, 0:N], xTk(mc, 0, N), ALU.add)
                BANK.put(ps)
        rmsnorm_fm([xTk(k, 0, N) for k in range(8)], [cfc(l, O_GFFN + k) for k in range(8)], D,
                   [hTk(k, 0, N) for k in range(8)], 128, N)
        for i in range(8):
            wu = ring_next(7 + 2 * i)
            a = [GEN.get() for _ in range(4)]
            for fc in range(4):
                ps = BANK.get()
                proj_fm(ps, wu, 512, fc * 128, 128, N)
                r = GEN.get()
                P.act(gb(r, N), ps[:, 0:N], AF.Relu)
                BANK.put(ps)
                P.tt(gb(a[fc], N), gb(r, N), gb(r, N), ALU.mult, eng="pool")
                GEN.put(r)
            wd = ring_next(8 + 2 * i)
            for mc in range(8):
                ps = BANK.get()
                for fc in range(4):
                    P.mm(ps[:, 0:N], wd[:, fc * 1024 + mc * 128:fc * 1024 + (mc + 1) * 128], gb(a[fc], N),
                         start=(fc == 0), stop=(fc == 3))
                P.tt(xTk(mc, 0, N), ps[:, 0:N], xTk(mc, 0, N), ALU.add)
                BANK.put(ps)
            GEN.put(*a)
        xv = xT[:].rearrange("p (k t) -> p k t", k=8)[:, :, 0:N]
        if not last_layer:
            P.dma(XT.ap().rearrange("(k p) t -> p k t", p=128)[:, :, t0:t0 + N], xv)
        else:
            yo = [GEN.get() for _ in range(8)]
            rmsnorm_fm([xTk(k, 0, N) for k in range(8)], [cf[:, CG["FING"] + k:CG["FING"] + k + 1] for k in range(8)], D,
                       [gf(yo[k], N) for k in range(8)], 128, N)
            for k in range(8):
                outs.append(P.dma(yT.ap()[k * 128:(k + 1) * 128, t0:t0 + N], gf(yo[k], N)))
            GEN.put(*yo)


    def attention_sample(l, qlat, qpe, ckb_new):
        QL = GEN.get()
        QP = GEN.get()
        for h in range(6):
            P.copy(gb(QL, 768).rearrange("p (q h t) -> p q h t", q=8, h=6)[:, :, h, :],
                   gb(qlat[h], 128).rearrange("p (q t) -> p q t", q=8), eng="pool")
        P.memset(gb(QP, 768, 0, 96), 0.0)
        for h in range(6):
            rr = 32 * (h % 3)
            P.copy(gb(QP, 768, rr, rr + 32).rearrange("p (q h t) -> p q h t", q=8, h=6)[:, :, h, :],
                   gb(qpe[h // 3], 128, rr, rr + 32).rearrange("p (q t) -> p q t", q=8), eng="pool")
        GEN.put(*qlat, *qpe)
        OL = GEN.get()
        gi = 0
        for q in range(8):
            QLq = gb(QL, 768)[:, q * 96:(q + 1) * 96]
            QPq = gb(QP, 768, 0, 96)[:, q * 96:(q + 1) * 96]
            psO = BANK.get()
            first_o = True
            for ch in range(16):
                X = xg[gi % 2]
                gi += 1
                P.rec("pool", lambda e, X=X, q=q, ch=ch: e.indirect_dma_start(
                    out=X[:, 0:1024], out_offset=None, in_=ckvc[l].ap(),
                    in_offset=bass.IndirectOffsetOnAxis(ap=pt_sb[:, q:q + 1], axis=0), element_offset=ch * 1024),
                    [pt_sb[:]], [X[:]], dma=True, rkeys=["ckvc%d" % l])
                Xr = GEN.get()
                P.rec("pool", lambda e, Xr=Xr, q=q, ch=ch: e.indirect_dma_start(
                    out=gf(Xr, 256), out_offset=None, in_=ckrc[l].ap(),
                    in_offset=bass.IndirectOffsetOnAxis(ap=pt_sb[:, q:q + 1], axis=0), element_offset=ch * 256),
                    [pt_sb[:]], [gf(Xr, 256)], dma=True, rkeys=["ckrc%d" % l])
                Xb = GEN.get()
                P.copy(gb(Xb, 1024), X[:, 0:1024], eng="dve")
                Xr3 = GEN.get()
                P.copy(gb(Xr3, 768).rearrange("p (t r c) -> p t r c", t=8, r=3),
                       gf(Xr, 256).rearrange("p (t c) -> p t c", t=8).unsqueeze(2).broadcast_to([128, 8, 3, 32]), eng="act")
                GEN.put(Xr)
                pT = BANK.get()
                pTb = pT[:, 0:512].bitcast(BF16)
                for t in range(8):
                    P.tr(pTb[:, t * 128:(t + 1) * 128], gb(Xb, 1024)[:, t * 128:(t + 1) * 128], ident)
                KT = GEN.get()
                P.copy(gb(KT, 1024), pTb[:, 0:1024], eng="act")
                BANK.put(pT)
                pT2 = BANK.get()
                pT2b = pT2[:, 0:512].bitcast(BF16)
                for t in range(8):
                    P.tr(pT2b[0:96, t * 128:(t + 1) * 128], gb(Xr3, 768)[:, t * 96:(t + 1) * 96], ident)
                KrT = GEN.get()
                P.copy(gb(KrT, 1024, 0, 96), pT2b[0:96, 0:1024], eng="dve")
                BANK.put(pT2)
                GEN.put(Xr3)
                psS = [BANK.get(), BANK.get()]
                for t in range(8):
                    bk = psS[t // 5]
                    c0 = (t % 5) * 96
                    P.mm(bk[:, c0:c0 + 96], gb(KT, 1024)[:, t * 128:(t + 1) * 128], QLq,
                         start=(t % 5 == 0), stop=False, skip_group_check=True)
                    P.mm(bk[:, c0:c0 + 96], gb(KrT, 1024, 0, 96)[:, t * 128:(t + 1) * 128], QPq,
                         start=False, stop=True, skip_group_check=True)
                GEN.put(KT, KrT)
                PT = GEN.get()
                P.act(gb(PT, 768)[:, 0:480], psS[0][:, 0:480], AF.Exp, scale=MLA_SCALE)
                P.act(gb(PT, 768)[:, 480:768], psS[1][:, 0:288], AF.Exp, scale=MLA_SCALE)
                BANK.put(*psS)
                P.tt(gb(PT, 768).rearrange("p (t c) -> p t c", t=8), gb(PT, 768).rearrange("p (t c) -> p t c", t=8),
                     pmask_b.unsqueeze(1).broadcast_to([128, 8, 96]), ALU.mult, eng="pool")
                for t in range(8):
                    P.mm(psO[0:96, 0:128], gb(PT, 768)[:, t * 96:(t + 1) * 96], gb(Xb, 1024)[:, t * 128:(t + 1) * 128],
                         start=first_o, stop=False, skip_group_check=True)
                    first_o = False
                    P.mm(psO[0:96, 128:129], gb(PT, 768)[:, t * 96:(t + 1) * 96], ones_b[:, 0:1],
                         start=False, stop=False, skip_group_check=True)
                GEN.put(PT, Xb)
            psN = BANK.get()
            P.mm(psN[:, 0:96], ckvT[:, S:S + 128], QLq, start=True, stop=False)
            P.mm(psN[:, 0:96], kpeT3[0:96, S:S + 128], QPq, start=False, stop=True)
            PN = GEN.get()
            P.act(gb(PN, 96), psN[:, 0:96], AF.Exp, scale=MLA_SCALE)
            BANK.put(psN)
            P.tt(gb(PN, 96), gb(PN, 96), nmask_b[:, q * 96:(q + 1) * 96], ALU.mult, eng="pool")
            P.mm(psO[0:96, 0:128], gb(PN, 96), gb(ckb_new, 128), start=False, stop=False, skip_group_check=True)
            P.mm(psO[0:96, 128:129], gb(PN, 96), ones_b[:, 0:1], start=False, stop=True, skip_group_check=True)
            GEN.put(PN)
            rd = GEN.get()
            P.recip(gf(rd, 1, 0, 96), psO[0:96, 128:129])
            ol = GEN.get()
            P.ts(gb(ol, 128, 0, 96), psO[0:96, 0:128], gf(rd, 1, 0, 96), None, ALU.mult)
            BANK.put(psO)
            GEN.put(rd)
            pT3 = BANK.get()
            pT3b = pT3[:, 0:256].bitcast(BF16)
            P.tr(pT3b[:, 0:96], gb(ol, 128, 0, 96), ident[0:96, 0:96])
            P.copy(gb(OL, 768).rearrange("p (h q t) -> p h q t", h=6, q=8)[:, :, q, :],
                   pT3b[:, 0:96].rearrange("p (h t) -> p h t", h=6), eng="act")
            BANK.put(pT3)
            GEN.put(ol)
        GEN.put(QL, QP, ckb_new)
        ot = [GEN.get() for _ in range(3)]
        oTv = [gb(ot[h // 2], 2 * T, 0, 64)[:, (h % 2) * T:(h % 2) * T + 128] for h in range(6)]
        for h in range(6):
            ps = BANK.get()
            P.mm(ps[0:64, 0:128], wsm[:, W_UV + h * 64:W_UV + (h + 1) * 64], gb(OL, 768)[:, h * 128:(h + 1) * 128],
                 start=True, stop=True)
            P.copy(oTv[h], ps[0:64, 0:128], eng="act")
            BANK.put(ps)
        GEN.put(OL)
        rmsnorm_fm(oTv, [cfc(l, O_MLAG + h, 1, 0, 64) for h in range(6)], 384, oTv, 64, 128)
        return ot, oTv

    def sample_block(l):
        N = 128
        src = xT_in if l == 0 else XT
        xv = xT[:].rearrange("p (k t) -> p k t", k=8)[:, :, 0:N]
        P.dma(xv, src.ap().rearrange("(k p) t -> p k t", p=128)[:, :, S:S + N])
        rmsnorm_fm([xTk(k, 0, N) for k in range(8)], [cfc(l, O_GMIX + k) for k in range(8)], D,
                   [hTk(k, 0, N) for k in range(8)], 128, N)
        w1 = ring_next(0)
        uet = [GEN.get(), GEN.get()]
        uv = [gf(uet[c], 368).rearrange("p (b e) -> p b e", b=16) for c in range(2)]
        for c in range(2):
            ps = BANK.get()
            proj_fm(ps, w1, 512, c * 128, 128, N)
            P.dma(uv[c][:, :, 0:15], spool.ap()[l, c * 128:(c + 1) * 128, :, :])
            P.copy(uv[c][:, :, 15:23], ps[:, 0:N].rearrange("p (b i) -> p b i", b=16), eng="act")
            BANK.put(ps)
        psq = [BANK.get(), BANK.get()]
        for c in range(2):
            proj_fm(psq[c], w1, 512, 256 + c * 128, 128, N)
        cqn = [GEN.get(), GEN.get()]
        rmsnorm_fm([psq[c][:, 0:N] for c in range(2)], [cfc(l, O_QG + c) for c in range(2)], 256,
                   [gb(cqn[c], N) for c in range(2)], 128, N)
        BANK.put(*psq)
        m_bf = [GEN.get(), GEN.get()]
        for c in range(2):
            ue = uv[c]
            sel = GEN.get()
            t2 = GEN.get()
            se = gf(sel, 368).rearrange("p (b e) -> p b e", b=16)
            t2f = gf(t2, 368).rearrange("p (b e) -> p b e", b=16)
            E_ = 23
            if c == 0:
                P.tt(se[0:64, :, 1:E_], ue[0:64, :, 1:E_], ue[0:64, :, 0:E_ - 1], ALU.add, eng="pool")
                P.tt(t2f[64:128, :, 1:E_], ue[64:128, :, 1:E_], ue[64:128, :, 0:E_ - 1], ALU.add, eng="pool")
                P.tt(se[64:128, :, 3:E_], t2f[64:128, :, 3:E_], t2f[64:128, :, 1:E_ - 2], ALU.add, eng="pool")
            else:
                t4 = GEN.get()
                t4f = gf(t4, 368).rearrange("p (b e) -> p b e", b=16)
                P.tt(t2f[:, :, 1:E_], ue[:, :, 1:E_], ue[:, :, 0:E_ - 1], ALU.add, eng="pool")
                P.tt(t4f[:, :, 3:E_], t2f[:, :, 3:E_], t2f[:, :, 1:E_ - 2], ALU.add, eng="pool")
                P.tt(se[0:64, :, 7:E_], t4f[0:64, :, 7:E_], t4f[0:64, :, 3:E_ - 4], ALU.add, eng="pool")
                P.tt(t2f[64:128, :, 7:E_], t4f[64:128, :, 7:E_], t4f[64:128, :, 3:E_ - 4], ALU.add, eng="pool")
                P.tt(se[64:128, :, 15:E_], t2f[64:128, :, 15:E_], t2f[64:128, :, 7:E_ - 8], ALU.add, eng="pool")
                GEN.put(t4)
            P.stt(gb(m_bf[c], N).rearrange("p (b i) -> p b i", b=16), se[:, :, 15:E_],
                  cf[:, CG["INVW"] + c:CG["INVW"] + c + 1], ue[:, :, 15:E_], ALU.mult, ALU.subtract)
            GEN.put(sel, t2)
            outs.append(P.dma(o_spool.ap()[l, c * 128:(c + 1) * 128, :, :], ue[:, :, 8:23]))
            GEN.put(uet[c])
        w2 = ring_next(1)
        zs = [GEN.get() for _ in range(3)]
        for j in range(3):
            ps = BANK.get()
            proj_fm(ps, w2, 550, j * 128, 128, N)
            P.act(gb(zs[j], N), ps[:, 0:N], AF.Silu)
            BANK.put(ps)
        ptk = [BANK.get()]
        for k in range(8):
            P.mm(ptk[0][:, 0:166], hTk(k, 0, 128), w2[:, k * 550 + 384:k * 550 + 550], start=(k == 0), stop=(k == 7))
        ckb_new = tokmajor_post(l, S, 0, S // 128, 1, ptk, o_skv.ap()[l, :, :], o_skr.ap()[l, :, :], keep_ckb=True)
        BANK.put(*ptk)
        w3 = ring_next(2)
        xs_bf = [GEN.get() for _ in range(3)]
        BT = GEN.get()
        CT = [GEN.get(), GEN.get()]
        P.memset(gb(CT[0], N, 64, 128), 0.0)
        P.memset(gb(CT[1], N, 0, 64), 0.0)
        for j in range(5):
            ps = BANK.get()
            proj_fm(ps, w3, 640, j * 128, 128, N)
            xet = GEN.get()
            xe = gf(xet, 176).rearrange("p (b e) -> p b e", b=16)
            P.dma(xe[:, :, 0:3], sconv.ap()[l, j * 128:(j + 1) * 128, :, :])
            P.copy(xe[:, :, 3:11], ps[:, 0:N].rearrange("p (b i) -> p b i", b=16), eng="act")
            BANK.put(ps)
            acc = GEN.get()
            af_ = gf(acc, N)
            a3 = af_.rearrange("p (b i) -> p b i", b=16)
            P.act(a3, xe[:, :, 0:8], AF.Identity, scale=cfc(l, O_CW + j * 4), bias=cfc(l, O_CB + j))
            for k in range(1, 4):
                P.stt(a3, xe[:, :, k:k + 8], cfc(l, O_CW + j * 4 + k), a3, ALU.mult, ALU.add)
            if j < 4:
                dst = xs_bf[j] if j < 3 else BT
                P.act(gb(dst, N), af_, AF.Silu)
            else:
                P.act(gb(CT[0], N, 0, 64), af_[0:64, :], AF.Silu)
                P.act(gb(CT[1], N, 64, 128), af_[64:128, :], AF.Silu)
            GEN.put(acc)
            outs.append(P.dma(o_sconv.ap()[l, j * 128:(j + 1) * 128, :, :], xe[:, :, 8:11]))
            GEN.put(xet)
        qlat, qpe = q_path(l, S, N, cqn)
        GEN.put(*cqn)
        mla_tiles, mlan = attention_sample(l, qlat, qpe, ckb_new)
        yv = [GEN.get() for _ in range(3)]

        def yoff(py, CsT):
            CsX = [GEN.get(), GEN.get()]
            for g in range(2):
                pdup = BANK.get()
                P.mm(pdup[:, 0:384], dup_b[g], gb(CsT, 384), start=True, stop=True)
                P.tt(gb(CsX[g], 384).rearrange("p (h c) -> p h c", h=3), pdup[:, 0:384].rearrange("p (h c) -> p h c", h=3),
                     parL_b.unsqueeze(1).broadcast_to([128, 3, 128]), ALU.mult)
                BANK.put(pdup)
            for h in range(6):
                g, hh, j, e = h // 3, h % 3, h // 2, h % 2
                H0f = GEN.get()
                for b2 in range(2):
                    P.dma(gf(H0f, 512, 64 * b2, 64 * b2 + 64).rearrange("p (q c) -> p q c", q=8),
                          sssm.ap()[l, :, b2, h, :, :].rearrange("q n p -> n q p"))
                H0p = GEN.get()
                P.memset(gb(H0p, 1024), 0.0)
                P.copy(gb(H0p, 1024).rearrange("p (q c) -> p q c", q=8)[:, :, e * 64:(e + 1) * 64],
                       gf(H0f, 512).rearrange("p (q c) -> p q c", q=8), eng="pool")
                GEN.put(H0f)
                for q in range(8):
                    P.mm(py[:, j * 128 + 16 * q:j * 128 + 16 * q + 16], gb(H0p, 1024)[:, q * 128:(q + 1) * 128],
                         gb(CsX[g], 384)[:, hh * 128 + 16 * q:hh * 128 + 16 * q + 16],
                         start=False, stop=False, skip_group_check=True)
                GEN.put(H0p)
            GEN.put(*CsX)

        def states(btok, xdtw, Ex):
            Bpar = [GEN.get(), GEN.get()]
            for g in range(2):
                for b2 in range(2):
                    P.ts(gb(Bpar[g], 128)[:, b2 * 64:(b2 + 1) * 64], gb(btok, 128)[:, g * 64:(g + 1) * 64],
                         parS_f[:, b2:b2 + 1], None, ALU.mult)
            for h in range(6):
                g, hh = h // 3, h % 3
                xq = GEN.get()
                P.tt(gb(xq, 512).rearrange("p (q c) -> p q c", q=8),
                     gb(xdtw, 384)[:, h * 64:(h + 1) * 64].unsqueeze(1).broadcast_to([128, 8, 64]),
                     pairm_f.unsqueeze(2).broadcast_to([128, 8, 64]), ALU.mult)
                pst = BANK.get()
                P.mm(pst[:, 0:512], gb(Bpar[g], 128), gb(xq, 512), start=True, stop=True)
                GEN.put(xq)
                H0f = GEN.get()
                for b2 in range(2):
                    P.dma(gf(H0f, 512, 64 * b2, 64 * b2 + 64).rearrange("p (q c) -> p q c", q=8),
                          sssm.ap()[l, :, b2, h, :, :].rearrange("q n p -> n q p"))
                hn = GEN.get()
                for b2 in range(2):
                    r0, r1 = 64 * b2, 64 * b2 + 64
                    cdv = gf(Ex[g], 384, r0, r1)[:, hh * 128:(hh + 1) * 128].rearrange("p (q r) -> p q r", r=16)[:, :, 8 * b2 + 7]
                    P.tt(gf(hn, 512, r0, r1).rearrange("p (q c) -> p q c", q=8),
                         gf(H0f, 512, r0, r1).rearrange("p (q c) -> p q c", q=8),
                         cdv.unsqueeze(2).broadcast_to([64, 8, 64]), ALU.mult, eng="pool")
                P.tt(gf(hn, 512), gf(hn, 512), pst[:, 0:512], ALU.add)
                BANK.put(pst)
                GEN.put(H0f)
                for b2 in range(2):
                    outs.append(P.dma(o_sssm.ap()[l, :, b2, h, :, :].rearrange("q n p -> n q p"),
                                      gf(hn, 512, 64 * b2, 64 * b2 + 64).rearrange("p (q c) -> p q c", q=8)))
                GEN.put(hn)
            GEN.put(*Bpar)

        ssd_chunk(l, 0, 0, xs_bf, BT, CT, yv, (UB_f, LSB_f), sample={"yoff": yoff, "states": states})
        GEN.put(BT, *CT, *xs_bf)
        mixT = [GEN.get() for _ in range(5)]
        for j in range(3):
            P.tt(gf(yv[j], N), gf(yv[j], N), gb(zs[j], N), ALU.mult, eng="pool")
        GEN.put(*zs)
        rmsnorm_fm([gf(yv[j], N) for j in range(3)], [cfc(l, O_SSDG + j) for j in range(3)], 384,
                   [gb(mixT[2 + j], N) for j in range(3)], 128, N)
        GEN.put(*yv)
        for c in range(2):
            ps = BANK.get()
            P.mm(ps[:, 0:N], wsm[:, W_PW + c * 128:W_PW + (c + 1) * 128], gb(m_bf[c], N), start=True, stop=True)
            P.act(gb(mixT[c], N), ps[:, 0:N], AF.Copy, scale=cfc(l, O_PSC + c))
            BANK.put(ps)
        GEN.put(*m_bf)
        dense_tail(l, N, mixT, mlan, S)
        GEN.put(*mixT, *mla_tiles)

    try:
        for l in range(depth):
            for b in range(NB):
                prompt_block(l, b)
            if with_sample:
                sample_block(l)
    except StopBuild:
        pass
    P.finish(final_wait_ops=outs)
    return nc, P
def _f32(a):
    return np.ascontiguousarray(np.asarray(a, dtype=np.float32))


def _pad_piece(a):
    a = a.reshape(128, -1)
    out = np.zeros((128, SLOT), np.float32)
    out[:, :a.shape[1]] = a
    return out


def make_wimg(inp, depth):
    w_in, w_out, w_up, w_down = (_f32(inp[k]) for k in ("w_in", "w_out", "w_up", "w_down"))
    w_uq, w_uk, w_uv, pool_w = (_f32(inp[k]) for k in ("w_uq", "w_uk", "w_uv", "pool_w"))
    img = np.zeros((depth, NP, 128, SLOT), np.float32)

    def kmaj(a):
        K = a.shape[0] // 128
        return a.reshape(K, 128, a.shape[1]).transpose(1, 0, 2)

    c1 = list(range(0, 256)) + list(range(1286, 1542))
    c2 = list(range(256, 640)) + list(range(1542, 1702)) + list(range(1280, 1286))
    c3 = list(range(640, 1280))
    nope = [h * 96 + d for h in range(6) for d in range(64)]
    peA = [h * 96 + 64 + d for h in range(3) for d in range(32)]
    peB = [h * 96 + 64 + d for h in range(3, 6) for d in range(32)]
    swA = [h * 96 + 64 + (d + 16) % 32 for h in range(3) for d in range(32)]
    swB = [h * 96 + 64 + (d + 16) % 32 for h in range(3, 6) for d in range(32)]
    uqcols = nope + peA + peB + swA + swB
    for l in range(depth):
        img[l, 0] = _pad_piece(kmaj(w_in[l][:, c1]))
        img[l, 1] = _pad_piece(kmaj(w_in[l][:, c2]))
        img[l, 2] = _pad_piece(kmaj(w_in[l][:, c3]))
        for jm in range(4):
            blk = np.zeros((128, 11, 256), np.float32)
            mc = slice(jm * 256, (jm + 1) * 256)
            for kc in range(5):
                blk[:, kc, :] = w_out[l][kc * 128:(kc + 1) * 128, mc]
            for h in range(6):
                blk[0:64, 5 + h, :] = w_out[l][640 + h * 64:640 + (h + 1) * 64, mc]
            img[l, 3 + jm] = _pad_piece(blk)
        for i in range(8):
            img[l, 7 + 2 * i] = _pad_piece(kmaj(w_up[l][:, 512 * i:512 * (i + 1)]))
            img[l, 8 + 2 * i] = _pad_piece(kmaj(w_down[l][512 * i:512 * (i + 1), :]))
        sm = np.zeros((128, 2560), np.float32)
        sm[:, W_UQ:W_UQ + 1536] = kmaj(w_uq[l][:, uqcols]).reshape(128, 1536)
        wk = w_uk[l].reshape(128, 3, 2, 64).transpose(2, 3, 1, 0).reshape(128, 3 * 128)
        sm[:, W_UK:W_UK + 384] = wk
        sm[:, W_UV:W_UV + 384] = w_uv[l].reshape(128, 384)
        pw = np.zeros((128, 2, 128), np.float32)
        for c in range(2):
            for e in range(2):
                pw[e * 64:(e + 1) * 64, c, e * 64:(e + 1) * 64] = pool_w[l][2 * c + e]
        sm[:, W_PW:W_PW + 256] = pw.reshape(128, 256)
        img[l, 23] = _pad_piece(sm)
    return img.reshape(depth * NP * 128, SLOT)


def make_cf(inp, depth):
    CG = cf_globals(depth)
    cf = np.zeros((128, CG["NCF"]), np.float32)
    p = np.arange(128)

    def col8(v):
        return _f32(v).reshape(-1, 128).T

    for l in range(depth):
        o = l * CF_PER_LAYER
        cf[:, o + O_GMIX:o + O_GMIX + 8] = col8(inp["norm_mix_g"][l])
        cf[:, o + O_GFFN:o + O_GFFN + 8] = col8(inp["norm_ffn_g"][l])
        cf[:, o + O_PSC:o + O_PSC + 2] = col8(inp["pool_scale"][l])
        cw = _f32(inp["conv_w"][l])
        for j in range(5):
            for k in range(4):
                cf[:, o + O_CW + j * 4 + k] = cw[k, j * 128:(j + 1) * 128]
        cf[:, o + O_CB:o + O_CB + 5] = col8(inp["conv_b"][l])
        ds = _f32(inp["d_skip"][l])
        for j in range(3):
            cf[:, o + O_DSK + j] = ds[2 * j + p // 64]
        cf[:, o + O_SSDG:o + O_SSDG + 3] = col8(inp["ssd_norm_g"][l])
        cf[:, o + O_QG:o + O_QG + 2] = col8(inp["q_norm_g"][l])
        mg = _f32(inp["mla_out_g"][l]).reshape(6, 64)
        cf[0:64, o + O_MLAG:o + O_MLAG + 6] = mg.T
        cf[:, o + O_KVG:o + O_KVG + 128] = _f32(inp["kv_norm_g"][l])[None, :]
        cf[:, o + O_DTB:o + O_DTB + 6] = _f32(inp["dt_bias"][l])[None, :]
        cf[:, o + O_ALOG:o + O_ALOG + 6] = _f32(inp["a_log"][l])[None, :]
    cf[:, CG["FING"]:CG["FING"] + 8] = col8(inp["final_norm_g"])
    for c in range(2):
        w = np.array([POOL_W[2 * c + q // 64] for q in range(128)], np.float32)
        cf[:, CG["INVW"] + c] = 1.0 / w
        for t in range(15):
            cf[:, CG["INVC"] + c * 15 + t] = 1.0 / np.minimum(w, t + 1)
    r = np.arange(128)[:, None]
    c_ = np.arange(128)[None, :]
    cf[:, CG["U"]:CG["U"] + 128] = (r <= c_)
    cf[:, CG["LS"]:CG["LS"] + 128] = (r > c_)
    same = (r // 8 == c_ // 8)
    cf[:, CG["UB"]:CG["UB"] + 128] = (r <= c_) & same
    cf[:, CG["LSB"]:CG["LSB"] + 128] = (r > c_) & same
    cf[:, CG["ONES"]:CG["ONES"] + 128] = 1.0
    cf[:, CG["SEQM"]:CG["SEQM"] + 16] = (np.arange(128)[:, None] // 8 == np.arange(16)[None, :])
    sidx = np.arange(128)[:, None]
    cf[:, CG["PARS"]:CG["PARS"] + 2] = ((sidx // 8) % 2 == np.arange(2)[None, :])
    cf[:, CG["PAIRM"]:CG["PAIRM"] + 8] = (sidx // 16 == np.arange(8)[None, :])
    cf[:, CG["LASTM"]:CG["LASTM"] + 128] = (np.arange(128)[None, :] == 8 * (sidx // 8) + 7)
    return cf


def make_cb():
    cb = np.zeros((128, NCB), np.float32)
    r = np.arange(128)[:, None]
    c_ = np.arange(128)[None, :]
    cb[:, CB_ID:CB_ID + 128] = np.eye(128)
    cb[:, CB_ONES:CB_ONES + 128] = 1.0
    cb[:, CB_U:CB_U + 128] = (r <= c_)
    p = np.arange(128)
    col = np.arange(96)
    cb[:, CB_PMASK:CB_PMASK + 96] = ((p[:, None] // 64) == ((col[None, :] % 16) // 8))
    nm = np.zeros((128, 8, 96), np.float32)
    bq, iq = p // 8, p % 8
    for q in range(8):
        b2c, ic = (col % 16) // 8, col % 8
        nm[:, q, :] = (bq[:, None] == 2 * q + b2c[None, :]) & (iq[:, None] <= ic[None, :])
    cb[:, CB_NMASK:CB_NMASK + 768] = nm.reshape(128, 768)
    for g in range(2):
        dup = np.zeros((128, 128), np.float32)
        for b2 in range(2):
            for n in range(64):
                dup[64 * g + n, b2 * 64 + n] = 1.0
        off = CB_DUP0 if g == 0 else CB_DUP1
        cb[:, off:off + 128] = dup
    cb[:, CB_PARL:CB_PARL + 128] = ((c_ // 8) % 2 == (r // 64))
    return cb.astype(ml_dtypes.bfloat16)


def sample_inputs(inp, c, depth):
    m = {}
    ckv = np.asarray(inp["cache_kv_latent"])
    ckr = np.asarray(inp["cache_k_rope"])
    n_phys = ckv.shape[1]
    for l in range(depth):
        m["ckvc%d" % l] = ckv[l].reshape(n_phys, 16384)
        m["ckrc%d" % l] = ckr[l].reshape(n_phys, 4096)
    pt = np.asarray(inp["page_table"])[16 * c:16 * c + 16].astype(np.int32)
    m["pt"] = np.ascontiguousarray(pt.reshape(8, 2, 64).transpose(1, 2, 0).reshape(128, 8))
    sp = _f32(inp["state_pool"])[:depth, 16 * c:16 * c + 16]
    m["spool"] = np.ascontiguousarray(sp.transpose(0, 3, 1, 2))
    sc = _f32(inp["state_conv"])[:depth, 16 * c:16 * c + 16]
    m["sconv"] = np.ascontiguousarray(sc.transpose(0, 3, 1, 2))
    ss = _f32(inp["state_ssm"])[:depth, 16 * c:16 * c + 16]
    m["sssm"] = np.ascontiguousarray(ss.transpose(0, 1, 2, 4, 3).reshape(depth, 8, 2, 6, 64, 64))
    return m


def sample_outputs(res, depth):
    s_kv = np.concatenate([res[c]["o_skv"].reshape(depth, 16, 8, 128) for c in range(8)], axis=1)
    s_kr = np.concatenate([res[c]["o_skr"].reshape(depth, 16, 8, 32) for c in range(8)], axis=1)
    s_pool = np.concatenate([res[c]["o_spool"].transpose(0, 2, 3, 1) for c in range(8)], axis=1)
    s_conv = np.concatenate([res[c]["o_sconv"].transpose(0, 2, 3, 1) for c in range(8)], axis=1)
    s_ssm = np.concatenate([res[c]["o_sssm"].reshape(depth, 16, 6, 64, 64).transpose(0, 1, 2, 4, 3) for c in range(8)], axis=1)
    return [s_kv, s_kr, s_pool, s_conv, s_ssm]


def make_rope(S, past_len):
    NTOK = S + 128
    NTILE = S // 128 + 1
    inv = (np.float32(10000.0) ** (-np.arange(16, dtype=np.float32) * np.float32(2.0 / 32))).astype(np.float32)
    pos = np.concatenate([np.arange(S), past_len + (np.arange(128) % 8)]).astype(np.float32)
    ang = (pos[:, None] * inv[None, :]).astype(np.float32)
    cos = np.cos(ang).astype(np.float32)
    sin = np.sin(ang).astype(np.float32)
    ropeq = np.zeros((192, NTOK), np.float32)
    for hh in range(3):
        for d in range(32):
            ropeq[hh * 32 + d] = cos[:, d % 16]
            ropeq[96 + hh * 32 + d] = sin[:, d % 16] * (-1.0 if d < 16 else 1.0)
    ropek = np.zeros((128, 2, NTILE, 16), np.float32)
    ropek[:, 0] = cos.reshape(NTILE, 128, 16).transpose(1, 0, 2)
    ropek[:, 1] = sin.reshape(NTILE, 128, 16).transpose(1, 0, 2)
    return ropeq, ropek.reshape(128, 2 * NTILE * 16)


_PROG_CACHE = {}


def kernel(**inp):
    x_prompt = _f32(inp["x_prompt"])
    x_sample = _f32(inp["x_sample"])
    S = x_prompt.shape[1]
    depth = inp["w_in"].shape[0]
    page_table = np.asarray(inp["page_table"])
    n_pages = page_table.shape[1]
    n_phys = inp["cache_kv_latent"].shape[1]
    past_len = n_pages * 128
    key = (S, depth, n_pages, n_phys, WITH_SAMPLE)
    if key not in _PROG_CACHE:
        _PROG_CACHE[key] = build_program(S, depth, n_pages, n_phys, with_sample=WITH_SAMPLE)
    nc, _ = _PROG_CACHE[key]
    wimg = make_wimg(inp, depth)
    cf = make_cf(inp, depth)
    cb = make_cb()
    ropeq, ropek = make_rope(S, past_len)
    in_maps = []
    for c in range(8):
        xs = x_sample[16 * c:16 * c + 16].reshape(128, D)
        xT_in = np.ascontiguousarray(np.concatenate([x_prompt[c % 4], xs], axis=0).T)
        m = dict(xT_in=xT_in, wimg=wimg, cf=cf, cb=cb, ropeq=ropeq, ropek=ropek)
        if WITH_SAMPLE:
            m.update(sample_inputs(inp, c, depth))
        in_maps.append(m)
    res = run_bass_kernel_spmd(nc, in_maps, core_ids=list(range(8))).results
    B = x_prompt.shape[0]
    y_prompt = np.stack([res[s]["yT"][:, :S].T for s in range(B)])
    y_sample = np.concatenate([res[c]["yT"][:, S:].T.reshape(16, 8, D) for c in range(8)], axis=0)
    p_kv = np.stack([np.stack([res[s]["o_pkv"][l] for s in range(B)]) for l in range(depth)])
    p_kr = np.stack([np.stack([res[s]["o_pkr"][l] for s in range(B)]) for l in range(depth)])
    p_pool = np.stack([np.stack([res[s]["o_ppool"][l].T for s in range(B)]) for l in range(depth)])
    p_conv = np.stack([np.stack([res[s]["o_pconv"][l].T for s in range(B)]) for l in range(depth)])
    p_ssm = np.stack([np.stack([res[s]["o_pssm"][l].transpose(0, 2, 1) for s in range(B)]) for l in range(depth)])
    outs = [y_prompt, y_sample, p_kv, p_kr, p_pool, p_conv, p_ssm]
    if WITH_SAMPLE:
        outs += sample_outputs(res, depth)
    return tuple(np.ascontiguousarray(o, dtype=np.float32) for o in outs)
```

```python
import numpy as np
import ml_dtypes
import concourse.bass as bass
import concourse.mybir as mybir
from concourse.bass_utils import run_bass_kernel_spmd
WITH_SAMPLE = True
F32 = mybir.dt.float32
BF16 = mybir.dt.bfloat16
I32 = mybir.dt.int32
AF = mybir.ActivationFunctionType
ALU = mybir.AluOpType

NDMA_SEM = 8


class _Op:
    __slots__ = ("eng", "fn", "deps", "dma", "sig", "signal", "idx", "dsem", "dval")

    def __init__(self, eng, fn, dma):
        self.eng = eng
        self.fn = fn
        self.dma = dma
        self.deps = ()
        self.sig = 0
        self.signal = False
        self.dsem = None
        self.dval = 0


class Prog:
    ENGS = ("pe", "act", "dve", "pool", "sp")

    def __init__(self, nc):
        self.nc = nc
        self.ops = {e: [] for e in self.ENGS}
        self.last_w = {}
        self.readers = {}
        self.ndma = {e: 0 for e in self.ENGS}
        self._cms = []
        self.nops = 0

    def sb(self, name, shape, dt):
        cm = self.nc.sbuf_tensor(name, list(shape), dt)
        t = cm.__enter__()
        self._cms.append(cm)
        return t

    def ps(self, name, shape, dt=F32):
        cm = self.nc.psum_tensor(name, list(shape), dt)
        t = cm.__enter__()
        self._cms.append(cm)
        return t

    @staticmethod
    def key(ap):
        return ap.tensor.name

    def rec(self, eng, fn, reads, writes, dma=False, rkeys=(), wkeys=()):
        op = _Op(eng, fn, dma)
        rk = set(self.key(a) for a in reads if a is not None)
        rk.update(rkeys)
        wk = set(self.key(a) for a in writes if a is not None)
        wk.update(wkeys)
        for k in list(rk):
            if k.startswith("pb"):
                wk.add(k)
        deps = set()
        for k in rk:
            w = self.last_w.get(k)
            if w is not None:
                deps.add(w)
        for k in wk:
            w = self.last_w.get(k)
            if w is not None:
                deps.add(w)
            for r in self.readers.get(k, ()):
                deps.add(r)
        deps.discard(op)
        if eng == "pe" and not dma:
            deps = set(d for d in deps if not (d.eng == "pe" and not d.dma))
        for d in deps:
            d.signal = True
        op.deps = tuple(deps)
        for k in wk:
            self.last_w[k] = op
            self.readers[k] = []
        for k in rk:
            if k in wk:
                continue
            lst = self.readers.setdefault(k, [])
            if not dma:
                lst[:] = [r for r in lst if r.dma or r.eng != eng]
            lst.append(op)
        op.idx = self.nops
        self.nops += 1
        self.ops[eng].append(op)
        return op

    def finish(self, final_wait_ops=()):
        nc = self.nc
        sem_cms = []

        def mksem(name):
            cm = nc.semaphore(name)
            s = cm.__enter__()
            sem_cms.append(cm)
            return s

        esem = {e: mksem("s_" + e) for e in ("pe", "act", "dve", "pool")}
        dsems = {e: [mksem("d_%s%d" % (e, i)) for i in range(NDMA_SEM)] for e in ("sp", "pool", "act")}
        for e in self.ENGS:
            c = 0
            nd = 0
            for op in self.ops[e]:
                if op.dma:
                    op.dsem = dsems[e][nd % NDMA_SEM]
                    op.dval = 16 * (nd // NDMA_SEM + 1)
                    nd += 1
                else:
                    if op.signal:
                        c += 1
                        op.sig = c
        self.stats = {e: len(self.ops[e]) for e in self.ENGS}

        def emit_engine(ename, eng):
            waited = {}

            def wait(sem, val):
                k = sem.name
                if waited.get(k, 0) >= val:
                    return
                waited[k] = val
                eng.wait_ge(sem, val)

            for op in self.ops[ename]:
                need = {}
                for d in op.deps:
                    if d.dma:
                        s, v = d.dsem, d.dval
                    else:
                        s, v = esem[d.eng], d.sig
                    if need.get(s.name, (None, 0))[1] < v:
                        need[s.name] = (s, v)
                if op.dma and op.dval > 16:
                    s, v = op.dsem, op.dval - 16
                    if need.get(s.name, (None, 0))[1] < v:
                        need[s.name] = (s, v)
                for s, v in need.values():
                    wait(s, v)
                inst = op.fn(eng)
                if op.dma:
                    inst.then_inc(op.dsem, 16)
                elif op.signal:
                    inst.then_inc(esem[ename], 1)
            for op in final_wait_ops:
                if op.eng == ename:
                    wait(op.dsem, op.dval)

        with nc.Block() as block:
            @block.tensor
            def _(e):
                emit_engine("pe", e)

            @block.scalar
            def _(e):
                emit_engine("act", e)

            @block.vector
            def _(e):
                emit_engine("dve", e)

            @block.gpsimd
            def _(e):
                emit_engine("pool", e)

            @block.sync
            def _(e):
                emit_engine("sp", e)

        for cm in reversed(sem_cms):
            cm.__exit__(None, None, None)
        for cm in reversed(self._cms):
            cm.__exit__(None, None, None)

    def dma(self, out, in_, q="sp", rkey=None, wkey=None, **kw):
        return self.rec(q, lambda e: e.dma_start(out=out, in_=in_, **kw),
                        [] if rkey else [in_], [] if wkey else [out], dma=True,
                        rkeys=[rkey] if rkey else (), wkeys=[wkey] if wkey else ())

    def mm(self, out, lhsT, rhs, start=True, stop=True, **kw):
        return self.rec("pe", lambda e: e.matmul(out, lhsT, rhs, start=start, stop=stop, **kw),
                        [lhsT, rhs], [out])

    def tr(self, out, in_, ident):
        return self.rec("pe", lambda e: e.transpose(out, in_, ident), [in_, ident], [out])

    def act(self, out, in_, func, bias=None, scale=None, accum_out=None, eng="act"):
        kw = {}
        rd = [in_]
        if bias is not None:
            kw["bias"] = bias
            if not isinstance(bias, (int, float)):
                rd.append(bias)
        if scale is not None:
            kw["scale"] = scale
            if not isinstance(scale, (int, float)):
                rd.append(scale)
        wr = [out]
        if accum_out is not None:
            kw["accum_out"] = accum_out
            wr.append(accum_out)
        return self.rec("act", lambda e: e.activation(out, in_, func, **kw), rd, wr)

    def tt(self, out, in0, in1, op, eng="dve"):
        return self.rec(eng, lambda e: e.tensor_tensor(out, in0, in1, op), [in0, in1], [out])

    def ts(self, out, in0, s1, s2, op0, op1=None, eng="dve", accum_out=None):
        rd = [in0]
        for s in (s1, s2):
            if s is not None and not isinstance(s, (int, float)):
                rd.append(s)
        kw = {}
        wr = [out]
        if accum_out is not None:
            kw["accum_out"] = accum_out
            wr.append(accum_out)
        if op1 is None:
            return self.rec(eng, lambda e: e.tensor_scalar(out, in0, s1, s2, op0, **kw), rd, wr)
        return self.rec(eng, lambda e: e.tensor_scalar(out, in0, s1, s2, op0, op1, **kw), rd, wr)

    def stt(self, out, in0, scalar, in1, op0, op1, accum_out=None):
        rd = [in0, in1]
        if not isinstance(scalar, (int, float)):
            rd.append(scalar)
        kw = {}
        wr = [out]
        if accum_out is not None:
            kw["accum_out"] = accum_out
            wr.append(accum_out)
        return self.rec("dve", lambda e: e.scalar_tensor_tensor(out, in0, scalar, in1, op0, op1, **kw), rd, wr)

    def copy(self, out, in_, eng="dve"):
        if eng == "act":
            return self.rec("act", lambda e: e.copy(out, in_), [in_], [out])
        return self.rec(eng, lambda e: e.tensor_copy(out, in_), [in_], [out])

    def memset(self, ap, val, eng="pool"):
        return self.rec(eng, lambda e: e.memset(ap, val), [], [ap])

    def recip(self, out, in_):
        return self.rec("dve", lambda e: e.reciprocal(out, in_), [in_], [out])
D = 1024
T = 512
SLOT = 5120
NP = 24
NRING = 4
NGEN = 30
GW = 640
EPS = 1e-6
MLA_SCALE = 96 ** -0.5
POOL_W = (2, 4, 8, 16)
FFN_ORDER = [0, 1, 2, 3, 4, 5, 6, 7] + [x for i in range(7) for x in (9 + 2 * i, 8 + 2 * i)] + [22]
NSEQ = 16
NTS = 8

CF_PER_LAYER = 57 + 140
O_GMIX, O_GFFN, O_PSC, O_CW, O_CB, O_DSK, O_SSDG, O_QG, O_MLAG = 0, 8, 16, 18, 38, 43, 46, 49, 51
O_KVG, O_DTB, O_ALOG = 57, 185, 191


def cf_globals(depth):
    base = depth * CF_PER_LAYER
    o = {}
    for name, n in (("FING", 8), ("INVW", 2), ("U", 128), ("LS", 128), ("UB", 128), ("LSB", 128),
                    ("ONES", 128), ("INVC", 30), ("SEQM", 16), ("PARS", 2), ("PAIRM", 8), ("LASTM", 128)):
        o[name] = base
        base += n
    o["NCF"] = base
    return o


CB_ID, CB_ONES, CB_U, CB_PMASK, CB_NMASK, CB_DUP0, CB_DUP1, CB_PARL, NCB = 0, 128, 256, 384, 480, 1248, 1376, 1504, 1632
W_UQ, W_UK, W_UV, W_PW = 0, 1536, 1920, 2304


class TilePool:
    def __init__(self, P, n, prefix, shape, dt, psum=False):
        alloc = P.ps if psum else P.sb
        self.tiles = [alloc("%s%d" % (prefix, i), shape, dt) for i in range(n)]
        self.free_list = list(self.tiles)

    def get(self):
        assert self.free_list, "tile pool exhausted"
        return self.free_list.pop(0)

    def put(self, *ts):
        for t in ts:
            assert all(t is not f for f in self.free_list)
            self.free_list.append(t)


def gf(t, n, p0=0, p1=128):
    return t[p0:p1, 0:n]


def gb(t, n, p0=0, p1=128):
    return t[p0:p1, 0:(n + 1) // 2].bitcast(BF16)[:, 0:n]


class StopBuild(Exception):
    pass


def build_program(S, depth, n_pages, n_phys, with_sample=True):
    import os
    KSTAGE = float(os.environ.get("KSTAGE", "99"))

    def stage(k):
        if KSTAGE <= k:
            raise StopBuild()
    NB = S // T
    NTOK = S + 128
    NTILE = S // 128 + 1
    CG = cf_globals(depth)
    NCF = CG["NCF"]
    nc = bass.Bass("TRN2", target_bir_lowering=False)

    def din(name, shape, dt=F32):
        return nc.dram_tensor(name, list(shape), dt, kind="ExternalInput")

    def dout(name, shape, dt=F32):
        return nc.dram_tensor(name, list(shape), dt, kind="ExternalOutput")

    xT_in = din("xT_in", [D, NTOK])
    wimg = din("wimg", [depth * NP * 128, SLOT])
    cf_d = din("cf", [128, NCF])
    cb_d = din("cb", [128, NCB], BF16)
    ropeq = din("ropeq", [192, NTOK])
    ropek = din("ropek", [128, 2 * NTILE * 16])
    yT = dout("yT", [D, NTOK])
    o_pkv = dout("o_pkv", [depth, S, 128])
    o_pkr = dout("o_pkr", [depth, S, 32])
    o_ppool = dout("o_ppool", [depth, 256, 15])
    o_pconv = dout("o_pconv", [depth, 640, 3])
    o_pssm = dout("o_pssm", [depth, 6, 64, 64])
    if with_sample:
        assert n_pages == 64
        ckvc = [din("ckvc%d" % l, [n_phys, 16384]) for l in range(depth)]
        ckrc = [din("ckrc%d" % l, [n_phys, 4096]) for l in range(depth)]
        pt_d = din("pt", [128, 8], I32)
        spool = din("spool", [depth, 256, 16, 15])
        sconv = din("sconv", [depth, 640, 16, 3])
        sssm = din("sssm", [depth, 8, 2, 6, 64, 64])
        o_skv = dout("o_skv", [depth, 128, 128])
        o_skr = dout("o_skr", [depth, 128, 32])
        o_spool = dout("o_spool", [depth, 256, 16, 15])
        o_sconv = dout("o_sconv", [depth, 640, 16, 3])
        o_sssm = dout("o_sssm", [depth, 8, 2, 6, 64, 64])
    XT = nc.dram_tensor("XT", [D, NTOK], F32, kind="Internal")
    wb = nc.dram_tensor("wb", [depth * NP * 128, SLOT], BF16, kind="Internal")

    P = Prog(nc)
    outs = []
    cf = P.sb("cf_sb", [128, NCF], F32)
    cb = P.sb("cb_sb", [128, NCB], BF16)
    ring = [P.sb("ring%d" % i, [128, SLOT], BF16) for i in range(NRING)]
    wsm = P.sb("wsm", [128, 2560], BF16)
    xT = P.sb("xT", [128, 8 * T], F32)
    hT = P.sb("hT", [128, 8 * T], BF16)
    ckvT = P.sb("ckvT", [128, S + 128], BF16)
    ckv_tok = P.sb("ckv_tok", [128, (S // 128) * 128], BF16)
    kpeT3 = P.sb("kpeT3", [96, S + 128], BF16)
    uhist = P.sb("uhist", [128, 2 * 15], F32)
    xhist = P.sb("xhist", [128, 5 * 3], F32)
    xdt_pad = [P.sb("xdtp%d" % i, [128, 6 * 128], BF16) for i in range(2)]
    hTp = P.sb("hTp", [128, 6 * 128], BF16)
    hTf = P.sb("hTf", [128, 6 * 64], F32)
    dt_tok = P.sb("dt_tok", [128, 4 * 6], F32)
    a_tok = P.sb("a_tok", [128, 4 * 6], F32)
    A_row = P.sb("A_row", [128, 6], F32)
    if with_sample:
        xg = [P.sb("xg%d" % i, [128, 1024], F32) for i in range(2)]
        pt_sb = P.sb("pt_sb", [128, 8], I32)
    GEN = TilePool(P, NGEN, "g", [128, GW], F32)
    BANK = TilePool(P, 8, "pb", [128, 512], F32, psum=True)

    ident = cb[:, CB_ID:CB_ID + 128]
    ones_b = cb[:, CB_ONES:CB_ONES + 128]
    U_b = cb[:, CB_U:CB_U + 128]
    U_f = cf[:, CG["U"]:CG["U"] + 128]
    LS_f = cf[:, CG["LS"]:CG["LS"] + 128]
    ones_f = cf[:, CG["ONES"]:CG["ONES"] + 128]
    UB_f = cf[:, CG["UB"]:CG["UB"] + 128]
    LSB_f = cf[:, CG["LSB"]:CG["LSB"] + 128]
    parS_f = cf[:, CG["PARS"]:CG["PARS"] + 2]
    lastm_f = cf[:, CG["LASTM"]:CG["LASTM"] + 128]
    pairm_f = cf[:, CG["PAIRM"]:CG["PAIRM"] + 8]
    pmask_b = cb[:, CB_PMASK:CB_PMASK + 96]
    nmask_b = cb[:, CB_NMASK:CB_NMASK + 768]
    dup_b = [cb[:, CB_DUP0:CB_DUP0 + 128], cb[:, CB_DUP1:CB_DUP1 + 128]]
    parL_b = cb[:, CB_PARL:CB_PARL + 128]

    def cfc(l, off, n=1, p0=0, p1=128):
        o = l * CF_PER_LAYER + off
        return cf[p0:p1, o:o + n]

    P.dma(cf[:], cf_d.ap())
    P.dma(cb[:], cb_d.ap())
    if with_sample:
        P.dma(pt_sb[:], pt_d.ap())
    for l in range(depth):
        for pc in range(NP):
            r0 = (l * NP + pc) * 128
            P.dma(wb.ap()[r0:r0 + 128, :], wimg.ap()[r0:r0 + 128, :], q="pool", wkey="wb_%d_%d" % (l, pc))
    for t_ in xdt_pad:
        P.memset(t_[:], 0.0)

    sched = []
    for l in range(depth):
        for b in range(NB + (1 if with_sample else 0)):
            for pc in FFN_ORDER:
                sched.append((l, pc))
    rstate = {"issued": 0, "used": 0}

    def ring_issue():
        i = rstate["issued"]
        if i >= len(sched):
            return
        l, pc = sched[i]
        r0 = (l * NP + pc) * 128
        P.dma(ring[i % NRING][:], wb.ap()[r0:r0 + 128, :], rkey="wb_%d_%d" % (l, pc))
        rstate["issued"] += 1

    def ring_next(expect):
        i = rstate["used"]
        assert sched[i][1] == expect, (sched[i], expect)
        while rstate["issued"] < min(len(sched), i + NRING):
            ring_issue()
        if rstate["issued"] <= i:
            ring_issue()
        rstate["used"] += 1
        return ring[i % NRING]

    def rmsnorm_fm(srcs, gcols, dim, dsts, Pn, N):
        ps = BANK.get()
        n = len(srcs)
        for i, s in enumerate(srcs):
            sq = GEN.get()
            P.act(gb(sq, N, 0, Pn), s, AF.Square)
            P.mm(ps[0:Pn, 0:N], ones_b[0:Pn, 0:Pn], gb(sq, N, 0, Pn), start=(i == 0), stop=(i == n - 1))
            GEN.put(sq)
        rs = GEN.get()
        P.act(gf(rs, N, 0, Pn), ps[0:Pn, 0:N], AF.Sqrt, scale=1.0 / dim, bias=EPS)
        BANK.put(ps)
        P.recip(gf(rs, N, 0, Pn), gf(rs, N, 0, Pn))
        for i, s in enumerate(srcs):
            P.stt(dsts[i], s, gcols[i], gf(rs, N, 0, Pn), ALU.mult, ALU.mult)
        GEN.put(rs)

    def hTk(k, n0, n1):
        return hT[:, k * T + n0:k * T + n1]

    def xTk(k, n0, n1):
        return xT[:, k * T + n0:k * T + n1]

    def proj_fm(ps_out, w, wstride, c0, M, N):
        for k in range(8):
            P.mm(ps_out[0:M, 0:N], w[:, k * wstride + c0:k * wstride + c0 + M], hTk(k, 0, N),
                 start=(k == 0), stop=(k == 7))

    def ssd_chunk(l, ci, N0, xs_bf, BT, CT, yv, masks, sample=None):
        Uf, LSf = masks
        cc = slice(N0, N0 + 128)
        xp = xdt_pad[ci % 2]
        dtc = dt_tok[:, ci * 6:(ci + 1) * 6]
        ac = a_tok[:, ci * 6:(ci + 1) * 6]
        ptr = BANK.get()
        ptb = ptr[:, 0:256].bitcast(BF16)
        for j in range(3):
            P.tr(ptb[:, j * 128:(j + 1) * 128], gb(xs_bf[j], T)[:, cc], ident)
        P.tr(ptb[:, 384:512], gb(BT, T)[:, cc], ident)
        btok = GEN.get()
        P.copy(gb(btok, 128), ptb[:, 384:512], eng="act")
        xs3 = ptb[:, 0:384].rearrange("p (j e d) -> p j e d", j=3, e=2)
        xp4 = xp[:, :].rearrange("p (j e c) -> p j e c", j=3, e=2)
        dt3 = dtc.rearrange("p (j e) -> p j e", e=2)
        for e in range(2):
            P.tt(xp4[:, :, e, e * 64:e * 64 + 64], xs3[:, :, e, :],
                 dt3[:, :, e:e + 1].broadcast_to([128, 3, 64]), ALU.mult)
        stage(6.1)
        pcb = BANK.get()
        for g in range(2):
            P.mm(pcb[:, g * 128:(g + 1) * 128], gb(BT, T)[:, cc], gb(CT[g], T)[:, cc],
                 start=(g == 0), stop=(g == 1), skip_group_check=True)
        stage(6.15)
        cbm = GEN.get()
        P.tt(gf(cbm, 256).rearrange("p (g c) -> p g c", g=2), pcb[:, 0:256].rearrange("p (g c) -> p g c", g=2),
             Uf[:, None, :].broadcast_to([128, 2, 128]), ALU.mult)
        BANK.put(pcb)
        stage(6.2)
        aU = [GEN.get(), GEN.get()]
        for i in range(2):
            P.tt(gf(aU[i], 384).rearrange("p (h c) -> p h c", h=3),
                 Uf[:, None, :].broadcast_to([128, 3, 128]),
                 ac[:, 3 * i:3 * i + 3].unsqueeze(2).broadcast_to([128, 3, 128]), ALU.mult)
        stage(6.25)
        Lx = [GEN.get(), GEN.get()]
        Ex = [GEN.get(), GEN.get()]
        for i in range(2):
            pseg = BANK.get()
            pacs = BANK.get()
            for hh in range(3):
                rhs = gf(aU[i], 384)[:, hh * 128:(hh + 1) * 128]
                P.mm(pseg[:, hh * 128:(hh + 1) * 128], LSf, rhs, start=(hh == 0), stop=(hh == 2), skip_group_check=True)
            for hh in range(3):
                rhs = gf(aU[i], 384)[:, hh * 128:(hh + 1) * 128]
                P.mm(pacs[:, hh * 128:(hh + 1) * 128], ones_f, rhs, start=(hh == 0), stop=(hh == 2), skip_group_check=True)
            P.act(gf(Lx[i], 384), pseg[:, 0:384], AF.Exp)
            P.act(gf(Ex[i], 384), pacs[:, 0:384], AF.Exp)
            BANK.put(pseg, pacs)
        GEN.put(*aU)
        stage(6.3)
        MT = [GEN.get(), GEN.get()]
        CsT = GEN.get()
        for g in range(2):
            P.tt(gb(MT[g], 384).rearrange("p (h c) -> p h c", h=3),
                 gf(Lx[g], 384).rearrange("p (h c) -> p h c", h=3),
                 gf(cbm, 256)[:, g * 128:(g + 1) * 128].unsqueeze(1).broadcast_to([128, 3, 128]), ALU.mult, eng="pool")
            r0, r1 = 64 * g, 64 * g + 64
            P.tt(gb(CsT, 384, r0, r1).rearrange("p (h c) -> p h c", h=3),
                 gf(Ex[g], 384, r0, r1).rearrange("p (h c) -> p h c", h=3),
                 gb(CT[g], T, r0, r1)[:, cc].unsqueeze(1).broadcast_to([64, 3, 128]), ALU.mult, eng="pool")
        GEN.put(cbm)
        wdec = GEN.get()
        for g in range(2):
            if sample is None:
                P.tt(gf(wdec, 6)[:, 3 * g:3 * g + 3], dtc[:, 3 * g:3 * g + 3],
                     gf(Lx[g], 384).rearrange("p (h c) -> p h c", h=3)[:, :, 127], ALU.mult)
            else:
                tmpm = GEN.get()
                tv_ = gf(tmpm, 384).rearrange("p (h c) -> p h c", h=3)
                P.tt(tv_, gf(Lx[g], 384).rearrange("p (h c) -> p h c", h=3),
                     lastm_f.unsqueeze(1).broadcast_to([128, 3, 128]), ALU.mult, eng="pool")
                wl = gf(tmpm, 400)[:, 392:395]
                P.rec("dve", lambda e, o=wl, i=tv_: e.reduce_sum(o, i, mybir.AxisListType.X), [tv_], [wl])
                P.tt(gf(wdec, 6)[:, 3 * g:3 * g + 3], dtc[:, 3 * g:3 * g + 3], wl, ALU.mult)
                GEN.put(tmpm)
        xdtw = GEN.get()
        P.tt(gb(xdtw, 384).rearrange("p (h d) -> p h d", h=6), ptb[:, 0:384].rearrange("p (h d) -> p h d", h=6),
             gf(wdec, 6).unsqueeze(2).broadcast_to([128, 6, 64]), ALU.mult)
        GEN.put(wdec)
        BANK.put(ptr)
        stage(6.4)
        py = BANK.get()
        first = True
        for j in range(3):
            oc = slice(j * 128, (j + 1) * 128)
            for e in range(2):
                h = 2 * j + e
                g, hh = h // 3, h % 3
                P.mm(py[:, oc], xp[:, h * 128:(h + 1) * 128], gb(MT[g], 384)[:, hh * 128:(hh + 1) * 128],
                     start=first, stop=False, skip_group_check=True)
                first = False
            for e in range(2):
                h = 2 * j + e
                g, hh = h // 3, h % 3
                r0, r1 = 64 * g, 64 * g + 64
                if sample is None:
                    P.mm(py[:, oc], hTp[:, h * 128:(h + 1) * 128], gb(CsT, 384)[:, hh * 128:(hh + 1) * 128],
                         start=False, stop=(j == 2 and e == 1), skip_group_check=True)
        GEN.put(*MT)
        if sample is not None:
            sample["yoff"](py, CsT)
        for j in range(3):
            P.stt(gf(yv[j], T)[:, cc], gb(xs_bf[j], T)[:, cc], cfc(l, O_DSK + j), py[:, j * 128:(j + 1) * 128],
                  ALU.mult, ALU.add)
        BANK.put(py)
        GEN.put(CsT)
        stage(6.5)
        if sample is None:
            pst = BANK.get()
            for h in range(6):
                P.mm(pst[:, h * 64:(h + 1) * 64], gb(btok, 128), gb(xdtw, 384)[:, h * 64:(h + 1) * 64],
                     start=(h == 0), stop=(h == 5), skip_group_check=True)
            for h in range(6):
                g, hh = h // 3, h % 3
                r0, r1 = 64 * g, 64 * g + 64
                cd = gf(Ex[g], 384, r0, r1)[:, hh * 128 + 127:hh * 128 + 128]
                P.stt(hTf[r0:r1, h * 64:(h + 1) * 64], hTf[r0:r1, h * 64:(h + 1) * 64], cd,
                      pst[r0:r1, h * 64:(h + 1) * 64], ALU.mult, ALU.add)
                e = h % 2
                P.copy(hTp[r0:r1, h * 128 + e * 64:h * 128 + e * 64 + 64], hTf[r0:r1, h * 64:(h + 1) * 64], eng="pool")
            BANK.put(pst)
        else:
            sample["states"](btok, xdtw, Ex)
        GEN.put(btok, xdtw, *Lx, *Ex)

    def prompt_block(l, b):
        t0 = b * T
        last_layer = (l == depth - 1)
        src = xT_in if l == 0 else XT
        P.dma(xT[:].rearrange("p (k t) -> p k t", k=8),
              src.ap().rearrange("(k p) t -> p k t", p=128)[:, :, t0:t0 + T], rkey="XT_%d" % t0)
        if b == 0:
            P.dma(wsm[:], wb.ap()[(l * NP + 23) * 128:(l * NP + 24) * 128, 0:2560], rkey="wb_%d_23" % l)
            P.act(A_row[:], cfc(l, O_ALOG, 6), AF.Exp)
            P.ts(A_row[:], A_row[:], -1.0, None, ALU.mult)
            P.memset(hTp[:], 0.0)
            P.memset(hTf[:], 0.0)
            P.memset(uhist[:], 0.0)
            P.memset(xhist[:], 0.0)
        stage(0)
        rmsnorm_fm([xTk(k, 0, T) for k in range(8)], [cfc(l, O_GMIX + k) for k in range(8)], D,
                   [hTk(k, 0, T) for k in range(8)], 128, T)
        stage(1)
        w1 = ring_next(0)
        uet = [GEN.get(), GEN.get()]
        for c in range(2):
            ps = BANK.get()
            proj_fm(ps, w1, 512, c * 128, 128, T)
            P.copy(gf(uet[c], 15 + T)[:, 15:15 + T], ps[:, 0:T], eng="act")
            P.copy(gf(uet[c], 15), uhist[:, c * 15:(c + 1) * 15], eng="pool")
            BANK.put(ps)
        psq = [BANK.get(), BANK.get()]
        for c in range(2):
            proj_fm(psq[c], w1, 512, 256 + c * 128, 128, T)
        cqn = [GEN.get(), GEN.get()]
        rmsnorm_fm([psq[c][:, 0:T] for c in range(2)], [cfc(l, O_QG + c) for c in range(2)], 256,
                   [gb(cqn[c], T) for c in range(2)], 128, T)
        BANK.put(*psq)
        E_ = 15 + T
        m_bf = [GEN.get(), GEN.get()]
        for c in range(2):
            ue = gf(uet[c], 15 + T)
            sel = GEN.get()
            t2 = GEN.get()
            se, t2f = gf(sel, E_), gf(t2, E_)
            if c == 0:
                P.tt(se[0:64, 1:E_], ue[0:64, 1:E_], ue[0:64, 0:E_ - 1], ALU.add, eng="pool")
                P.tt(t2f[64:128, 1:E_], ue[64:128, 1:E_], ue[64:128, 0:E_ - 1], ALU.add, eng="pool")
                P.tt(se[64:128, 3:E_], t2f[64:128, 3:E_], t2f[64:128, 1:E_ - 2], ALU.add, eng="pool")
            else:
                t4 = GEN.get()
                t4f = gf(t4, E_)
                P.tt(t2f[:, 1:E_], ue[:, 1:E_], ue[:, 0:E_ - 1], ALU.add, eng="pool")
                P.tt(t4f[:, 3:E_], t2f[:, 3:E_], t2f[:, 1:E_ - 2], ALU.add, eng="pool")
                P.tt(se[0:64, 7:E_], t4f[0:64, 7:E_], t4f[0:64, 3:E_ - 4], ALU.add, eng="pool")
                P.tt(t2f[64:128, 7:E_], t4f[64:128, 7:E_], t4f[64:128, 3:E_ - 4], ALU.add, eng="pool")
                P.tt(se[64:128, 15:E_], t2f[64:128, 15:E_], t2f[64:128, 7:E_ - 8], ALU.add, eng="pool")
                GEN.put(t4)
            P.stt(gb(m_bf[c], T), se[:, 15:E_], cf[:, CG["INVW"] + c:CG["INVW"] + c + 1], ue[:, 15:E_],
                  ALU.mult, ALU.subtract)
            if b == 0:
                tq = GEN.get()
                P.tt(gf(tq, 15), se[:, 15:30], cf[:, CG["INVC"] + c * 15:CG["INVC"] + c * 15 + 15], ALU.mult)
                P.tt(gb(m_bf[c], T)[:, 0:15], gf(tq, 15), ue[:, 15:30], ALU.subtract)
                GEN.put(tq)
            GEN.put(sel, t2)
            if b == NB - 1:
                outs.append(P.dma(o_ppool.ap()[l, c * 128:(c + 1) * 128, :], ue[:, T:T + 15]))
            P.copy(uhist[:, c * 15:(c + 1) * 15], ue[:, T:T + 15], eng="pool")
            GEN.put(uet[c])
        stage(2)
        w2 = ring_next(1)
        zs = [GEN.get() for _ in range(3)]
        for j in range(3):
            ps = BANK.get()
            proj_fm(ps, w2, 550, j * 128, 128, T)
            P.act(gb(zs[j], T), ps[:, 0:T], AF.Silu)
            BANK.put(ps)
        ptk = [BANK.get(), BANK.get()]
        for i in range(4):
            pso = ptk[i // 2][:, (i % 2) * 166:(i % 2) * 166 + 166]
            for k in range(8):
                P.mm(pso, hTk(k, i * 128, (i + 1) * 128), w2[:, k * 550 + 384:k * 550 + 550],
                     start=(k == 0 and i % 2 == 0), stop=(k == 7), skip_group_check=True)
        tokmajor_post(l, t0, S // 128, b * 4, 4, ptk, o_pkv.ap()[l, t0:t0 + T, :], o_pkr.ap()[l, t0:t0 + T, :])
        BANK.put(*ptk)
        stage(3)
        w3 = ring_next(2)
        xs_bf = [GEN.get() for _ in range(3)]
        BT = GEN.get()
        CT = [GEN.get(), GEN.get()]
        P.memset(gb(CT[0], T, 64, 128), 0.0)
        P.memset(gb(CT[1], T, 0, 64), 0.0)
        for j in range(5):
            ps = BANK.get()
            proj_fm(ps, w3, 640, j * 128, 128, T)
            xet = GEN.get()
            xe = gf(xet, 3 + T)
            P.copy(xe[:, 3:3 + T], ps[:, 0:T], eng="act")
            P.copy(xe[:, 0:3], xhist[:, j * 3:(j + 1) * 3], eng="pool")
            BANK.put(ps)
            acc = GEN.get()
            af_ = gf(acc, T)
            P.act(af_, xe[:, 0:T], AF.Identity, scale=cfc(l, O_CW + j * 4), bias=cfc(l, O_CB + j))
            for k in range(1, 4):
                P.stt(af_, xe[:, k:k + T], cfc(l, O_CW + j * 4 + k), af_, ALU.mult, ALU.add)
            if j < 4:
                dst = xs_bf[j] if j < 3 else BT
                P.act(gb(dst, T), af_, AF.Silu)
            else:
                P.act(gb(CT[0], T, 0, 64), af_[0:64, :], AF.Silu)
                P.act(gb(CT[1], T, 64, 128), af_[64:128, :], AF.Silu)
            GEN.put(acc)
            if b == NB - 1:
                outs.append(P.dma(o_pconv.ap()[l, j * 128:(j + 1) * 128, :], xe[:, T:T + 3]))
            P.copy(xhist[:, j * 3:(j + 1) * 3], xe[:, T:T + 3], eng="pool")
            GEN.put(xet)
        stage(4)
        qlat, qpe = q_path(l, t0, T, cqn)
        stage(5)
        GEN.put(*cqn)
        mla_tiles, mlan = attention_prompt(l, b, qlat, qpe)
        stage(6)
        yv = [GEN.get() for _ in range(3)]
        for ci in range(4):
            ssd_chunk(l, ci, ci * 128, xs_bf, BT, CT, yv, (U_f, LS_f))
        GEN.put(BT, *CT, *xs_bf)
        if b == NB - 1:
            for h in range(6):
                g = h // 3
                outs.append(P.dma(o_pssm.ap()[l, h, :, :], hTf[64 * g:64 * g + 64, h * 64:(h + 1) * 64]))
        stage(7)
        mixT = [GEN.get() for _ in range(5)]
        for j in range(3):
            P.tt(gf(yv[j], T), gf(yv[j], T), gb(zs[j], T), ALU.mult, eng="pool")
        GEN.put(*zs)
        rmsnorm_fm([gf(yv[j], T) for j in range(3)], [cfc(l, O_SSDG + j) for j in range(3)], 384,
                   [gb(mixT[2 + j], T) for j in range(3)], 128, T)
        GEN.put(*yv)
        for c in range(2):
            ps = BANK.get()
            P.mm(ps[:, 0:T], wsm[:, W_PW + c * 128:W_PW + (c + 1) * 128], gb(m_bf[c], T), start=True, stop=True)
            P.act(gb(mixT[c], T), ps[:, 0:T], AF.Copy, scale=cfc(l, O_PSC + c))
            BANK.put(ps)
        GEN.put(*m_bf)
        dense_tail(l, T, mixT, mlan, t0)
        GEN.put(*mixT, *mla_tiles)

    def tokmajor_post(l, t0, tile_cap, tg0, ntile, ptk, dkv, dkr, per_bank=2, keep_ckb=False):
        nb_ = (ntile + per_bank - 1) // per_bank
        ss = GEN.get()
        junk = GEN.get()
        for i in range(ntile):
            v = ptk[i // per_bank][:, (i % per_bank) * 166:(i % per_bank) * 166 + 128]
            P.act(gf(junk, 128), v, AF.Square, accum_out=gf(ss, ntile)[:, i:i + 1])
        GEN.put(junk)
        P.act(gf(ss, ntile), gf(ss, ntile), AF.Sqrt, scale=1.0 / 128, bias=EPS)
        P.recip(gf(ss, ntile), gf(ss, ntile))
        ckn = GEN.get()
        for i in range(ntile):
            v = ptk[i // per_bank][:, (i % per_bank) * 166:(i % per_bank) * 166 + 128]
            P.stt(gf(ckn, ntile * 128)[:, i * 128:(i + 1) * 128], v, gf(ss, ntile)[:, i:i + 1],
                  cfc(l, O_KVG, 128), ALU.mult, ALU.mult)
        GEN.put(ss)
        outs.append(P.dma(dkv.rearrange("(i p) c -> p i c", p=128),
                          gf(ckn, ntile * 128).rearrange("p (i c) -> p i c", i=ntile)))
        ckb = GEN.get()
        P.copy(gb(ckb, ntile * 128), gf(ckn, ntile * 128), eng="act")
        GEN.put(ckn)
        if tg0 + ntile <= tile_cap:
            P.copy(ckv_tok[:, tg0 * 128:(tg0 + ntile) * 128], gb(ckb, ntile * 128), eng="pool")
        pT_ = BANK.get()
        pTb = pT_[:, 0:256].bitcast(BF16)
        for i in range(ntile):
            P.tr(pTb[:, i * 128:(i + 1) * 128], gb(ckb, ntile * 128)[:, i * 128:(i + 1) * 128], ident)
        P.copy(ckvT[:, t0:t0 + ntile * 128], pTb[:, 0:ntile * 128], eng="act")
        BANK.put(pT_)
        kpr = GEN.get()
        kv = gf(kpr, ntile * 32).rearrange("p (i c) -> p i c", i=ntile)
        tm = [GEN.get() for _ in range(4)]
        rkt = GEN.get()
        P.dma(gf(rkt, ntile * 16), ropek.ap()[:, tg0 * 16:(tg0 + ntile) * 16])
        P.dma(gf(rkt, 2 * ntile * 16)[:, ntile * 16:2 * ntile * 16],
              ropek.ap()[:, (NTILE + tg0) * 16:(NTILE + tg0 + ntile) * 16])
        cosv = gf(rkt, ntile * 16).rearrange("p (i c) -> p i c", c=16)
        sinv = gf(rkt, 2 * ntile * 16)[:, ntile * 16:2 * ntile * 16].rearrange("p (i c) -> p i c", c=16)
        for bi in range(nb_):
            n_in = min(per_bank, ntile - bi * per_bank)
            bv = ptk[bi][:, 0:per_bank * 166].rearrange("p (i c) -> p i c", i=per_bank)
            x1 = bv[:, 0:n_in, 128:144]
            x2 = bv[:, 0:n_in, 144:160]
            tg = bi * per_bank
            cs_ = cosv[:, tg:tg + n_in, :]
            sn_ = sinv[:, tg:tg + n_in, :]
            tv = [gf(t_, n_in * 16).rearrange("p (i c) -> p i c", i=n_in) for t_ in tm]
            P.tt(tv[0], x1, cs_, ALU.mult)
            P.tt(tv[1], x2, sn_, ALU.mult)
            P.tt(tv[2], x1, sn_, ALU.mult)
            P.tt(tv[3], x2, cs_, ALU.mult)
            i0 = bi * per_bank
            P.tt(kv[:, i0:i0 + n_in, 0:16], tv[0], tv[1], ALU.subtract, eng="pool")
            P.tt(kv[:, i0:i0 + n_in, 16:32], tv[2], tv[3], ALU.add, eng="pool")
        for t_ in tm:
            GEN.put(t_)
        GEN.put(rkt)
        outs.append(P.dma(dkr.rearrange("(i p) c -> p i c", p=128), kv))
        kp3 = GEN.get()
        P.copy(gb(kp3, ntile * 96).rearrange("p (i r c) -> p i r c", i=ntile, r=3),
               kv.unsqueeze(2).broadcast_to([128, ntile, 3, 32]), eng="act")
        GEN.put(kpr)
        pT2 = BANK.get()
        pT2b = pT2[:, 0:256].bitcast(BF16)
        for i in range(ntile):
            P.tr(pT2b[0:96, i * 128:(i + 1) * 128], gb(kp3, ntile * 96)[:, i * 96:(i + 1) * 96], ident)
        P.copy(kpeT3[:, t0:t0 + ntile * 128], pT2b[0:96, 0:ntile * 128], eng="act")
        BANK.put(pT2)
        GEN.put(kp3)
        if not keep_ckb:
            GEN.put(ckb)
        tx = GEN.get()
        for bi in range(nb_):
            n_in = min(per_bank, ntile - bi * per_bank)
            bv = ptk[bi][:, 0:per_bank * 166].rearrange("p (i c) -> p i c", i=per_bank)
            i0 = bi * per_bank
            P.tt(gf(tx, ntile * 6).rearrange("p (i c) -> p i c", i=ntile)[:, i0:i0 + n_in, :], bv[:, 0:n_in, 160:166],
                 cfc(l, O_DTB, 6).unsqueeze(1).broadcast_to([128, n_in, 6]), ALU.add)
        P.act(gf(tx, ntile * 6), gf(tx, ntile * 6), AF.Exp)
        P.act(dt_tok[:, 0:ntile * 6], gf(tx, ntile * 6), AF.Ln, bias=1.0)
        GEN.put(tx)
        P.tt(a_tok[:, 0:ntile * 6].rearrange("p (i c) -> p i c", i=ntile),
             dt_tok[:, 0:ntile * 6].rearrange("p (i c) -> p i c", i=ntile),
             A_row[:, :].unsqueeze(1).broadcast_to([128, ntile, 6]), ALU.mult)
        return ckb if keep_ckb else None

    def q_path(l, t0, N, cqn):
        qn = [GEN.get() for _ in range(3)]
        for j in range(3):
            ps = BANK.get()
            for k in range(2):
                P.mm(ps[:, 0:N], wsm[:, W_UQ + k * 768 + j * 128:W_UQ + k * 768 + (j + 1) * 128], gb(cqn[k], N),
                     start=(k == 0), stop=(k == 1))
            P.copy(gb(qn[j], N), ps[:, 0:N], eng="act")
            BANK.put(ps)
        cosT = GEN.get()
        sinT = GEN.get()
        P.dma(gf(cosT, N, 0, 96), ropeq.ap()[0:96, t0:t0 + N])
        P.dma(gf(sinT, N, 0, 96), ropeq.ap()[96:192, t0:t0 + N])
        qpe = [GEN.get(), GEN.get()]
        for x in range(2):
            pp = BANK.get()
            psw = BANK.get()
            for k in range(2):
                c0 = W_UQ + k * 768 + 384 + x * 96
                P.mm(pp[0:96, 0:N], wsm[:, c0:c0 + 96], gb(cqn[k], N), start=(k == 0), stop=(k == 1))
            for k in range(2):
                c0 = W_UQ + k * 768 + 576 + x * 96
                P.mm(psw[0:96, 0:N], wsm[:, c0:c0 + 96], gb(cqn[k], N), start=(k == 0), stop=(k == 1))
            ta = GEN.get()
            tb_ = GEN.get()
            P.tt(gf(ta, N, 0, 96), pp[0:96, 0:N], gf(cosT, N, 0, 96), ALU.mult)
            P.tt(gf(tb_, N, 0, 96), psw[0:96, 0:N], gf(sinT, N, 0, 96), ALU.mult)
            P.tt(gb(qpe[x], N, 0, 96), gf(ta, N, 0, 96), gf(tb_, N, 0, 96), ALU.add, eng="pool")
            BANK.put(pp, psw)
            GEN.put(ta, tb_)
        GEN.put(cosT, sinT)
        qlat = [GEN.get() for _ in range(6)]
        for h in range(6):
            r0 = 64 * (h % 2)
            ps = BANK.get()
            P.mm(ps[:, 0:N], wsm[r0:r0 + 64, W_UK + (h // 2) * 128:W_UK + (h // 2 + 1) * 128],
                 gb(qn[h // 2], N, r0, r0 + 64), start=True, stop=True)
            P.copy(gb(qlat[h], N), ps[:, 0:N], eng="act")
            BANK.put(ps)
        GEN.put(*qn)
        return qlat, qpe

    def attention_prompt(l, b, qlat, qpe):
        nkt = 4 * b + 4
        ot = [GEN.get() for _ in range(3)]
        oTv = [gb(ot[h // 2], 2 * T, 0, 64)[:, (h % 2) * T:(h % 2 + 1) * T] for h in range(6)]
        pending = []

        def finalize(h, psO, psD):
            rden = GEN.get()
            P.recip(gf(rden, T), psD[:, 0:T])
            olat = GEN.get()
            P.tt(gb(olat, T), psO[:, 0:T], gf(rden, T), ALU.mult)
            BANK.put(psO, psD)
            GEN.put(rden)
            ps = BANK.get()
            P.mm(ps[0:64, 0:T], wsm[:, W_UV + h * 64:W_UV + (h + 1) * 64], gb(olat, T), start=True, stop=True)
            P.copy(oTv[h], ps[0:64, 0:T], eng="act")
            BANK.put(ps)
            GEN.put(olat, qlat[h])

        for h in range(6):
            psO = BANK.get()
            psD = BANK.get()
            rr = 32 * (h % 3)

            def stage_a(kt, h=h, rr=rr):
                i = kt - 4 * b
                q0 = max(0, i) * 128
                psS = BANK.get()
                P.mm(psS[:, q0:T], ckvT[:, kt * 128:(kt + 1) * 128], gb(qlat[h], T)[:, q0:T], start=True, stop=False)
                P.mm(psS[:, q0:T], kpeT3[rr:rr + 32, kt * 128:(kt + 1) * 128], gb(qpe[h // 3], T, rr, rr + 32)[:, q0:T],
                     start=False, stop=True)
                pT = GEN.get()
                P.act(gb(pT, T)[:, q0:T], psS[:, q0:T], AF.Exp, scale=MLA_SCALE)
                BANK.put(psS)
                if i >= 0:
                    P.tt(gb(pT, T)[:, q0:q0 + 128], gb(pT, T)[:, q0:q0 + 128], U_b, ALU.mult, eng="pool")
                return (kt, pT, q0)

            def stage_b(kt, pT, q0, psO=psO, psD=psD):
                P.mm(psO[:, q0:T], ckv_tok[:, kt * 128:(kt + 1) * 128], gb(pT, T)[:, q0:T],
                     start=(kt == 0), stop=(kt == nkt - 1), skip_group_check=True)
                P.mm(psD[:, q0:T], ones_b, gb(pT, T)[:, q0:T],
                     start=(kt == 0), stop=(kt == nkt - 1), skip_group_check=True)
                GEN.put(pT)

            fifo = []
            for kt in range(nkt):
                fifo.append(stage_a(kt))
                if kt == 1 and pending:
                    pending.pop(0)()
                la = 1 if pending else 2
                while len(fifo) > la:
                    stage_b(*fifo.pop(0))
            while fifo:
                stage_b(*fifo.pop(0))
            pending.append(lambda h=h, psO=psO, psD=psD: finalize(h, psO, psD))
        while pending:
            pending.pop(0)()
        GEN.put(*qpe)
        rmsnorm_fm(oTv, [cfc(l, O_MLAG + h, 1, 0, 64) for h in range(6)], 384, oTv, 64, T)
        return ot, oTv

    def dense_tail(l, N, mixT, mlan, t0):
        last_layer = (l == depth - 1)
        for jm in range(4):
            w = ring_next(3 + jm)
            for e in range(2):
                mc = 2 * jm + e
                ps = BANK.get()
                for kc in range(5):
                    P.mm(ps[:, 0:N], w[:, kc * 256 + e * 128:kc * 256 + (e + 1) * 128], gb(mixT[kc], N),
                         start=(kc == 0), stop=False)
                for h in range(6):
                    kc = 5 + h
                    P.mm(ps[:, 0:N], w[0:64, kc * 256 + e * 128:kc * 256 + (e + 1) * 128], mlan[h][:, 0:N],
                         start=False, stop=(h == 5))
                P.tt(xTk(mc, 0, N), ps[:, 0:N], xTk(mc, 0, N), ALU.add)
                BANK.put(ps)
        rmsnorm_fm([xTk(k, 0, N) for k in range(8)], [cfc(l, O_GFFN + k) for k in range(8)], D,
                   [hTk(k, 0, N) for k in range(8)], 128, N)
        def ffn_up(i):
            wu = ring_next(7 + 2 * i)
            a = [GEN.get() for _ in range(4)]
            for fc in range(4):
                ps = BANK.get()
                proj_fm(ps, wu, 512, fc * 128, 128, N)
                r = GEN.get()
                P.act(gb(r, N), ps[:, 0:N], AF.Relu)
                BANK.put(ps)
                P.tt(gb(a[fc], N), gb(r, N), gb(r, N), ALU.mult, eng="pool")
                GEN.put(r)
            return a

        def ffn_down(i, a):
            wd = ring_next(8 + 2 * i)
            for mc in range(8):
                ps = BANK.get()
                for fc in range(4):
                    P.mm(ps[:, 0:N], wd[:, fc * 1024 + mc * 128:fc * 1024 + (mc + 1) * 128], gb(a[fc], N),
                         start=(fc == 0), stop=(fc == 3))
                P.tt(xTk(mc, 0, N), ps[:, 0:N], xTk(mc, 0, N), ALU.add)
                BANK.put(ps)
            GEN.put(*a)

        a_cur = ffn_up(0)
        for i in range(8):
            a_next = ffn_up(i + 1) if i < 7 else None
            ffn_down(i, a_cur)
            a_cur = a_next
        xv = xT[:].rearrange("p (k t) -> p k t", k=8)[:, :, 0:N]
        if not last_layer:
            P.dma(XT.ap().rearrange("(k p) t -> p k t", p=128)[:, :, t0:t0 + N], xv, wkey="XT_%d" % t0)
        else:
            yo = [GEN.get() for _ in range(8)]
            rmsnorm_fm([xTk(k, 0, N) for k in range(8)], [cf[:, CG["FING"] + k:CG["FING"] + k + 1] for k in range(8)], D,
                       [gf(yo[k], N) for k in range(8)], 128, N)
            for k in range(8):
                outs.append(P.dma(yT.ap()[k * 128:(k + 1) * 128, t0:t0 + N], gf(yo[k], N)))
            GEN.put(*yo)


    def attention_sample(l, qlat, qpe, ckb_new):
        QL = GEN.get()
        QP = GEN.get()
        for h in range(6):
            P.copy(gb(QL, 768).rearrange("p (q h t) -> p q h t", q=8, h=6)[:, :, h, :],
                   gb(qlat[h], 128).rearrange("p (q t) -> p q t", q=8), eng="pool")
        P.memset(gb(QP, 768, 0, 96), 0.0)
        for h in range(6):
            rr = 32 * (h % 3)
            P.copy(gb(QP, 768, rr, rr + 32).rearrange("p (q h t) -> p q h t", q=8, h=6)[:, :, h, :],
                   gb(qpe[h // 3], 128, rr, rr + 32).rearrange("p (q t) -> p q t", q=8), eng="pool")
        GEN.put(*qlat, *qpe)
        OL = GEN.get()
        gic = [0]
        for q in range(8):
            QLq = gb(QL, 768)[:, q * 96:(q + 1) * 96]
            QPq = gb(QP, 768, 0, 96)[:, q * 96:(q + 1) * 96]
            psO = BANK.get()
            fo = [True]

            def st_a(ch, q=q):
                X = xg[gic[0] % 2]
                gic[0] += 1
                P.rec("pool", lambda e, X=X, q=q, ch=ch: e.indirect_dma_start(
                    out=X[:, 0:1024], out_offset=None, in_=ckvc[l].ap(),
                    in_offset=bass.IndirectOffsetOnAxis(ap=pt_sb[:, q:q + 1], axis=0), element_offset=ch * 1024),
                    [pt_sb[:]], [X[:]], dma=True, rkeys=["ckvc%d" % l])
                Xr = GEN.get()
                P.rec("pool", lambda e, Xr=Xr, q=q, ch=ch: e.indirect_dma_start(
                    out=gf(Xr, 256), out_offset=None, in_=ckrc[l].ap(),
                    in_offset=bass.IndirectOffsetOnAxis(ap=pt_sb[:, q:q + 1], axis=0), element_offset=ch * 256),
                    [pt_sb[:]], [gf(Xr, 256)], dma=True, rkeys=["ckrc%d" % l])
                Xb = GEN.get()
                P.copy(gb(Xb, 1024), X[:, 0:1024], eng="dve")
                Xr3 = GEN.get()
                P.copy(gb(Xr3, 768).rearrange("p (t r c) -> p t r c", t=8, r=3),
                       gf(Xr, 256).rearrange("p (t c) -> p t c", t=8).unsqueeze(2).broadcast_to([128, 8, 3, 32]), eng="act")
                GEN.put(Xr)
                pT = BANK.get()
                pTb = pT[:, 0:512].bitcast(BF16)
                for t in range(8):
                    P.tr(pTb[:, t * 128:(t + 1) * 128], gb(Xb, 1024)[:, t * 128:(t + 1) * 128], ident)
                KT = GEN.get()
                P.copy(gb(KT, 1024), pTb[:, 0:1024], eng="act")
                BANK.put(pT)
                pT2 = BANK.get()
                pT2b = pT2[:, 0:512].bitcast(BF16)
                for t in range(8):
                    P.tr(pT2b[0:96, t * 128:(t + 1) * 128], gb(Xr3, 768)[:, t * 96:(t + 1) * 96], ident)
                KrT = GEN.get()
                P.copy(gb(KrT, 1024, 0, 96), pT2b[0:96, 0:1024], eng="dve")
                BANK.put(pT2)
                GEN.put(Xr3)
                return (Xb, KT, KrT)

            def st_b1(Xb, KT, KrT, QLq=QLq, QPq=QPq):
                psS = [BANK.get(), BANK.get()]
                for t in range(8):
                    bk = psS[t // 5]
                    c0 = (t % 5) * 96
                    P.mm(bk[:, c0:c0 + 96], gb(KT, 1024)[:, t * 128:(t + 1) * 128], QLq,
                         start=(t % 5 == 0), stop=False, skip_group_check=True)
                    P.mm(bk[:, c0:c0 + 96], gb(KrT, 1024, 0, 96)[:, t * 128:(t + 1) * 128], QPq,
                         start=False, stop=True, skip_group_check=True)
                GEN.put(KT, KrT)
                PT = GEN.get()
                P.act(gb(PT, 768)[:, 0:480], psS[0][:, 0:480], AF.Exp, scale=MLA_SCALE)
                P.act(gb(PT, 768)[:, 480:768], psS[1][:, 0:288], AF.Exp, scale=MLA_SCALE)
                BANK.put(*psS)
                P.tt(gb(PT, 768).rearrange("p (t c) -> p t c", t=8), gb(PT, 768).rearrange("p (t c) -> p t c", t=8),
                     pmask_b.unsqueeze(1).broadcast_to([128, 8, 96]), ALU.mult, eng="pool")
                return (Xb, PT)

            def st_b2(Xb, PT, psO=psO, fo=fo):
                for t in range(8):
                    P.mm(psO[0:96, 0:128], gb(PT, 768)[:, t * 96:(t + 1) * 96], gb(Xb, 1024)[:, t * 128:(t + 1) * 128],
                         start=fo[0], stop=False, skip_group_check=True)
                    fo[0] = False
                    P.mm(psO[0:96, 128:129], gb(PT, 768)[:, t * 96:(t + 1) * 96], ones_b[:, 0:1],
                         start=False, stop=False, skip_group_check=True)
                GEN.put(PT, Xb)

            aq, bq = [], []
            for it in range(16 + 2):
                if it < 16:
                    aq.append(st_a(it))
                if 1 <= it <= 16:
                    bq.append(st_b1(*aq.pop(0)))
                if it >= 2:
                    st_b2(*bq.pop(0))
            psN = BANK.get()
            P.mm(psN[:, 0:96], ckvT[:, S:S + 128], QLq, start=True, stop=False)
            P.mm(psN[:, 0:96], kpeT3[0:96, S:S + 128], QPq, start=False, stop=True)
            PN = GEN.get()
            P.act(gb(PN, 96), psN[:, 0:96], AF.Exp, scale=MLA_SCALE)
            BANK.put(psN)
            P.tt(gb(PN, 96), gb(PN, 96), nmask_b[:, q * 96:(q + 1) * 96], ALU.mult, eng="pool")
            P.mm(psO[0:96, 0:128], gb(PN, 96), gb(ckb_new, 128), start=False, stop=False, skip_group_check=True)
            P.mm(psO[0:96, 128:129], gb(PN, 96), ones_b[:, 0:1], start=False, stop=True, skip_group_check=True)
            GEN.put(PN)
            rd = GEN.get()
            P.recip(gf(rd, 1, 0, 96), psO[0:96, 128:129])
            ol = GEN.get()
            P.ts(gb(ol, 128, 0, 96), psO[0:96, 0:128], gf(rd, 1, 0, 96), None, ALU.mult)
            BANK.put(psO)
            GEN.put(rd)
            pT3 = BANK.get()
            pT3b = pT3[:, 0:256].bitcast(BF16)
            P.tr(pT3b[:, 0:96], gb(ol, 128, 0, 96), ident[0:96, 0:96])
            P.copy(gb(OL, 768).rearrange("p (h q t) -> p h q t", h=6, q=8)[:, :, q, :],
                   pT3b[:, 0:96].rearrange("p (h t) -> p h t", h=6), eng="act")
            BANK.put(pT3)
            GEN.put(ol)
        GEN.put(QL, QP, ckb_new)
        ot = [GEN.get() for _ in range(3)]
        oTv = [gb(ot[h // 2], 2 * T, 0, 64)[:, (h % 2) * T:(h % 2) * T + 128] for h in range(6)]
        for h in range(6):
            ps = BANK.get()
            P.mm(ps[0:64, 0:128], wsm[:, W_UV + h * 64:W_UV + (h + 1) * 64], gb(OL, 768)[:, h * 128:(h + 1) * 128],
                 start=True, stop=True)
            P.copy(oTv[h], ps[0:64, 0:128], eng="act")
            BANK.put(ps)
        GEN.put(OL)
        rmsnorm_fm(oTv, [cfc(l, O_MLAG + h, 1, 0, 64) for h in range(6)], 384, oTv, 64, 128)
        return ot, oTv

    def sample_block(l):
        N = 128
        src = xT_in if l == 0 else XT
        xv = xT[:].rearrange("p (k t) -> p k t", k=8)[:, :, 0:N]
        P.dma(xv, src.ap().rearrange("(k p) t -> p k t", p=128)[:, :, S:S + N], rkey="XT_%d" % S)
        rmsnorm_fm([xTk(k, 0, N) for k in range(8)], [cfc(l, O_GMIX + k) for k in range(8)], D,
                   [hTk(k, 0, N) for k in range(8)], 128, N)
        w1 = ring_next(0)
        uet = [GEN.get(), GEN.get()]
        uv = [gf(uet[c], 368).rearrange("p (b e) -> p b e", b=16) for c in range(2)]
        for c in range(2):
            ps = BANK.get()
            proj_fm(ps, w1, 512, c * 128, 128, N)
            P.dma(uv[c][:, :, 0:15], spool.ap()[l, c * 128:(c + 1) * 128, :, :])
            P.copy(uv[c][:, :, 15:23], ps[:, 0:N].rearrange("p (b i) -> p b i", b=16), eng="act")
            BANK.put(ps)
        psq = [BANK.get(), BANK.get()]
        for c in range(2):
            proj_fm(psq[c], w1, 512, 256 + c * 128, 128, N)
        cqn = [GEN.get(), GEN.get()]
        rmsnorm_fm([psq[c][:, 0:N] for c in range(2)], [cfc(l, O_QG + c) for c in range(2)], 256,
                   [gb(cqn[c], N) for c in range(2)], 128, N)
        BANK.put(*psq)
        m_bf = [GEN.get(), GEN.get()]
        for c in range(2):
            ue = uv[c]
            sel = GEN.get()
            t2 = GEN.get()
            se = gf(sel, 368).rearrange("p (b e) -> p b e", b=16)
            t2f = gf(t2, 368).rearrange("p (b e) -> p b e", b=16)
            E_ = 23
            if c == 0:
                P.tt(se[0:64, :, 1:E_], ue[0:64, :, 1:E_], ue[0:64, :, 0:E_ - 1], ALU.add, eng="pool")
                P.tt(t2f[64:128, :, 1:E_], ue[64:128, :, 1:E_], ue[64:128, :, 0:E_ - 1], ALU.add, eng="pool")
                P.tt(se[64:128, :, 3:E_], t2f[64:128, :, 3:E_], t2f[64:128, :, 1:E_ - 2], ALU.add, eng="pool")
            else:
                t4 = GEN.get()
                t4f = gf(t4, 368).rearrange("p (b e) -> p b e", b=16)
                P.tt(t2f[:, :, 1:E_], ue[:, :, 1:E_], ue[:, :, 0:E_ - 1], ALU.add, eng="pool")
                P.tt(t4f[:, :, 3:E_], t2f[:, :, 3:E_], t2f[:, :, 1:E_ - 2], ALU.add, eng="pool")
                P.tt(se[0:64, :, 7:E_], t4f[0:64, :, 7:E_], t4f[0:64, :, 3:E_ - 4], ALU.add, eng="pool")
                P.tt(t2f[64:128, :, 7:E_], t4f[64:128, :, 7:E_], t4f[64:128, :, 3:E_ - 4], ALU.add, eng="pool")
                P.tt(se[64:128, :, 15:E_], t2f[64:128, :, 15:E_], t2f[64:128, :, 7:E_ - 8], ALU.add, eng="pool")
                GEN.put(t4)
            P.stt(gb(m_bf[c], N).rearrange("p (b i) -> p b i", b=16), se[:, :, 15:E_],
                  cf[:, CG["INVW"] + c:CG["INVW"] + c + 1], ue[:, :, 15:E_], ALU.mult, ALU.subtract)
            GEN.put(sel, t2)
            outs.append(P.dma(o_spool.ap()[l, c * 128:(c + 1) * 128, :, :], ue[:, :, 8:23]))
            GEN.put(uet[c])
        w2 = ring_next(1)
        zs = [GEN.get() for _ in range(3)]
        for j in range(3):
            ps = BANK.get()
            proj_fm(ps, w2, 550, j * 128, 128, N)
            P.act(gb(zs[j], N), ps[:, 0:N], AF.Silu)
            BANK.put(ps)
        ptk = [BANK.get()]
        for k in range(8):
            P.mm(ptk[0][:, 0:166], hTk(k, 0, 128), w2[:, k * 550 + 384:k * 550 + 550], start=(k == 0), stop=(k == 7))
        ckb_new = tokmajor_post(l, S, 0, S // 128, 1, ptk, o_skv.ap()[l, :, :], o_skr.ap()[l, :, :], keep_ckb=True)
        BANK.put(*ptk)
        w3 = ring_next(2)
        xs_bf = [GEN.get() for _ in range(3)]
        BT = GEN.get()
        CT = [GEN.get(), GEN.get()]
        P.memset(gb(CT[0], N, 64, 128), 0.0)
        P.memset(gb(CT[1], N, 0, 64), 0.0)
        for j in range(5):
            ps = BANK.get()
            proj_fm(ps, w3, 640, j * 128, 128, N)
            xet = GEN.get()
            xe = gf(xet, 176).rearrange("p (b e) -> p b e", b=16)
            P.dma(xe[:, :, 0:3], sconv.ap()[l, j * 128:(j + 1) * 128, :, :])
            P.copy(xe[:, :, 3:11], ps[:, 0:N].rearrange("p (b i) -> p b i", b=16), eng="act")
            BANK.put(ps)
            acc = GEN.get()
            af_ = gf(acc, N)
            a3 = af_.rearrange("p (b i) -> p b i", b=16)
            P.act(a3, xe[:, :, 0:8], AF.Identity, scale=cfc(l, O_CW + j * 4), bias=cfc(l, O_CB + j))
            for k in range(1, 4):
                P.stt(a3, xe[:, :, k:k + 8], cfc(l, O_CW + j * 4 + k), a3, ALU.mult, ALU.add)
            if j < 4:
                dst = xs_bf[j] if j < 3 else BT
                P.act(gb(dst, N), af_, AF.Silu)
            else:
                P.act(gb(CT[0], N, 0, 64), af_[0:64, :], AF.Silu)
                P.act(gb(CT[1], N, 64, 128), af_[64:128, :], AF.Silu)
            GEN.put(acc)
            outs.append(P.dma(o_sconv.ap()[l, j * 128:(j + 1) * 128, :, :], xe[:, :, 8:11]))
            GEN.put(xet)
        qlat, qpe = q_path(l, S, N, cqn)
        GEN.put(*cqn)
        mla_tiles, mlan = attention_sample(l, qlat, qpe, ckb_new)
        yv = [GEN.get() for _ in range(3)]

        def yoff(py, CsT):
            CsX = [GEN.get(), GEN.get()]
            for g in range(2):
                pdup = BANK.get()
                P.mm(pdup[:, 0:384], dup_b[g], gb(CsT, 384), start=True, stop=True)
                P.tt(gb(CsX[g], 384).rearrange("p (h c) -> p h c", h=3), pdup[:, 0:384].rearrange("p (h c) -> p h c", h=3),
                     parL_b.unsqueeze(1).broadcast_to([128, 3, 128]), ALU.mult)
                BANK.put(pdup)
            for h in range(6):
                g, hh, j, e = h // 3, h % 3, h // 2, h % 2
                H0f = GEN.get()
                for b2 in range(2):
                    P.dma(gf(H0f, 512, 64 * b2, 64 * b2 + 64).rearrange("p (q c) -> p q c", q=8),
                          sssm.ap()[l, :, b2, h, :, :].rearrange("q n p -> n q p"))
                H0p = GEN.get()
                P.memset(gb(H0p, 1024), 0.0)
                P.copy(gb(H0p, 1024).rearrange("p (q c) -> p q c", q=8)[:, :, e * 64:(e + 1) * 64],
                       gf(H0f, 512).rearrange("p (q c) -> p q c", q=8), eng="pool")
                GEN.put(H0f)
                for q in range(8):
                    P.mm(py[:, j * 128 + 16 * q:j * 128 + 16 * q + 16], gb(H0p, 1024)[:, q * 128:(q + 1) * 128],
                         gb(CsX[g], 384)[:, hh * 128 + 16 * q:hh * 128 + 16 * q + 16],
                         start=False, stop=False, skip_group_check=True)
                GEN.put(H0p)
            GEN.put(*CsX)

        def states(btok, xdtw, Ex):
            Bpar = [GEN.get(), GEN.get()]
            for g in range(2):
                for b2 in range(2):
                    P.ts(gb(Bpar[g], 128)[:, b2 * 64:(b2 + 1) * 64], gb(btok, 128)[:, g * 64:(g + 1) * 64],
                         parS_f[:, b2:b2 + 1], None, ALU.mult)
            for h in range(6):
                g, hh = h // 3, h % 3
                xq = GEN.get()
                P.tt(gb(xq, 512).rearrange("p (q c) -> p q c", q=8),
                     gb(xdtw, 384)[:, h * 64:(h + 1) * 64].unsqueeze(1).broadcast_to([128, 8, 64]),
                     pairm_f.unsqueeze(2).broadcast_to([128, 8, 64]), ALU.mult)
                pst = BANK.get()
                P.mm(pst[:, 0:512], gb(Bpar[g], 128), gb(xq, 512), start=True, stop=True)
                GEN.put(xq)
                H0f = GEN.get()
                for b2 in range(2):
                    P.dma(gf(H0f, 512, 64 * b2, 64 * b2 + 64).rearrange("p (q c) -> p q c", q=8),
                          sssm.ap()[l, :, b2, h, :, :].rearrange("q n p -> n q p"))
                hn = GEN.get()
                for b2 in range(2):
                    r0, r1 = 64 * b2, 64 * b2 + 64
                    cdv = gf(Ex[g], 384, r0, r1)[:, hh * 128:(hh + 1) * 128].rearrange("p (q r) -> p q r", r=16)[:, :, 8 * b2 + 7]
                    P.tt(gf(hn, 512, r0, r1).rearrange("p (q c) -> p q c", q=8),
                         gf(H0f, 512, r0, r1).rearrange("p (q c) -> p q c", q=8),
                         cdv.unsqueeze(2).broadcast_to([64, 8, 64]), ALU.mult, eng="pool")
                P.tt(gf(hn, 512), gf(hn, 512), pst[:, 0:512], ALU.add)
                BANK.put(pst)
                GEN.put(H0f)
                for b2 in range(2):
                    outs.append(P.dma(o_sssm.ap()[l, :, b2, h, :, :].rearrange("q n p -> n q p"),
                                      gf(hn, 512, 64 * b2, 64 * b2 + 64).rearrange("p (q c) -> p q c", q=8)))
                GEN.put(hn)
            GEN.put(*Bpar)

        ssd_chunk(l, 0, 0, xs_bf, BT, CT, yv, (UB_f, LSB_f), sample={"yoff": yoff, "states": states})
        GEN.put(BT, *CT, *xs_bf)
        mixT = [GEN.get() for _ in range(5)]
        for j in range(3):
            P.tt(gf(yv[j], N), gf(yv[j], N), gb(zs[j], N), ALU.mult, eng="pool")
        GEN.put(*zs)
        rmsnorm_fm([gf(yv[j], N) for j in range(3)], [cfc(l, O_SSDG + j) for j in range(3)], 384,
                   [gb(mixT[2 + j], N) for j in range(3)], 128, N)
        GEN.put(*yv)
        for c in range(2):
            ps = BANK.get()
            P.mm(ps[:, 0:N], wsm[:, W_PW + c * 128:W_PW + (c + 1) * 128], gb(m_bf[c], N), start=True, stop=True)
            P.act(gb(mixT[c], N), ps[:, 0:N], AF.Copy, scale=cfc(l, O_PSC + c))
            BANK.put(ps)
        GEN.put(*m_bf)
        dense_tail(l, N, mixT, mlan, S)
        GEN.put(*mixT, *mla_tiles)

    try:
        for l in range(depth):
            for b in range(NB):
                prompt_block(l, b)
            if with_sample:
                sample_block(l)
    except StopBuild:
        pass
    P.finish(final_wait_ops=outs)
    return nc, P
def _f32(a):
    return np.ascontiguousarray(np.asarray(a, dtype=np.float32))


def _pad_piece(a):
    a = a.reshape(128, -1)
    out = np.zeros((128, SLOT), np.float32)
    out[:, :a.shape[1]] = a
    return out


def make_wimg(inp, depth):
    w_in, w_out, w_up, w_down = (_f32(inp[k]) for k in ("w_in", "w_out", "w_up", "w_down"))
    w_uq, w_uk, w_uv, pool_w = (_f32(inp[k]) for k in ("w_uq", "w_uk", "w_uv", "pool_w"))
    img = np.zeros((depth, NP, 128, SLOT), np.float32)

    def kmaj(a):
        K = a.shape[0] // 128
        return a.reshape(K, 128, a.shape[1]).transpose(1, 0, 2)

    c1 = list(range(0, 256)) + list(range(1286, 1542))
    c2 = list(range(256, 640)) + list(range(1542, 1702)) + list(range(1280, 1286))
    c3 = list(range(640, 1280))
    nope = [h * 96 + d for h in range(6) for d in range(64)]
    peA = [h * 96 + 64 + d for h in range(3) for d in range(32)]
    peB = [h * 96 + 64 + d for h in range(3, 6) for d in range(32)]
    swA = [h * 96 + 64 + (d + 16) % 32 for h in range(3) for d in range(32)]
    swB = [h * 96 + 64 + (d + 16) % 32 for h in range(3, 6) for d in range(32)]
    uqcols = nope + peA + peB + swA + swB
    for l in range(depth):
        img[l, 0] = _pad_piece(kmaj(w_in[l][:, c1]))
        img[l, 1] = _pad_piece(kmaj(w_in[l][:, c2]))
        img[l, 2] = _pad_piece(kmaj(w_in[l][:, c3]))
        for jm in range(4):
            blk = np.zeros((128, 11, 256), np.float32)
            mc = slice(jm * 256, (jm + 1) * 256)
            for kc in range(5):
                blk[:, kc, :] = w_out[l][kc * 128:(kc + 1) * 128, mc]
            for h in range(6):
                blk[0:64, 5 + h, :] = w_out[l][640 + h * 64:640 + (h + 1) * 64, mc]
            img[l, 3 + jm] = _pad_piece(blk)
        for i in range(8):
            img[l, 7 + 2 * i] = _pad_piece(kmaj(w_up[l][:, 512 * i:512 * (i + 1)]))
            img[l, 8 + 2 * i] = _pad_piece(kmaj(w_down[l][512 * i:512 * (i + 1), :]))
        sm = np.zeros((128, 2560), np.float32)
        sm[:, W_UQ:W_UQ + 1536] = kmaj(w_uq[l][:, uqcols]).reshape(128, 1536)
        wk = w_uk[l].reshape(128, 3, 2, 64).transpose(2, 3, 1, 0).reshape(128, 3 * 128)
        sm[:, W_UK:W_UK + 384] = wk
        sm[:, W_UV:W_UV + 384] = w_uv[l].reshape(128, 384)
        pw = np.zeros((128, 2, 128), np.float32)
        for c in range(2):
            for e in range(2):
                pw[e * 64:(e + 1) * 64, c, e * 64:(e + 1) * 64] = pool_w[l][2 * c + e]
        sm[:, W_PW:W_PW + 256] = pw.reshape(128, 256)
        img[l, 23] = _pad_piece(sm)
    return img.reshape(depth * NP * 128, SLOT)


def make_cf(inp, depth):
    CG = cf_globals(depth)
    cf = np.zeros((128, CG["NCF"]), np.float32)
    p = np.arange(128)

    def col8(v):
        return _f32(v).reshape(-1, 128).T

    for l in range(depth):
        o = l * CF_PER_LAYER
        cf[:, o + O_GMIX:o + O_GMIX + 8] = col8(inp["norm_mix_g"][l])
        cf[:, o + O_GFFN:o + O_GFFN + 8] = col8(inp["norm_ffn_g"][l])
        cf[:, o + O_PSC:o + O_PSC + 2] = col8(inp["pool_scale"][l])
        cw = _f32(inp["conv_w"][l])
        for j in range(5):
            for k in range(4):
                cf[:, o + O_CW + j * 4 + k] = cw[k, j * 128:(j + 1) * 128]
        cf[:, o + O_CB:o + O_CB + 5] = col8(inp["conv_b"][l])
        ds = _f32(inp["d_skip"][l])
        for j in range(3):
            cf[:, o + O_DSK + j] = ds[2 * j + p // 64]
        cf[:, o + O_SSDG:o + O_SSDG + 3] = col8(inp["ssd_norm_g"][l])
        cf[:, o + O_QG:o + O_QG + 2] = col8(inp["q_norm_g"][l])
        mg = _f32(inp["mla_out_g"][l]).reshape(6, 64)
        cf[0:64, o + O_MLAG:o + O_MLAG + 6] = mg.T
        cf[:, o + O_KVG:o + O_KVG + 128] = _f32(inp["kv_norm_g"][l])[None, :]
        cf[:, o + O_DTB:o + O_DTB + 6] = _f32(inp["dt_bias"][l])[None, :]
        cf[:, o + O_ALOG:o + O_ALOG + 6] = _f32(inp["a_log"][l])[None, :]
    cf[:, CG["FING"]:CG["FING"] + 8] = col8(inp["final_norm_g"])
    for c in range(2):
        w = np.array([POOL_W[2 * c + q // 64] for q in range(128)], np.float32)
        cf[:, CG["INVW"] + c] = 1.0 / w
        for t in range(15):
            cf[:, CG["INVC"] + c * 15 + t] = 1.0 / np.minimum(w, t + 1)
    r = np.arange(128)[:, None]
    c_ = np.arange(128)[None, :]
    cf[:, CG["U"]:CG["U"] + 128] = (r <= c_)
    cf[:, CG["LS"]:CG["LS"] + 128] = (r > c_)
    same = (r // 8 == c_ // 8)
    cf[:, CG["UB"]:CG["UB"] + 128] = (r <= c_) & same
    cf[:, CG["LSB"]:CG["LSB"] + 128] = (r > c_) & same
    cf[:, CG["ONES"]:CG["ONES"] + 128] = 1.0
    cf[:, CG["SEQM"]:CG["SEQM"] + 16] = (np.arange(128)[:, None] // 8 == np.arange(16)[None, :])
    sidx = np.arange(128)[:, None]
    cf[:, CG["PARS"]:CG["PARS"] + 2] = ((sidx // 8) % 2 == np.arange(2)[None, :])
    cf[:, CG["PAIRM"]:CG["PAIRM"] + 8] = (sidx // 16 == np.arange(8)[None, :])
    cf[:, CG["LASTM"]:CG["LASTM"] + 128] = (np.arange(128)[None, :] == 8 * (sidx // 8) + 7)
    return cf


def make_cb():
    cb = np.zeros((128, NCB), np.float32)
    r = np.arange(128)[:, None]
    c_ = np.arange(128)[None, :]
    cb[:, CB_ID:CB_ID + 128] = np.eye(128)
    cb[:, CB_ONES:CB_ONES + 128] = 1.0
    cb[:, CB_U:CB_U + 128] = (r <= c_)
    p = np.arange(128)
    col = np.arange(96)
    cb[:, CB_PMASK:CB_PMASK + 96] = ((p[:, None] // 64) == ((col[None, :] % 16) // 8))
    nm = np.zeros((128, 8, 96), np.float32)
    bq, iq = p // 8, p % 8
    for q in range(8):
        b2c, ic = (col % 16) // 8, col % 8
        nm[:, q, :] = (bq[:, None] == 2 * q + b2c[None, :]) & (iq[:, None] <= ic[None, :])
    cb[:, CB_NMASK:CB_NMASK + 768] = nm.reshape(128, 768)
    for g in range(2):
        dup = np.zeros((128, 128), np.float32)
        for b2 in range(2):
            for n in range(64):
                dup[64 * g + n, b2 * 64 + n] = 1.0
        off = CB_DUP0 if g == 0 else CB_DUP1
        cb[:, off:off + 128] = dup
    cb[:, CB_PARL:CB_PARL + 128] = ((c_ // 8) % 2 == (r // 64))
    return cb.astype(ml_dtypes.bfloat16)


def sample_inputs(inp, c, depth):
    m = {}
    ckv = np.asarray(inp["cache_kv_latent"])
    ckr = np.asarray(inp["cache_k_rope"])
    n_phys = ckv.shape[1]
    for l in range(depth):
        m["ckvc%d" % l] = ckv[l].reshape(n_phys, 16384)
        m["ckrc%d" % l] = ckr[l].reshape(n_phys, 4096)
    pt = np.asarray(inp["page_table"])[16 * c:16 * c + 16].astype(np.int32)
    m["pt"] = np.ascontiguousarray(pt.reshape(8, 2, 64).transpose(1, 2, 0).reshape(128, 8))
    sp = _f32(inp["state_pool"])[:depth, 16 * c:16 * c + 16]
    m["spool"] = np.ascontiguousarray(sp.transpose(0, 3, 1, 2))
    sc = _f32(inp["state_conv"])[:depth, 16 * c:16 * c + 16]
    m["sconv"] = np.ascontiguousarray(sc.transpose(0, 3, 1, 2))
    ss = _f32(inp["state_ssm"])[:depth, 16 * c:16 * c + 16]
    m["sssm"] = np.ascontiguousarray(ss.transpose(0, 1, 2, 4, 3).reshape(depth, 8, 2, 6, 64, 64))
    return m


def sample_outputs(res, depth):
    s_kv = np.concatenate([res[c]["o_skv"].reshape(depth, 16, 8, 128) for c in range(8)], axis=1)
    s_kr = np.concatenate([res[c]["o_skr"].reshape(depth, 16, 8, 32) for c in range(8)], axis=1)
    s_pool = np.concatenate([res[c]["o_spool"].transpose(0, 2, 3, 1) for c in range(8)], axis=1)
    s_conv = np.concatenate([res[c]["o_sconv"].transpose(0, 2, 3, 1) for c in range(8)], axis=1)
    s_ssm = np.concatenate([res[c]["o_sssm"].reshape(depth, 16, 6, 64, 64).transpose(0, 1, 2, 4, 3) for c in range(8)], axis=1)
    return [s_kv, s_kr, s_pool, s_conv, s_ssm]


def make_rope(S, past_len):
    NTOK = S + 128
    NTILE = S // 128 + 1
    inv = (np.float32(10000.0) ** (-np.arange(16, dtype=np.float32) * np.float32(2.0 / 32))).astype(np.float32)
    pos = np.concatenate([np.arange(S), past_len + (np.arange(128) % 8)]).astype(np.float32)
    ang = (pos[:, None] * inv[None, :]).astype(np.float32)
    cos = np.cos(ang).astype(np.float32)
    sin = np.sin(ang).astype(np.float32)
    ropeq = np.zeros((192, NTOK), np.float32)
    for hh in range(3):
        for d in range(32):
            ropeq[hh * 32 + d] = cos[:, d % 16]
            ropeq[96 + hh * 32 + d] = sin[:, d % 16] * (-1.0 if d < 16 else 1.0)
    ropek = np.zeros((128, 2, NTILE, 16), np.float32)
    ropek[:, 0] = cos.reshape(NTILE, 128, 16).transpose(1, 0, 2)
    ropek[:, 1] = sin.reshape(NTILE, 128, 16).transpose(1, 0, 2)
    return ropeq, ropek.reshape(128, 2 * NTILE * 16)


_PROG_CACHE = {}


def kernel(**inp):
    x_prompt = _f32(inp["x_prompt"])
    x_sample = _f32(inp["x_sample"])
    S = x_prompt.shape[1]
    depth = inp["w_in"].shape[0]
    page_table = np.asarray(inp["page_table"])
    n_pages = page_table.shape[1]
    n_phys = inp["cache_kv_latent"].shape[1]
    past_len = n_pages * 128
    key = (S, depth, n_pages, n_phys, WITH_SAMPLE)
    if key not in _PROG_CACHE:
        _PROG_CACHE[key] = build_program(S, depth, n_pages, n_phys, with_sample=WITH_SAMPLE)
    nc, _ = _PROG_CACHE[key]
    wimg = make_wimg(inp, depth)
    cf = make_cf(inp, depth)
    cb = make_cb()
    ropeq, ropek = make_rope(S, past_len)
    in_maps = []
    for c in range(8):
        xs = x_sample[16 * c:16 * c + 16].reshape(128, D)
        xT_in = np.ascontiguousarray(np.concatenate([x_prompt[c % 4], xs], axis=0).T)
        m = dict(xT_in=xT_in, wimg=wimg, cf=cf, cb=cb, ropeq=ropeq, ropek=ropek)
        if WITH_SAMPLE:
            m.update(sample_inputs(inp, c, depth))
        in_maps.append(m)
    res = run_bass_kernel_spmd(nc, in_maps, core_ids=list(range(8))).results
    B = x_prompt.shape[0]
    y_prompt = np.stack([res[s]["yT"][:, :S].T for s in range(B)])
    y_sample = np.concatenate([res[c]["yT"][:, S:].T.reshape(16, 8, D) for c in range(8)], axis=0)
    p_kv = np.stack([np.stack([res[s]["o_pkv"][l] for s in range(B)]) for l in range(depth)])
    p_kr = np.stack([np.stack([res[s]["o_pkr"][l] for s in range(B)]) for l in range(depth)])
    p_pool = np.stack([np.stack([res[s]["o_ppool"][l].T for s in range(B)]) for l in range(depth)])
    p_conv = np.stack([np.stack([res[s]["o_pconv"][l].T for s in range(B)]) for l in range(depth)])
    p_ssm = np.stack([np.stack([res[s]["o_pssm"][l].transpose(0, 2, 1) for s in range(B)]) for l in range(depth)])
    outs = [y_prompt, y_sample, p_kv, p_kr, p_pool, p_conv, p_ssm]
    if WITH_SAMPLE:
        outs += sample_outputs(res, depth)
    return tuple(np.ascontiguousarray(o, dtype=np.float32) for o in outs)
```

```python
import numpy as np
import ml_dtypes
import concourse.bass as bass
import concourse.mybir as mybir
from concourse.bass_utils import run_bass_kernel_spmd
WITH_SAMPLE = True
F32 = mybir.dt.float32
BF16 = mybir.dt.bfloat16
I32 = mybir.dt.int32
AF = mybir.ActivationFunctionType
ALU = mybir.AluOpType

NDMA_SEM = 8


class _Op:
    __slots__ = ("eng", "fn", "deps", "dma", "sig", "signal", "idx", "dsem", "dval")

    def __init__(self, eng, fn, dma):
        self.eng = eng
        self.fn = fn
        self.dma = dma
        self.deps = ()
        self.sig = 0
        self.signal = False
        self.dsem = None
        self.dval = 0


class Prog:
    ENGS = ("pe", "act", "dve", "pool", "sp")

    def __init__(self, nc):
        self.nc = nc
        self.ops = {e: [] for e in self.ENGS}
        self.last_w = {}
        self.readers = {}
        self.ndma = {e: 0 for e in self.ENGS}
        self._cms = []
        self.nops = 0

    def sb(self, name, shape, dt):
        cm = self.nc.sbuf_tensor(name, list(shape), dt)
        t = cm.__enter__()
        self._cms.append(cm)
        return t

    def ps(self, name, shape, dt=F32):
        cm = self.nc.psum_tensor(name, list(shape), dt)
        t = cm.__enter__()
        self._cms.append(cm)
        return t

    @staticmethod
    def key(ap):
        return ap.tensor.name

    def rec(self, eng, fn, reads, writes, dma=False, rkeys=(), wkeys=()):
        op = _Op(eng, fn, dma)
        rk = set(self.key(a) for a in reads if a is not None)
        rk.update(rkeys)
        wk = set(self.key(a) for a in writes if a is not None)
        wk.update(wkeys)
        for k in list(rk):
            if k.startswith("pb"):
                wk.add(k)
        deps = set()
        for k in rk:
            w = self.last_w.get(k)
            if w is not None:
                deps.add(w)
        for k in wk:
            w = self.last_w.get(k)
            if w is not None:
                deps.add(w)
            for r in self.readers.get(k, ()):
                deps.add(r)
        deps.discard(op)
        if eng == "pe" and not dma:
            deps = set(d for d in deps if not (d.eng == "pe" and not d.dma))
        for d in deps:
            d.signal = True
        op.deps = tuple(deps)
        for k in wk:
            self.last_w[k] = op
            self.readers[k] = []
        for k in rk:
            if k in wk:
                continue
            lst = self.readers.setdefault(k, [])
            if not dma:
                lst[:] = [r for r in lst if r.dma or r.eng != eng]
            lst.append(op)
        op.idx = self.nops
        self.nops += 1
        self.ops[eng].append(op)
        return op

    def finish(self, final_wait_ops=()):
        nc = self.nc
        sem_cms = []

        def mksem(name):
            cm = nc.semaphore(name)
            s = cm.__enter__()
            sem_cms.append(cm)
            return s

        esem = {e: mksem("s_" + e) for e in ("pe", "act", "dve", "pool")}
        dsems = {e: [mksem("d_%s%d" % (e, i)) for i in range(NDMA_SEM)] for e in ("sp", "pool", "act")}
        for e in self.ENGS:
            c = 0
            nd = 0
            for op in self.ops[e]:
                if op.dma:
                    op.dsem = dsems[e][nd % NDMA_SEM]
                    op.dval = 16 * (nd // NDMA_SEM + 1)
                    nd += 1
                else:
                    if op.signal:
                        c += 1
                        op.sig = c
        self.stats = {e: len(self.ops[e]) for e in self.ENGS}

        def emit_engine(ename, eng):
            waited = {}

            def wait(sem, val):
                k = sem.name
                if waited.get(k, 0) >= val:
                    return
                waited[k] = val
                eng.wait_ge(sem, val)

            for op in self.ops[ename]:
                need = {}
                for d in op.deps:
                    if d.dma:
                        s, v = d.dsem, d.dval
                    else:
                        s, v = esem[d.eng], d.sig
                    if need.get(s.name, (None, 0))[1] < v:
                        need[s.name] = (s, v)
                if op.dma and op.dval > 16:
                    s, v = op.dsem, op.dval - 16
                    if need.get(s.name, (None, 0))[1] < v:
                        need[s.name] = (s, v)
                for s, v in need.values():
                    wait(s, v)
                inst = op.fn(eng)
                if op.dma:
                    inst.then_inc(op.dsem, 16)
                elif op.signal:
                    inst.then_inc(esem[ename], 1)
            for op in final_wait_ops:
                if op.eng == ename:
                    wait(op.dsem, op.dval)

        with nc.Block() as block:
            @block.tensor
            def _(e):
                emit_engine("pe", e)

            @block.scalar
            def _(e):
                emit_engine("act", e)

            @block.vector
            def _(e):
                emit_engine("dve", e)

            @block.gpsimd
            def _(e):
                emit_engine("pool", e)

            @block.sync
            def _(e):
                emit_engine("sp", e)

        for cm in reversed(sem_cms):
            cm.__exit__(None, None, None)
        for cm in reversed(self._cms):
            cm.__exit__(None, None, None)

    def dma(self, out, in_, q="sp", rkey=None, wkey=None, **kw):
        return self.rec(q, lambda e: e.dma_start(out=out, in_=in_, **kw),
                        [] if rkey else [in_], [] if wkey else [out], dma=True,
                        rkeys=[rkey] if rkey else (), wkeys=[wkey] if wkey else ())

    def mm(self, out, lhsT, rhs, start=True, stop=True, **kw):
        return self.rec("pe", lambda e: e.matmul(out, lhsT, rhs, start=start, stop=stop, **kw),
                        [lhsT, rhs], [out])

    def tr(self, out, in_, ident):
        return self.rec("pe", lambda e: e.transpose(out, in_, ident), [in_, ident], [out])

    def act(self, out, in_, func, bias=None, scale=None, accum_out=None, eng="act"):
        kw = {}
        rd = [in_]
        if bias is not None:
            kw["bias"] = bias
            if not isinstance(bias, (int, float)):
                rd.append(bias)
        if scale is not None:
            kw["scale"] = scale
            if not isinstance(scale, (int, float)):
                rd.append(scale)
        wr = [out]
        if accum_out is not None:
            kw["accum_out"] = accum_out
            wr.append(accum_out)
        return self.rec("act", lambda e: e.activation(out, in_, func, **kw), rd, wr)

    def tt(self, out, in0, in1, op, eng="dve"):
        return self.rec(eng, lambda e: e.tensor_tensor(out, in0, in1, op), [in0, in1], [out])

    def ts(self, out, in0, s1, s2, op0, op1=None, eng="dve", accum_out=None):
        rd = [in0]
        for s in (s1, s2):
            if s is not None and not isinstance(s, (int, float)):
                rd.append(s)
        kw = {}
        wr = [out]
        if accum_out is not None:
            kw["accum_out"] = accum_out
            wr.append(accum_out)
        if op1 is None:
            return self.rec(eng, lambda e: e.tensor_scalar(out, in0, s1, s2, op0, **kw), rd, wr)
        return self.rec(eng, lambda e: e.tensor_scalar(out, in0, s1, s2, op0, op1, **kw), rd, wr)

    def stt(self, out, in0, scalar, in1, op0, op1, accum_out=None):
        rd = [in0, in1]
        if not isinstance(scalar, (int, float)):
            rd.append(scalar)
        kw = {}
        wr = [out]
        if accum_out is not None:
            kw["accum_out"] = accum_out
            wr.append(accum_out)
        return self.rec("dve", lambda e: e.scalar_tensor_tensor(out, in0, scalar, in1, op0, op1, **kw), rd, wr)

    def copy(self, out, in_, eng="dve"):
        if eng == "act":
            return self.rec("act", lambda e: e.copy(out, in_), [in_], [out])
        return self.rec(eng, lambda e: e.tensor_copy(out, in_), [in_], [out])

    def memset(self, ap, val, eng="pool"):
        return self.rec(eng, lambda e: e.memset(ap, val), [], [ap])

    def recip(self, out, in_):
        return self.rec("dve", lambda e: e.reciprocal(out, in_), [in_], [out])
D = 1024
T = 512
SLOT = 5120
NP = 24
NRING = 4
NGEN = 30
GW = 640
EPS = 1e-6
MLA_SCALE = 96 ** -0.5
POOL_W = (2, 4, 8, 16)
FFN_ORDER = [0, 1, 2, 3, 4, 5, 6, 7] + [x for i in range(7) for x in (9 + 2 * i, 8 + 2 * i)] + [22]
NSEQ = 16
NTS = 8

CF_PER_LAYER = 57 + 140
O_GMIX, O_GFFN, O_PSC, O_CW, O_CB, O_DSK, O_SSDG, O_QG, O_MLAG = 0, 8, 16, 18, 38, 43, 46, 49, 51
O_KVG, O_DTB, O_ALOG = 57, 185, 191


def cf_globals(depth):
    base = depth * CF_PER_LAYER
    o = {}
    for name, n in (("FING", 8), ("INVW", 2), ("U", 128), ("LS", 128), ("UB", 128), ("LSB", 128),
                    ("ONES", 128), ("INVC", 30), ("SEQM", 16), ("PARS", 2), ("PAIRM", 8), ("LASTM", 128)):
        o[name] = base
        base += n
    o["NCF"] = base
    return o


CB_ID, CB_ONES, CB_U, CB_PMASK, CB_NMASK, CB_DUP0, CB_DUP1, CB_PARL, NCB = 0, 128, 256, 384, 480, 1248, 1376, 1504, 1632
W_UQ, W_UK, W_UV, W_PW = 0, 1536, 1920, 2304


class TilePool:
    def __init__(self, P, n, prefix, shape, dt, psum=False):
        alloc = P.ps if psum else P.sb
        self.tiles = [alloc("%s%d" % (prefix, i), shape, dt) for i in range(n)]
        self.free_list = list(self.tiles)

    def get(self):
        assert self.free_list, "tile pool exhausted"
        return self.free_list.pop(0)

    def put(self, *ts):
        for t in ts:
            assert all(t is not f for f in self.free_list)
            self.free_list.append(t)


def gf(t, n, p0=0, p1=128):
    return t[p0:p1, 0:n]


def gb(t, n, p0=0, p1=128):
    return t[p0:p1, 0:(n + 1) // 2].bitcast(BF16)[:, 0:n]


class StopBuild(Exception):
    pass


def build_program(S, depth, n_pages, n_phys, with_sample=True):
    import os
    KSTAGE = float(os.environ.get("KSTAGE", "99"))

    def stage(k):
        if KSTAGE <= k:
            raise StopBuild()
    NB = S // T
    NTOK = S + 128
    NTILE = S // 128 + 1
    CG = cf_globals(depth)
    NCF = CG["NCF"]
    nc = bass.Bass("TRN2", target_bir_lowering=False)

    def din(name, shape, dt=F32):
        return nc.dram_tensor(name, list(shape), dt, kind="ExternalInput")

    def dout(name, shape, dt=F32):
        return nc.dram_tensor(name, list(shape), dt, kind="ExternalOutput")

    xT_in = din("xT_in", [D, NTOK])
    wimg = din("wimg", [depth * NP * 128, SLOT])
    cf_d = din("cf", [128, NCF])
    cb_d = din("cb", [128, NCB], BF16)
    ropeq = din("ropeq", [192, NTOK])
    ropek = din("ropek", [128, 2 * NTILE * 16])
    yT = dout("yT", [D, NTOK])
    o_pkv = dout("o_pkv", [depth, S, 128])
    o_pkr = dout("o_pkr", [depth, S, 32])
    o_ppool = dout("o_ppool", [depth, 256, 15])
    o_pconv = dout("o_pconv", [depth, 640, 3])
    o_pssm = dout("o_pssm", [depth, 6, 64, 64])
    if with_sample:
        assert n_pages == 64
        ckvc = [din("ckvc%d" % l, [n_phys, 16384]) for l in range(depth)]
        ckrc = [din("ckrc%d" % l, [n_phys, 4096]) for l in range(depth)]
        pt_d = din("pt", [128, 8], I32)
        spool = din("spool", [depth, 256, 16, 15])
        sconv = din("sconv", [depth, 640, 16, 3])
        sssm = din("sssm", [depth, 8, 2, 6, 64, 64])
        o_skv = dout("o_skv", [depth, 128, 128])
        o_skr = dout("o_skr", [depth, 128, 32])
        o_spool = dout("o_spool", [depth, 256, 16, 15])
        o_sconv = dout("o_sconv", [depth, 640, 16, 3])
        o_sssm = dout("o_sssm", [depth, 8, 2, 6, 64, 64])
    XT = nc.dram_tensor("XT", [D, NTOK], F32, kind="Internal")
    wb = nc.dram_tensor("wb", [depth * NP * 128, SLOT], BF16, kind="Internal")

    P = Prog(nc)
    outs = []
    cf = P.sb("cf_sb", [128, NCF], F32)
    cb = P.sb("cb_sb", [128, NCB], BF16)
    ring = [P.sb("ring%d" % i, [128, SLOT], BF16) for i in range(NRING)]
    wsm = P.sb("wsm", [128, 2560], BF16)
    xT = P.sb("xT", [128, 8 * T], F32)
    hT = P.sb("hT", [128, 8 * T], BF16)
    ckvT = P.sb("ckvT", [128, S + 128], BF16)
    ckv_tok = P.sb("ckv_tok", [128, (S // 128) * 128], BF16)
    kpeT3 = P.sb("kpeT3", [96, S + 128], BF16)
    uhist = P.sb("uhist", [128, 2 * 15], F32)
    xhist = P.sb("xhist", [128, 5 * 3], F32)
    xdt_pad = [P.sb("xdtp%d" % i, [128, 6 * 128], BF16) for i in range(2)]
    hTp = P.sb("hTp", [128, 6 * 128], BF16)
    hTf = P.sb("hTf", [128, 6 * 64], F32)
    dt_tok = P.sb("dt_tok", [128, 4 * 6], F32)
    a_tok = P.sb("a_tok", [128, 4 * 6], F32)
    A_row = P.sb("A_row", [128, 6], F32)
    if with_sample:
        xg = [P.sb("xg%d" % i, [128, 2048], BF16) for i in range(4)]
        pt_sb = P.sb("pt_sb", [128, 8], I32)
    GEN = TilePool(P, NGEN, "g", [128, GW], F32)
    BANK = TilePool(P, 8, "pb", [128, 512], F32, psum=True)

    ident = cb[:, CB_ID:CB_ID + 128]
    ones_b = cb[:, CB_ONES:CB_ONES + 128]
    U_b = cb[:, CB_U:CB_U + 128]
    U_f = cf[:, CG["U"]:CG["U"] + 128]
    LS_f = cf[:, CG["LS"]:CG["LS"] + 128]
    ones_f = cf[:, CG["ONES"]:CG["ONES"] + 128]
    UB_f = cf[:, CG["UB"]:CG["UB"] + 128]
    LSB_f = cf[:, CG["LSB"]:CG["LSB"] + 128]
    parS_f = cf[:, CG["PARS"]:CG["PARS"] + 2]
    lastm_f = cf[:, CG["LASTM"]:CG["LASTM"] + 128]
    pairm_f = cf[:, CG["PAIRM"]:CG["PAIRM"] + 8]
    pmask_b = cb[:, CB_PMASK:CB_PMASK + 96]
    nmask_b = cb[:, CB_NMASK:CB_NMASK + 768]
    dup_b = [cb[:, CB_DUP0:CB_DUP0 + 128], cb[:, CB_DUP1:CB_DUP1 + 128]]
    parL_b = cb[:, CB_PARL:CB_PARL + 128]

    def cfc(l, off, n=1, p0=0, p1=128):
        o = l * CF_PER_LAYER + off
        return cf[p0:p1, o:o + n]

    P.dma(cf[:], cf_d.ap())
    P.dma(cb[:], cb_d.ap())
    if with_sample:
        P.dma(pt_sb[:], pt_d.ap())
    for l in range(depth):
        for pc in range(NP):
            r0 = (l * NP + pc) * 128
            P.dma(wb.ap()[r0:r0 + 128, :], wimg.ap()[r0:r0 + 128, :], q="pool", wkey="wb_%d_%d" % (l, pc))
    for t_ in xdt_pad:
        P.memset(t_[:], 0.0)

    sched = []
    for l in range(depth):
        for b in range(NB + (1 if with_sample else 0)):
            for pc in FFN_ORDER:
                sched.append((l, pc))
    rstate = {"issued": 0, "used": 0}

    def ring_issue():
        i = rstate["issued"]
        if i >= len(sched):
            return
        l, pc = sched[i]
        r0 = (l * NP + pc) * 128
        P.dma(ring[i % NRING][:], wb.ap()[r0:r0 + 128, :], rkey="wb_%d_%d" % (l, pc))
        rstate["issued"] += 1

    def ring_next(expect):
        i = rstate["used"]
        assert sched[i][1] == expect, (sched[i], expect)
        while rstate["issued"] < min(len(sched), i + NRING):
            ring_issue()
        if rstate["issued"] <= i:
            ring_issue()
        rstate["used"] += 1
        return ring[i % NRING]

    def rmsnorm_fm(srcs, gcols, dim, dsts, Pn, N):
        ps = BANK.get()
        n = len(srcs)
        for i, s in enumerate(srcs):
            sq = GEN.get()
            P.act(gb(sq, N, 0, Pn), s, AF.Square)
            P.mm(ps[0:Pn, 0:N], ones_b[0:Pn, 0:Pn], gb(sq, N, 0, Pn), start=(i == 0), stop=(i == n - 1))
            GEN.put(sq)
        rs = GEN.get()
        P.act(gf(rs, N, 0, Pn), ps[0:Pn, 0:N], AF.Sqrt, scale=1.0 / dim, bias=EPS)
        BANK.put(ps)
        P.recip(gf(rs, N, 0, Pn), gf(rs, N, 0, Pn))
        for i, s in enumerate(srcs):
            P.stt(dsts[i], s, gcols[i], gf(rs, N, 0, Pn), ALU.mult, ALU.mult)
        GEN.put(rs)

    def hTk(k, n0, n1):
        return hT[:, k * T + n0:k * T + n1]

    def xTk(k, n0, n1):
        return xT[:, k * T + n0:k * T + n1]

    def proj_fm(ps_out, w, wstride, c0, M, N):
        for k in range(8):
            P.mm(ps_out[0:M, 0:N], w[:, k * wstride + c0:k * wstride + c0 + M], hTk(k, 0, N),
                 start=(k == 0), stop=(k == 7))

    def ssd_chunk(l, ci, N0, xs_bf, BT, CT, yv, masks, sample=None):
        Uf, LSf = masks
        cc = slice(N0, N0 + 128)
        xp = xdt_pad[ci % 2]
        dtc = dt_tok[:, ci * 6:(ci + 1) * 6]
        ac = a_tok[:, ci * 6:(ci + 1) * 6]
        ptr = BANK.get()
        ptb = ptr[:, 0:256].bitcast(BF16)
        for j in range(3):
            P.tr(ptb[:, j * 128:(j + 1) * 128], gb(xs_bf[j], T)[:, cc], ident)
        P.tr(ptb[:, 384:512], gb(BT, T)[:, cc], ident)
        btok = GEN.get()
        P.copy(gb(btok, 128), ptb[:, 384:512], eng="act")
        xs3 = ptb[:, 0:384].rearrange("p (j e d) -> p j e d", j=3, e=2)
        xp4 = xp[:, :].rearrange("p (j e c) -> p j e c", j=3, e=2)
        dt3 = dtc.rearrange("p (j e) -> p j e", e=2)
        for e in range(2):
            P.tt(xp4[:, :, e, e * 64:e * 64 + 64], xs3[:, :, e, :],
                 dt3[:, :, e:e + 1].broadcast_to([128, 3, 64]), ALU.mult)
        stage(6.1)
        pcb = BANK.get()
        for g in range(2):
            P.mm(pcb[:, g * 128:(g + 1) * 128], gb(BT, T)[:, cc], gb(CT[g], T)[:, cc],
                 start=(g == 0), stop=(g == 1), skip_group_check=True)
        stage(6.15)
        cbm = GEN.get()
        P.tt(gf(cbm, 256).rearrange("p (g c) -> p g c", g=2), pcb[:, 0:256].rearrange("p (g c) -> p g c", g=2),
             Uf[:, None, :].broadcast_to([128, 2, 128]), ALU.mult)
        BANK.put(pcb)
        stage(6.2)
        aU = [GEN.get(), GEN.get()]
        for i in range(2):
            P.tt(gf(aU[i], 384).rearrange("p (h c) -> p h c", h=3),
                 Uf[:, None, :].broadcast_to([128, 3, 128]),
                 ac[:, 3 * i:3 * i + 3].unsqueeze(2).broadcast_to([128, 3, 128]), ALU.mult)
        stage(6.25)
        Lx = [GEN.get(), GEN.get()]
        Ex = [GEN.get(), GEN.get()]
        for i in range(2):
            pseg = BANK.get()
            pacs = BANK.get()
            for hh in range(3):
                rhs = gf(aU[i], 384)[:, hh * 128:(hh + 1) * 128]
                P.mm(pseg[:, hh * 128:(hh + 1) * 128], LSf, rhs, start=(hh == 0), stop=(hh == 2), skip_group_check=True)
            for hh in range(3):
                rhs = gf(aU[i], 384)[:, hh * 128:(hh + 1) * 128]
                P.mm(pacs[:, hh * 128:(hh + 1) * 128], ones_f, rhs, start=(hh == 0), stop=(hh == 2), skip_group_check=True)
            P.act(gf(Lx[i], 384), pseg[:, 0:384], AF.Exp)
            P.act(gf(Ex[i], 384), pacs[:, 0:384], AF.Exp)
            BANK.put(pseg, pacs)
        GEN.put(*aU)
        stage(6.3)
        MT = [GEN.get(), GEN.get()]
        CsT = GEN.get()
        for g in range(2):
            P.tt(gb(MT[g], 384).rearrange("p (h c) -> p h c", h=3),
                 gf(Lx[g], 384).rearrange("p (h c) -> p h c", h=3),
                 gf(cbm, 256)[:, g * 128:(g + 1) * 128].unsqueeze(1).broadcast_to([128, 3, 128]), ALU.mult, eng="pool")
            r0, r1 = 64 * g, 64 * g + 64
            P.tt(gb(CsT, 384, r0, r1).rearrange("p (h c) -> p h c", h=3),
                 gf(Ex[g], 384, r0, r1).rearrange("p (h c) -> p h c", h=3),
                 gb(CT[g], T, r0, r1)[:, cc].unsqueeze(1).broadcast_to([64, 3, 128]), ALU.mult, eng="pool")
        GEN.put(cbm)
        wdec = GEN.get()
        for g in range(2):
            if sample is None:
                P.tt(gf(wdec, 6)[:, 3 * g:3 * g + 3], dtc[:, 3 * g:3 * g + 3],
                     gf(Lx[g], 384).rearrange("p (h c) -> p h c", h=3)[:, :, 127], ALU.mult)
            else:
                tmpm = GEN.get()
                tv_ = gf(tmpm, 384).rearrange("p (h c) -> p h c", h=3)
                P.tt(tv_, gf(Lx[g], 384).rearrange("p (h c) -> p h c", h=3),
                     lastm_f.unsqueeze(1).broadcast_to([128, 3, 128]), ALU.mult, eng="pool")
                wl = gf(tmpm, 400)[:, 392:395]
                P.rec("dve", lambda e, o=wl, i=tv_: e.reduce_sum(o, i, mybir.AxisListType.X), [tv_], [wl])
                P.tt(gf(wdec, 6)[:, 3 * g:3 * g + 3], dtc[:, 3 * g:3 * g + 3], wl, ALU.mult)
                GEN.put(tmpm)
        xdtw = GEN.get()
        P.tt(gb(xdtw, 384).rearrange("p (h d) -> p h d", h=6), ptb[:, 0:384].rearrange("p (h d) -> p h d", h=6),
             gf(wdec, 6).unsqueeze(2).broadcast_to([128, 6, 64]), ALU.mult)
        GEN.put(wdec)
        BANK.put(ptr)
        stage(6.4)
        py = BANK.get()
        first = True
        for j in range(3):
            oc = slice(j * 128, (j + 1) * 128)
            for e in range(2):
                h = 2 * j + e
                g, hh = h // 3, h % 3
                P.mm(py[:, oc], xp[:, h * 128:(h + 1) * 128], gb(MT[g], 384)[:, hh * 128:(hh + 1) * 128],
                     start=first, stop=False, skip_group_check=True)
                first = False
            for e in range(2):
                h = 2 * j + e
                g, hh = h // 3, h % 3
                r0, r1 = 64 * g, 64 * g + 64
                if sample is None:
                    P.mm(py[:, oc], hTp[:, h * 128:(h + 1) * 128], gb(CsT, 384)[:, hh * 128:(hh + 1) * 128],
                         start=False, stop=(j == 2 and e == 1), skip_group_check=True)
        GEN.put(*MT)
        if sample is not None:
            sample["yoff"](py, CsT)
        for j in range(3):
            P.stt(gf(yv[j], T)[:, cc], gb(xs_bf[j], T)[:, cc], cfc(l, O_DSK + j), py[:, j * 128:(j + 1) * 128],
                  ALU.mult, ALU.add)
        BANK.put(py)
        GEN.put(CsT)
        stage(6.5)
        if sample is None:
            pst = BANK.get()
            for h in range(6):
                P.mm(pst[:, h * 64:(h + 1) * 64], gb(btok, 128), gb(xdtw, 384)[:, h * 64:(h + 1) * 64],
                     start=(h == 0), stop=(h == 5), skip_group_check=True)
            for h in range(6):
                g, hh = h // 3, h % 3
                r0, r1 = 64 * g, 64 * g + 64
                cd = gf(Ex[g], 384, r0, r1)[:, hh * 128 + 127:hh * 128 + 128]
                P.stt(hTf[r0:r1, h * 64:(h + 1) * 64], hTf[r0:r1, h * 64:(h + 1) * 64], cd,
                      pst[r0:r1, h * 64:(h + 1) * 64], ALU.mult, ALU.add)
                e = h % 2
                P.copy(hTp[r0:r1, h * 128 + e * 64:h * 128 + e * 64 + 64], hTf[r0:r1, h * 64:(h + 1) * 64], eng="pool")
            BANK.put(pst)
        else:
            sample["states"](btok, xdtw, Ex)
        GEN.put(btok, xdtw, *Lx, *Ex)

    def prompt_block(l, b):
        t0 = b * T
        last_layer = (l == depth - 1)
        src = xT_in if l == 0 else XT
        P.dma(xT[:].rearrange("p (k t) -> p k t", k=8),
              src.ap().rearrange("(k p) t -> p k t", p=128)[:, :, t0:t0 + T], rkey="XT_%d" % t0)
        if b == 0:
            P.dma(wsm[:], wb.ap()[(l * NP + 23) * 128:(l * NP + 24) * 128, 0:2560], rkey="wb_%d_23" % l)
            P.act(A_row[:], cfc(l, O_ALOG, 6), AF.Exp)
            P.ts(A_row[:], A_row[:], -1.0, None, ALU.mult)
            P.memset(hTp[:], 0.0)
            P.memset(hTf[:], 0.0)
            P.memset(uhist[:], 0.0)
            P.memset(xhist[:], 0.0)
        stage(0)
        rmsnorm_fm([xTk(k, 0, T) for k in range(8)], [cfc(l, O_GMIX + k) for k in range(8)], D,
                   [hTk(k, 0, T) for k in range(8)], 128, T)
        stage(1)
        w1 = ring_next(0)
        uet = [GEN.get(), GEN.get()]
        for c in range(2):
            ps = BANK.get()
            proj_fm(ps, w1, 512, c * 128, 128, T)
            P.copy(gf(uet[c], 15 + T)[:, 15:15 + T], ps[:, 0:T], eng="act")
            P.copy(gf(uet[c], 15), uhist[:, c * 15:(c + 1) * 15], eng="pool")
            BANK.put(ps)
        psq = [BANK.get(), BANK.get()]
        for c in range(2):
            proj_fm(psq[c], w1, 512, 256 + c * 128, 128, T)
        cqn = [GEN.get(), GEN.get()]
        rmsnorm_fm([psq[c][:, 0:T] for c in range(2)], [cfc(l, O_QG + c) for c in range(2)], 256,
                   [gb(cqn[c], T) for c in range(2)], 128, T)
        BANK.put(*psq)
        E_ = 15 + T
        m_bf = [GEN.get(), GEN.get()]
        for c in range(2):
            ue = gf(uet[c], 15 + T)
            sel = GEN.get()
            t2 = GEN.get()
            se, t2f = gf(sel, E_), gf(t2, E_)
            if c == 0:
                P.tt(se[0:64, 1:E_], ue[0:64, 1:E_], ue[0:64, 0:E_ - 1], ALU.add, eng="pool")
                P.tt(t2f[64:128, 1:E_], ue[64:128, 1:E_], ue[64:128, 0:E_ - 1], ALU.add, eng="pool")
                P.tt(se[64:128, 3:E_], t2f[64:128, 3:E_], t2f[64:128, 1:E_ - 2], ALU.add, eng="pool")
            else:
                t4 = GEN.get()
                t4f = gf(t4, E_)
                P.tt(t2f[:, 1:E_], ue[:, 1:E_], ue[:, 0:E_ - 1], ALU.add, eng="pool")
                P.tt(t4f[:, 3:E_], t2f[:, 3:E_], t2f[:, 1:E_ - 2], ALU.add, eng="pool")
                P.tt(se[0:64, 7:E_], t4f[0:64, 7:E_], t4f[0:64, 3:E_ - 4], ALU.add, eng="pool")
                P.tt(t2f[64:128, 7:E_], t4f[64:128, 7:E_], t4f[64:128, 3:E_ - 4], ALU.add, eng="pool")
                P.tt(se[64:128, 15:E_], t2f[64:128, 15:E_], t2f[64:128, 7:E_ - 8], ALU.add, eng="pool")
                GEN.put(t4)
            P.stt(gb(m_bf[c], T), se[:, 15:E_], cf[:, CG["INVW"] + c:CG["INVW"] + c + 1], ue[:, 15:E_],
                  ALU.mult, ALU.subtract)
            if b == 0:
                tq = GEN.get()
                P.tt(gf(tq, 15), se[:, 15:30], cf[:, CG["INVC"] + c * 15:CG["INVC"] + c * 15 + 15], ALU.mult)
                P.tt(gb(m_bf[c], T)[:, 0:15], gf(tq, 15), ue[:, 15:30], ALU.subtract)
                GEN.put(tq)
            GEN.put(sel, t2)
            if b == NB - 1:
                outs.append(P.dma(o_ppool.ap()[l, c * 128:(c + 1) * 128, :], ue[:, T:T + 15]))
            P.copy(uhist[:, c * 15:(c + 1) * 15], ue[:, T:T + 15], eng="pool")
            GEN.put(uet[c])
        stage(2)
        w2 = ring_next(1)
        zs = [GEN.get() for _ in range(3)]
        for j in range(3):
            ps = BANK.get()
            proj_fm(ps, w2, 550, j * 128, 128, T)
            P.act(gb(zs[j], T), ps[:, 0:T], AF.Silu)
            BANK.put(ps)
        ptk = [BANK.get(), BANK.get()]
        for i in range(4):
            pso = ptk[i // 2][:, (i % 2) * 166:(i % 2) * 166 + 166]
            for k in range(8):
                P.mm(pso, hTk(k, i * 128, (i + 1) * 128), w2[:, k * 550 + 384:k * 550 + 550],
                     start=(k == 0 and i % 2 == 0), stop=(k == 7), skip_group_check=True)
        tokmajor_post(l, t0, S // 128, b * 4, 4, ptk, o_pkv.ap()[l, t0:t0 + T, :], o_pkr.ap()[l, t0:t0 + T, :])
        BANK.put(*ptk)
        stage(3)
        w3 = ring_next(2)
        xs_bf = [GEN.get() for _ in range(3)]
        BT = GEN.get()
        CT = [GEN.get(), GEN.get()]
        P.memset(gb(CT[0], T, 64, 128), 0.0)
        P.memset(gb(CT[1], T, 0, 64), 0.0)
        for j in range(5):
            ps = BANK.get()
            proj_fm(ps, w3, 640, j * 128, 128, T)
            xet = GEN.get()
            xe = gf(xet, 3 + T)
            P.copy(xe[:, 3:3 + T], ps[:, 0:T], eng="act")
            P.copy(xe[:, 0:3], xhist[:, j * 3:(j + 1) * 3], eng="pool")
            BANK.put(ps)
            acc = GEN.get()
            af_ = gf(acc, T)
            P.act(af_, xe[:, 0:T], AF.Identity, scale=cfc(l, O_CW + j * 4), bias=cfc(l, O_CB + j))
            for k in range(1, 4):
                P.stt(af_, xe[:, k:k + T], cfc(l, O_CW + j * 4 + k), af_, ALU.mult, ALU.add)
            if j < 4:
                dst = xs_bf[j] if j < 3 else BT
                P.act(gb(dst, T), af_, AF.Silu)
            else:
                P.act(gb(CT[0], T, 0, 64), af_[0:64, :], AF.Silu)
                P.act(gb(CT[1], T, 64, 128), af_[64:128, :], AF.Silu)
            GEN.put(acc)
            if b == NB - 1:
                outs.append(P.dma(o_pconv.ap()[l, j * 128:(j + 1) * 128, :], xe[:, T:T + 3]))
            P.copy(xhist[:, j * 3:(j + 1) * 3], xe[:, T:T + 3], eng="pool")
            GEN.put(xet)
        stage(4)
        qlat, qpe = q_path(l, t0, T, cqn)
        stage(5)
        GEN.put(*cqn)
        mla_tiles, mlan = attention_prompt(l, b, qlat, qpe)
        stage(6)
        yv = [GEN.get() for _ in range(3)]
        for ci in range(4):
            ssd_chunk(l, ci, ci * 128, xs_bf, BT, CT, yv, (U_f, LS_f))
        GEN.put(BT, *CT, *xs_bf)
        if b == NB - 1:
            for h in range(6):
                g = h // 3
                outs.append(P.dma(o_pssm.ap()[l, h, :, :], hTf[64 * g:64 * g + 64, h * 64:(h + 1) * 64]))
        stage(7)
        mixT = [GEN.get() for _ in range(5)]
        for j in range(3):
            P.tt(gf(yv[j], T), gf(yv[j], T), gb(zs[j], T), ALU.mult, eng="pool")
        GEN.put(*zs)
        rmsnorm_fm([gf(yv[j], T) for j in range(3)], [cfc(l, O_SSDG + j) for j in range(3)], 384,
                   [gb(mixT[2 + j], T) for j in range(3)], 128, T)
        GEN.put(*yv)
        for c in range(2):
            ps = BANK.get()
            P.mm(ps[:, 0:T], wsm[:, W_PW + c * 128:W_PW + (c + 1) * 128], gb(m_bf[c], T), start=True, stop=True)
            P.act(gb(mixT[c], T), ps[:, 0:T], AF.Copy, scale=cfc(l, O_PSC + c))
            BANK.put(ps)
        GEN.put(*m_bf)
        dense_tail(l, T, mixT, mlan, t0)
        GEN.put(*mixT, *mla_tiles)

    def tokmajor_post(l, t0, tile_cap, tg0, ntile, ptk, dkv, dkr, per_bank=2, keep_ckb=False):
        nb_ = (ntile + per_bank - 1) // per_bank
        ss = GEN.get()
        junk = GEN.get()
        for i in range(ntile):
            v = ptk[i // per_bank][:, (i % per_bank) * 166:(i % per_bank) * 166 + 128]
            P.act(gf(junk, 128), v, AF.Square, accum_out=gf(ss, ntile)[:, i:i + 1])
        GEN.put(junk)
        P.act(gf(ss, ntile), gf(ss, ntile), AF.Sqrt, scale=1.0 / 128, bias=EPS)
        P.recip(gf(ss, ntile), gf(ss, ntile))
        ckn = GEN.get()
        for i in range(ntile):
            v = ptk[i // per_bank][:, (i % per_bank) * 166:(i % per_bank) * 166 + 128]
            P.stt(gf(ckn, ntile * 128)[:, i * 128:(i + 1) * 128], v, gf(ss, ntile)[:, i:i + 1],
                  cfc(l, O_KVG, 128), ALU.mult, ALU.mult)
        GEN.put(ss)
        outs.append(P.dma(dkv.rearrange("(i p) c -> p i c", p=128),
                          gf(ckn, ntile * 128).rearrange("p (i c) -> p i c", i=ntile)))
        ckb = GEN.get()
        P.copy(gb(ckb, ntile * 128), gf(ckn, ntile * 128), eng="act")
        GEN.put(ckn)
        if tg0 + ntile <= tile_cap:
            P.copy(ckv_tok[:, tg0 * 128:(tg0 + ntile) * 128], gb(ckb, ntile * 128), eng="pool")
        pT_ = BANK.get()
        pTb = pT_[:, 0:256].bitcast(BF16)
        for i in range(ntile):
            P.tr(pTb[:, i * 128:(i + 1) * 128], gb(ckb, ntile * 128)[:, i * 128:(i + 1) * 128], ident)
        P.copy(ckvT[:, t0:t0 + ntile * 128], pTb[:, 0:ntile * 128], eng="act")
        BANK.put(pT_)
        kpr = GEN.get()
        kv = gf(kpr, ntile * 32).rearrange("p (i c) -> p i c", i=ntile)
        tm = [GEN.get() for _ in range(4)]
        rkt = GEN.get()
        P.dma(gf(rkt, ntile * 16), ropek.ap()[:, tg0 * 16:(tg0 + ntile) * 16])
        P.dma(gf(rkt, 2 * ntile * 16)[:, ntile * 16:2 * ntile * 16],
              ropek.ap()[:, (NTILE + tg0) * 16:(NTILE + tg0 + ntile) * 16])
        cosv = gf(rkt, ntile * 16).rearrange("p (i c) -> p i c", c=16)
        sinv = gf(rkt, 2 * ntile * 16)[:, ntile * 16:2 * ntile * 16].rearrange("p (i c) -> p i c", c=16)
        for bi in range(nb_):
            n_in = min(per_bank, ntile - bi * per_bank)
            bv = ptk[bi][:, 0:per_bank * 166].rearrange("p (i c) -> p i c", i=per_bank)
            x1 = bv[:, 0:n_in, 128:144]
            x2 = bv[:, 0:n_in, 144:160]
            tg = bi * per_bank
            cs_ = cosv[:, tg:tg + n_in, :]
            sn_ = sinv[:, tg:tg + n_in, :]
            tv = [gf(t_, n_in * 16).rearrange("p (i c) -> p i c", i=n_in) for t_ in tm]
            P.tt(tv[0], x1, cs_, ALU.mult)
            P.tt(tv[1], x2, sn_, ALU.mult)
            P.tt(tv[2], x1, sn_, ALU.mult)
            P.tt(tv[3], x2, cs_, ALU.mult)
            i0 = bi * per_bank
            P.tt(kv[:, i0:i0 + n_in, 0:16], tv[0], tv[1], ALU.subtract, eng="pool")
            P.tt(kv[:, i0:i0 + n_in, 16:32], tv[2], tv[3], ALU.add, eng="pool")
        for t_ in tm:
            GEN.put(t_)
        GEN.put(rkt)
        outs.append(P.dma(dkr.rearrange("(i p) c -> p i c", p=128), kv))
        kp3 = GEN.get()
        P.copy(gb(kp3, ntile * 96).rearrange("p (i r c) -> p i r c", i=ntile, r=3),
               kv.unsqueeze(2).broadcast_to([128, ntile, 3, 32]), eng="act")
        GEN.put(kpr)
        pT2 = BANK.get()
        pT2b = pT2[:, 0:256].bitcast(BF16)
        for i in range(ntile):
            P.tr(pT2b[0:96, i * 128:(i + 1) * 128], gb(kp3, ntile * 96)[:, i * 96:(i + 1) * 96], ident)
        P.copy(kpeT3[:, t0:t0 + ntile * 128], pT2b[0:96, 0:ntile * 128], eng="act")
        BANK.put(pT2)
        GEN.put(kp3)
        if not keep_ckb:
            GEN.put(ckb)
        tx = GEN.get()
        for bi in range(nb_):
            n_in = min(per_bank, ntile - bi * per_bank)
            bv = ptk[bi][:, 0:per_bank * 166].rearrange("p (i c) -> p i c", i=per_bank)
            i0 = bi * per_bank
            P.tt(gf(tx, ntile * 6).rearrange("p (i c) -> p i c", i=ntile)[:, i0:i0 + n_in, :], bv[:, 0:n_in, 160:166],
                 cfc(l, O_DTB, 6).unsqueeze(1).broadcast_to([128, n_in, 6]), ALU.add)
        P.act(gf(tx, ntile * 6), gf(tx, ntile * 6), AF.Exp)
        P.act(dt_tok[:, 0:ntile * 6], gf(tx, ntile * 6), AF.Ln, bias=1.0)
        GEN.put(tx)
        P.tt(a_tok[:, 0:ntile * 6].rearrange("p (i c) -> p i c", i=ntile),
             dt_tok[:, 0:ntile * 6].rearrange("p (i c) -> p i c", i=ntile),
             A_row[:, :].unsqueeze(1).broadcast_to([128, ntile, 6]), ALU.mult)
        return ckb if keep_ckb else None

    def q_path(l, t0, N, cqn):
        qn = [GEN.get() for _ in range(3)]
        for j in range(3):
            ps = BANK.get()
            for k in range(2):
                P.mm(ps[:, 0:N], wsm[:, W_UQ + k * 768 + j * 128:W_UQ + k * 768 + (j + 1) * 128], gb(cqn[k], N),
                     start=(k == 0), stop=(k == 1))
            P.copy(gb(qn[j], N), ps[:, 0:N], eng="act")
            BANK.put(ps)
        cosT = GEN.get()
        sinT = GEN.get()
        P.dma(gf(cosT, N, 0, 96), ropeq.ap()[0:96, t0:t0 + N])
        P.dma(gf(sinT, N, 0, 96), ropeq.ap()[96:192, t0:t0 + N])
        qpe = [GEN.get(), GEN.get()]
        for x in range(2):
            pp = BANK.get()
            psw = BANK.get()
            for k in range(2):
                c0 = W_UQ + k * 768 + 384 + x * 96
                P.mm(pp[0:96, 0:N], wsm[:, c0:c0 + 96], gb(cqn[k], N), start=(k == 0), stop=(k == 1))
            for k in range(2):
                c0 = W_UQ + k * 768 + 576 + x * 96
                P.mm(psw[0:96, 0:N], wsm[:, c0:c0 + 96], gb(cqn[k], N), start=(k == 0), stop=(k == 1))
            ta = GEN.get()
            tb_ = GEN.get()
            P.tt(gf(ta, N, 0, 96), pp[0:96, 0:N], gf(cosT, N, 0, 96), ALU.mult)
            P.tt(gf(tb_, N, 0, 96), psw[0:96, 0:N], gf(sinT, N, 0, 96), ALU.mult)
            P.tt(gb(qpe[x], N, 0, 96), gf(ta, N, 0, 96), gf(tb_, N, 0, 96), ALU.add, eng="pool")
            BANK.put(pp, psw)
            GEN.put(ta, tb_)
        GEN.put(cosT, sinT)
        qlat = [GEN.get() for _ in range(6)]
        for h in range(6):
            r0 = 64 * (h % 2)
            ps = BANK.get()
            P.mm(ps[:, 0:N], wsm[r0:r0 + 64, W_UK + (h // 2) * 128:W_UK + (h // 2 + 1) * 128],
                 gb(qn[h // 2], N, r0, r0 + 64), start=True, stop=True)
            P.copy(gb(qlat[h], N), ps[:, 0:N], eng="act")
            BANK.put(ps)
        GEN.put(*qn)
        return qlat, qpe

    def attention_prompt(l, b, qlat, qpe):
        nkt = 4 * b + 4
        ot = [GEN.get() for _ in range(3)]
        oTv = [gb(ot[h // 2], 2 * T, 0, 64)[:, (h % 2) * T:(h % 2 + 1) * T] for h in range(6)]
        pending = []

        def finalize(h, psO, psD):
            rden = GEN.get()
            P.recip(gf(rden, T), psD[:, 0:T])
            olat = GEN.get()
            P.tt(gb(olat, T), psO[:, 0:T], gf(rden, T), ALU.mult)
            BANK.put(psO, psD)
            GEN.put(rden)
            ps = BANK.get()
            P.mm(ps[0:64, 0:T], wsm[:, W_UV + h * 64:W_UV + (h + 1) * 64], gb(olat, T), start=True, stop=True)
            P.copy(oTv[h], ps[0:64, 0:T], eng="act")
            BANK.put(ps)
            GEN.put(olat, qlat[h])

        for h in range(6):
            psO = BANK.get()
            psD = BANK.get()
            rr = 32 * (h % 3)

            def stage_a(kt, h=h, rr=rr):
                i = kt - 4 * b
                q0 = max(0, i) * 128
                psS = BANK.get()
                P.mm(psS[:, q0:T], ckvT[:, kt * 128:(kt + 1) * 128], gb(qlat[h], T)[:, q0:T], start=True, stop=False)
                P.mm(psS[:, q0:T], kpeT3[rr:rr + 32, kt * 128:(kt + 1) * 128], gb(qpe[h // 3], T, rr, rr + 32)[:, q0:T],
                     start=False, stop=True)
                pT = GEN.get()
                P.act(gb(pT, T)[:, q0:T], psS[:, q0:T], AF.Exp, scale=MLA_SCALE)
                BANK.put(psS)
                if i >= 0:
                    P.tt(gb(pT, T)[:, q0:q0 + 128], gb(pT, T)[:, q0:q0 + 128], U_b, ALU.mult, eng="pool")
                return (kt, pT, q0)

            def stage_b(kt, pT, q0, psO=psO, psD=psD):
                P.mm(psO[:, q0:T], ckv_tok[:, kt * 128:(kt + 1) * 128], gb(pT, T)[:, q0:T],
                     start=(kt == 0), stop=(kt == nkt - 1), skip_group_check=True)
                P.mm(psD[:, q0:T], ones_b, gb(pT, T)[:, q0:T],
                     start=(kt == 0), stop=(kt == nkt - 1), skip_group_check=True)
                GEN.put(pT)

            fifo = []
            for kt in range(nkt):
                fifo.append(stage_a(kt))
                if kt == 1 and pending:
                    pending.pop(0)()
                la = 1 if pending else 2
                while len(fifo) > la:
                    stage_b(*fifo.pop(0))
            while fifo:
                stage_b(*fifo.pop(0))
            pending.append(lambda h=h, psO=psO, psD=psD: finalize(h, psO, psD))
        while pending:
            pending.pop(0)()
        GEN.put(*qpe)
        rmsnorm_fm(oTv, [cfc(l, O_MLAG + h, 1, 0, 64) for h in range(6)], 384, oTv, 64, T)
        return ot, oTv

    def dense_tail(l, N, mixT, mlan, t0):
        last_layer = (l == depth - 1)
        for jm in range(4):
            w = ring_next(3 + jm)
            for e in range(2):
                mc = 2 * jm + e
                ps = BANK.get()
                for kc in range(5):
                    P.mm(ps[:, 0:N], w[:, kc * 256 + e * 128:kc * 256 + (e + 1) * 128], gb(mixT[kc], N),
                         start=(kc == 0), stop=False)
                for h in range(6):
                    kc = 5 + h
                    P.mm(ps[:, 0:N], w[0:64, kc * 256 + e * 128:kc * 256 + (e + 1) * 128], mlan[h][:, 0:N],
                         start=False, stop=(h == 5))
                P.tt(xTk(mc, 0, N), ps[:, 0:N], xTk(mc, 0, N), ALU.add)
                BANK.put(ps)
        rmsnorm_fm([xTk(k, 0, N) for k in range(8)], [cfc(l, O_GFFN + k) for k in range(8)], D,
                   [hTk(k, 0, N) for k in range(8)], 128, N)
        def ffn_up(i):
            wu = ring_next(7 + 2 * i)
            a = [GEN.get() for _ in range(4)]
            for fc in range(4):
                ps = BANK.get()
                proj_fm(ps, wu, 512, fc * 128, 128, N)
                r = GEN.get()
                P.act(gb(r, N), ps[:, 0:N], AF.Relu)
                BANK.put(ps)
                P.tt(gb(a[fc], N), gb(r, N), gb(r, N), ALU.mult, eng="pool")
                GEN.put(r)
            return a

        def ffn_down(i, a):
            wd = ring_next(8 + 2 * i)
            for mc in range(8):
                ps = BANK.get()
                for fc in range(4):
                    P.mm(ps[:, 0:N], wd[:, fc * 1024 + mc * 128:fc * 1024 + (mc + 1) * 128], gb(a[fc], N),
                         start=(fc == 0), stop=(fc == 3))
                P.tt(xTk(mc, 0, N), ps[:, 0:N], xTk(mc, 0, N), ALU.add)
                BANK.put(ps)
            GEN.put(*a)

        a_cur = ffn_up(0)
        for i in range(8):
            a_next = ffn_up(i + 1) if i < 7 else None
            ffn_down(i, a_cur)
            a_cur = a_next
        xv = xT[:].rearrange("p (k t) -> p k t", k=8)[:, :, 0:N]
        if not last_layer:
            P.dma(XT.ap().rearrange("(k p) t -> p k t", p=128)[:, :, t0:t0 + N], xv, wkey="XT_%d" % t0)
        else:
            yo = [GEN.get() for _ in range(8)]
            rmsnorm_fm([xTk(k, 0, N) for k in range(8)], [cf[:, CG["FING"] + k:CG["FING"] + k + 1] for k in range(8)], D,
                       [gf(yo[k], N) for k in range(8)], 128, N)
            for k in range(8):
                outs.append(P.dma(yT.ap()[k * 128:(k + 1) * 128, t0:t0 + N], gf(yo[k], N)))
            GEN.put(*yo)


    def attention_sample(l, qlat, qpe, ckb_new):
        QL = GEN.get()
        QP = GEN.get()
        for h in range(6):
            P.copy(gb(QL, 768).rearrange("p (q h t) -> p q h t", q=8, h=6)[:, :, h, :],
                   gb(qlat[h], 128).rearrange("p (q t) -> p q t", q=8), eng="pool")
        P.memset(gb(QP, 768, 0, 96), 0.0)
        for h in range(6):
            rr = 32 * (h % 3)
            P.copy(gb(QP, 768, rr, rr + 32).rearrange("p (q h t) -> p q h t", q=8, h=6)[:, :, h, :],
                   gb(qpe[h // 3], 128, rr, rr + 32).rearrange("p (q t) -> p q t", q=8), eng="pool")
        GEN.put(*qlat, *qpe)
        OL = GEN.get()
        gic = [0]
        for q in range(8):
            QLq = gb(QL, 768)[:, q * 96:(q + 1) * 96]
            QPq = gb(QP, 768, 0, 96)[:, q * 96:(q + 1) * 96]
            psO = BANK.get()
            fo = [True]

            def st_g(k, q=q):
                X = xg[gic[0] % 4]
                gic[0] += 1
                P.rec("pool", lambda e, X=X, q=q, k=k: e.indirect_dma_start(
                    out=X[:, 0:2048], out_offset=None, in_=ckvc[l].ap(),
                    in_offset=bass.IndirectOffsetOnAxis(ap=pt_sb[:, q:q + 1], axis=0), element_offset=k * 2048),
                    [pt_sb[:]], [X[:]], dma=True, rkeys=["ckvc%d" % l])
                Xr = GEN.get()
                P.rec("pool", lambda e, Xr=Xr, q=q, k=k: e.indirect_dma_start(
                    out=gb(Xr, 512), out_offset=None, in_=ckrc[l].ap(),
                    in_offset=bass.IndirectOffsetOnAxis(ap=pt_sb[:, q:q + 1], axis=0), element_offset=k * 512),
                    [pt_sb[:]], [gb(Xr, 512)], dma=True, rkeys=["ckrc%d" % l])
                return (X, Xr)

            def st_a(hc, X, Xr):
                hf = hc % 2
                Xb = X[:, hf * 1024:(hf + 1) * 1024]
                Xr3 = GEN.get()
                P.copy(gb(Xr3, 768).rearrange("p (t r c) -> p t r c", t=8, r=3),
                       gb(Xr, 512)[:, hf * 256:(hf + 1) * 256].rearrange("p (t c) -> p t c", t=8).unsqueeze(2).broadcast_to([128, 8, 3, 32]),
                       eng="act")
                if hf == 1:
                    GEN.put(Xr)
                pT = BANK.get()
                pTb = pT[:, 0:512].bitcast(BF16)
                for t in range(8):
                    P.tr(pTb[:, t * 128:(t + 1) * 128], Xb[:, t * 128:(t + 1) * 128], ident)
                KT = GEN.get()
                P.copy(gb(KT, 1024), pTb[:, 0:1024], eng="act")
                BANK.put(pT)
                pT2 = BANK.get()
                pT2b = pT2[:, 0:512].bitcast(BF16)
                for t in range(8):
                    P.tr(pT2b[0:96, t * 128:(t + 1) * 128], gb(Xr3, 768)[:, t * 96:(t + 1) * 96], ident)
                KrT = GEN.get()
                P.copy(gb(KrT, 1024, 0, 96), pT2b[0:96, 0:1024], eng="dve")
                BANK.put(pT2)
                GEN.put(Xr3)
                return (Xb, KT, KrT)

            def st_b1(Xb, KT, KrT, QLq=QLq, QPq=QPq):
                psS = [BANK.get(), BANK.get()]
                for t in range(8):
                    bk = psS[t // 5]
                    c0 = (t % 5) * 96
                    P.mm(bk[:, c0:c0 + 96], gb(KT, 1024)[:, t * 128:(t + 1) * 128], QLq,
                         start=(t % 5 == 0), stop=False, skip_group_check=True)
                    P.mm(bk[:, c0:c0 + 96], gb(KrT, 1024, 0, 96)[:, t * 128:(t + 1) * 128], QPq,
                         start=False, stop=True, skip_group_check=True)
                GEN.put(KT, KrT)
                PT = GEN.get()
                P.act(gb(PT, 768)[:, 0:480], psS[0][:, 0:480], AF.Exp, scale=MLA_SCALE)
                P.act(gb(PT, 768)[:, 480:768], psS[1][:, 0:288], AF.Exp, scale=MLA_SCALE)
                BANK.put(*psS)
                P.tt(gb(PT, 768).rearrange("p (t c) -> p t c", t=8), gb(PT, 768).rearrange("p (t c) -> p t c", t=8),
                     pmask_b.unsqueeze(1).broadcast_to([128, 8, 96]), ALU.mult, eng="dve")
                return (Xb, PT)

            def st_b2(Xb, PT, psO=psO, fo=fo):
                for t in range(8):
                    P.mm(psO[0:96, 0:128], gb(PT, 768)[:, t * 96:(t + 1) * 96], Xb[:, t * 128:(t + 1) * 128],
                         start=fo[0], stop=False, skip_group_check=True)
                    fo[0] = False
                    P.mm(psO[0:96, 128:129], gb(PT, 768)[:, t * 96:(t + 1) * 96], ones_b[:, 0:1],
                         start=False, stop=False, skip_group_check=True)
                GEN.put(PT)

            gq, aq, bq = {}, [], []
            gq[0] = st_g(0)
            for it in range(16 + 2):
                if it % 2 == 0 and it // 2 + 1 < 8:
                    gq[it // 2 + 1] = st_g(it // 2 + 1)
                if it < 16:
                    X, Xr = gq[it // 2]
                    aq.append(st_a(it, X, Xr))
                if 1 <= it <= 16:
                    bq.append(st_b1(*aq.pop(0)))
                if it >= 2:
                    st_b2(*bq.pop(0))
            psN = BANK.get()
            P.mm(psN[:, 0:96], ckvT[:, S:S + 128], QLq, start=True, stop=False)
            P.mm(psN[:, 0:96], kpeT3[0:96, S:S + 128], QPq, start=False, stop=True)
            PN = GEN.get()
            P.act(gb(PN, 96), psN[:, 0:96], AF.Exp, scale=MLA_SCALE)
            BANK.put(psN)
            P.tt(gb(PN, 96), gb(PN, 96), nmask_b[:, q * 96:(q + 1) * 96], ALU.mult, eng="pool")
            P.mm(psO[0:96, 0:128], gb(PN, 96), gb(ckb_new, 128), start=False, stop=False, skip_group_check=True)
            P.mm(psO[0:96, 128:129], gb(PN, 96), ones_b[:, 0:1], start=False, stop=True, skip_group_check=True)
            GEN.put(PN)
            rd = GEN.get()
            P.recip(gf(rd, 1, 0, 96), psO[0:96, 128:129])
            ol = GEN.get()
            P.ts(gb(ol, 128, 0, 96), psO[0:96, 0:128], gf(rd, 1, 0, 96), None, ALU.mult)
            BANK.put(psO)
            GEN.put(rd)
            pT3 = BANK.get()
            pT3b = pT3[:, 0:256].bitcast(BF16)
            P.tr(pT3b[:, 0:96], gb(ol, 128, 0, 96), ident[0:96, 0:96])
            P.copy(gb(OL, 768).rearrange("p (h q t) -> p h q t", h=6, q=8)[:, :, q, :],
                   pT3b[:, 0:96].rearrange("p (h t) -> p h t", h=6), eng="act")
            BANK.put(pT3)
            GEN.put(ol)
        GEN.put(QL, QP, ckb_new)
        ot = [GEN.get() for _ in range(3)]
        oTv = [gb(ot[h // 2], 2 * T, 0, 64)[:, (h % 2) * T:(h % 2) * T + 128] for h in range(6)]
        for h in range(6):
            ps = BANK.get()
            P.mm(ps[0:64, 0:128], wsm[:, W_UV + h * 64:W_UV + (h + 1) * 64], gb(OL, 768)[:, h * 128:(h + 1) * 128],
                 start=True, stop=True)
            P.copy(oTv[h], ps[0:64, 0:128], eng="act")
            BANK.put(ps)
        GEN.put(OL)
        rmsnorm_fm(oTv, [cfc(l, O_MLAG + h, 1, 0, 64) for h in range(6)], 384, oTv, 64, 128)
        return ot, oTv

    def sample_block(l):
        N = 128
        src = xT_in if l == 0 else XT
        xv = xT[:].rearrange("p (k t) -> p k t", k=8)[:, :, 0:N]
        P.dma(xv, src.ap().rearrange("(k p) t -> p k t", p=128)[:, :, S:S + N], rkey="XT_%d" % S)
        rmsnorm_fm([xTk(k, 0, N) for k in range(8)], [cfc(l, O_GMIX + k) for k in range(8)], D,
                   [hTk(k, 0, N) for k in range(8)], 128, N)
        w1 = ring_next(0)
        uet = [GEN.get(), GEN.get()]
        uv = [gf(uet[c], 368).rearrange("p (b e) -> p b e", b=16) for c in range(2)]
        for c in range(2):
            ps = BANK.get()
            proj_fm(ps, w1, 512, c * 128, 128, N)
            P.dma(uv[c][:, :, 0:15], spool.ap()[l, c * 128:(c + 1) * 128, :, :])
            P.copy(uv[c][:, :, 15:23], ps[:, 0:N].rearrange("p (b i) -> p b i", b=16), eng="act")
            BANK.put(ps)
        psq = [BANK.get(), BANK.get()]
        for c in range(2):
            proj_fm(psq[c], w1, 512, 256 + c * 128, 128, N)
        cqn = [GEN.get(), GEN.get()]
        rmsnorm_fm([psq[c][:, 0:N] for c in range(2)], [cfc(l, O_QG + c) for c in range(2)], 256,
                   [gb(cqn[c], N) for c in range(2)], 128, N)
        BANK.put(*psq)
        m_bf = [GEN.get(), GEN.get()]
        for c in range(2):
            ue = uv[c]
            sel = GEN.get()
            t2 = GEN.get()
            se = gf(sel, 368).rearrange("p (b e) -> p b e", b=16)
            t2f = gf(t2, 368).rearrange("p (b e) -> p b e", b=16)
            E_ = 23
            if c == 0:
                P.tt(se[0:64, :, 1:E_], ue[0:64, :, 1:E_], ue[0:64, :, 0:E_ - 1], ALU.add, eng="pool")
                P.tt(t2f[64:128, :, 1:E_], ue[64:128, :, 1:E_], ue[64:128, :, 0:E_ - 1], ALU.add, eng="pool")
                P.tt(se[64:128, :, 3:E_], t2f[64:128, :, 3:E_], t2f[64:128, :, 1:E_ - 2], ALU.add, eng="pool")
            else:
                t4 = GEN.get()
                t4f = gf(t4, 368).rearrange("p (b e) -> p b e", b=16)
                P.tt(t2f[:, :, 1:E_], ue[:, :, 1:E_], ue[:, :, 0:E_ - 1], ALU.add, eng="pool")
                P.tt(t4f[:, :, 3:E_], t2f[:, :, 3:E_], t2f[:, :, 1:E_ - 2], ALU.add, eng="pool")
                P.tt(se[0:64, :, 7:E_], t4f[0:64, :, 7:E_], t4f[0:64, :, 3:E_ - 4], ALU.add, eng="pool")
                P.tt(t2f[64:128, :, 7:E_], t4f[64:128, :, 7:E_], t4f[64:128, :, 3:E_ - 4], ALU.add, eng="pool")
                P.tt(se[64:128, :, 15:E_], t2f[64:128, :, 15:E_], t2f[64:128, :, 7:E_ - 8], ALU.add, eng="pool")
                GEN.put(t4)
            P.stt(gb(m_bf[c], N).rearrange("p (b i) -> p b i", b=16), se[:, :, 15:E_],
                  cf[:, CG["INVW"] + c:CG["INVW"] + c + 1], ue[:, :, 15:E_], ALU.mult, ALU.subtract)
            GEN.put(sel, t2)
            outs.append(P.dma(o_spool.ap()[l, c * 128:(c + 1) * 128, :, :], ue[:, :, 8:23]))
            GEN.put(uet[c])
        w2 = ring_next(1)
        zs = [GEN.get() for _ in range(3)]
        for j in range(3):
            ps = BANK.get()
            proj_fm(ps, w2, 550, j * 128, 128, N)
            P.act(gb(zs[j], N), ps[:, 0:N], AF.Silu)
            BANK.put(ps)
        ptk = [BANK.get()]
        for k in range(8):
            P.mm(ptk[0][:, 0:166], hTk(k, 0, 128), w2[:, k * 550 + 384:k * 550 + 550], start=(k == 0), stop=(k == 7))
        ckb_new = tokmajor_post(l, S, 0, S // 128, 1, ptk, o_skv.ap()[l, :, :], o_skr.ap()[l, :, :], keep_ckb=True)
        BANK.put(*ptk)
        w3 = ring_next(2)
        xs_bf = [GEN.get() for _ in range(3)]
        BT = GEN.get()
        CT = [GEN.get(), GEN.get()]
        P.memset(gb(CT[0], N, 64, 128), 0.0)
        P.memset(gb(CT[1], N, 0, 64), 0.0)
        for j in range(5):
            ps = BANK.get()
            proj_fm(ps, w3, 640, j * 128, 128, N)
            xet = GEN.get()
            xe = gf(xet, 176).rearrange("p (b e) -> p b e", b=16)
            P.dma(xe[:, :, 0:3], sconv.ap()[l, j * 128:(j + 1) * 128, :, :])
            P.copy(xe[:, :, 3:11], ps[:, 0:N].rearrange("p (b i) -> p b i", b=16), eng="act")
            BANK.put(ps)
            acc = GEN.get()
            af_ = gf(acc, N)
            a3 = af_.rearrange("p (b i) -> p b i", b=16)
            P.act(a3, xe[:, :, 0:8], AF.Identity, scale=cfc(l, O_CW + j * 4), bias=cfc(l, O_CB + j))
            for k in range(1, 4):
                P.stt(a3, xe[:, :, k:k + 8], cfc(l, O_CW + j * 4 + k), a3, ALU.mult, ALU.add)
            if j < 4:
                dst = xs_bf[j] if j < 3 else BT
                P.act(gb(dst, N), af_, AF.Silu)
            else:
                P.act(gb(CT[0], N, 0, 64), af_[0:64, :], AF.Silu)
                P.act(gb(CT[1], N, 64, 128), af_[64:128, :], AF.Silu)
            GEN.put(acc)
            outs.append(P.dma(o_sconv.ap()[l, j * 128:(j + 1) * 128, :, :], xe[:, :, 8:11]))
            GEN.put(xet)
        qlat, qpe = q_path(l, S, N, cqn)
        GEN.put(*cqn)
        mla_tiles, mlan = attention_sample(l, qlat, qpe, ckb_new)
        yv = [GEN.get() for _ in range(3)]

        def yoff(py, CsT):
            CsX = [GEN.get(), GEN.get()]
            for g in range(2):
                pdup = BANK.get()
                P.mm(pdup[:, 0:384], dup_b[g], gb(CsT, 384), start=True, stop=True)
                P.tt(gb(CsX[g], 384).rearrange("p (h c) -> p h c", h=3), pdup[:, 0:384].rearrange("p (h c) -> p h c", h=3),
                     parL_b.unsqueeze(1).broadcast_to([128, 3, 128]), ALU.mult)
                BANK.put(pdup)
            for h in range(6):
                g, hh, j, e = h // 3, h % 3, h // 2, h % 2
                H0f = GEN.get()
                for b2 in range(2):
                    P.dma(gf(H0f, 512, 64 * b2, 64 * b2 + 64).rearrange("p (q c) -> p q c", q=8),
                          sssm.ap()[l, :, b2, h, :, :].rearrange("q n p -> n q p"))
                H0p = GEN.get()
                P.memset(gb(H0p, 1024), 0.0)
                P.copy(gb(H0p, 1024).rearrange("p (q c) -> p q c", q=8)[:, :, e * 64:(e + 1) * 64],
                       gf(H0f, 512).rearrange("p (q c) -> p q c", q=8), eng="pool")
                GEN.put(H0f)
                for q in range(8):
                    P.mm(py[:, j * 128 + 16 * q:j * 128 + 16 * q + 16], gb(H0p, 1024)[:, q * 128:(q + 1) * 128],
                         gb(CsX[g], 384)[:, hh * 128 + 16 * q:hh * 128 + 16 * q + 16],
                         start=False, stop=False, skip_group_check=True)
                GEN.put(H0p)
            GEN.put(*CsX)

        def states(btok, xdtw, Ex):
            Bpar = [GEN.get(), GEN.get()]
            for g in range(2):
                for b2 in range(2):
                    P.ts(gb(Bpar[g], 128)[:, b2 * 64:(b2 + 1) * 64], gb(btok, 128)[:, g * 64:(g + 1) * 64],
                         parS_f[:, b2:b2 + 1], None, ALU.mult)
            for h in range(6):
                g, hh = h // 3, h % 3
                xq = GEN.get()
                P.tt(gb(xq, 512).rearrange("p (q c) -> p q c", q=8),
                     gb(xdtw, 384)[:, h * 64:(h + 1) * 64].unsqueeze(1).broadcast_to([128, 8, 64]),
                     pairm_f.unsqueeze(2).broadcast_to([128, 8, 64]), ALU.mult)
                pst = BANK.get()
                P.mm(pst[:, 0:512], gb(Bpar[g], 128), gb(xq, 512), start=True, stop=True)
                GEN.put(xq)
                H0f = GEN.get()
                for b2 in range(2):
                    P.dma(gf(H0f, 512, 64 * b2, 64 * b2 + 64).rearrange("p (q c) -> p q c", q=8),
                          sssm.ap()[l, :, b2, h, :, :].rearrange("q n p -> n q p"))
                hn = GEN.get()
                for b2 in range(2):
                    r0, r1 = 64 * b2, 64 * b2 + 64
                    cdv = gf(Ex[g], 384, r0, r1)[:, hh * 128:(hh + 1) * 128].rearrange("p (q r) -> p q r", r=16)[:, :, 8 * b2 + 7]
                    P.tt(gf(hn, 512, r0, r1).rearrange("p (q c) -> p q c", q=8),
                         gf(H0f, 512, r0, r1).rearrange("p (q c) -> p q c", q=8),
                         cdv.unsqueeze(2).broadcast_to([64, 8, 64]), ALU.mult, eng="pool")
                P.tt(gf(hn, 512), gf(hn, 512), pst[:, 0:512], ALU.add)
                BANK.put(pst)
                GEN.put(H0f)
                for b2 in range(2):
                    outs.append(P.dma(o_sssm.ap()[l, :, b2, h, :, :].rearrange("q n p -> n q p"),
                                      gf(hn, 512, 64 * b2, 64 * b2 + 64).rearrange("p (q c) -> p q c", q=8)))
                GEN.put(hn)
            GEN.put(*Bpar)

        ssd_chunk(l, 0, 0, xs_bf, BT, CT, yv, (UB_f, LSB_f), sample={"yoff": yoff, "states": states})
        GEN.put(BT, *CT, *xs_bf)
        mixT = [GEN.get() for _ in range(5)]
        for j in range(3):
            P.tt(gf(yv[j], N), gf(yv[j], N), gb(zs[j], N), ALU.mult, eng="pool")
        GEN.put(*zs)
        rmsnorm_fm([gf(yv[j], N) for j in range(3)], [cfc(l, O_SSDG + j) for j in range(3)], 384,
                   [gb(mixT[2 + j], N) for j in range(3)], 128, N)
        GEN.put(*yv)
        for c in range(2):
            ps = BANK.get()
            P.mm(ps[:, 0:N], wsm[:, W_PW + c * 128:W_PW + (c + 1) * 128], gb(m_bf[c], N), start=True, stop=True)
            P.act(gb(mixT[c], N), ps[:, 0:N], AF.Copy, scale=cfc(l, O_PSC + c))
            BANK.put(ps)
        GEN.put(*m_bf)
        dense_tail(l, N, mixT, mlan, S)
        GEN.put(*mixT, *mla_tiles)

    try:
        for l in range(depth):
            for b in range(NB):
                prompt_block(l, b)
            if with_sample:
                sample_block(l)
    except StopBuild:
        pass
    P.finish(final_wait_ops=outs)
    return nc, P
def _f32(a):
    return np.ascontiguousarray(np.asarray(a, dtype=np.float32))


def _pad_piece(a):
    a = a.reshape(128, -1)
    out = np.zeros((128, SLOT), np.float32)
    out[:, :a.shape[1]] = a
    return out


def make_wimg(inp, depth):
    w_in, w_out, w_up, w_down = (_f32(inp[k]) for k in ("w_in", "w_out", "w_up", "w_down"))
    w_uq, w_uk, w_uv, pool_w = (_f32(inp[k]) for k in ("w_uq", "w_uk", "w_uv", "pool_w"))
    img = np.zeros((depth, NP, 128, SLOT), np.float32)

    def kmaj(a):
        K = a.shape[0] // 128
        return a.reshape(K, 128, a.shape[1]).transpose(1, 0, 2)

    c1 = list(range(0, 256)) + list(range(1286, 1542))
    c2 = list(range(256, 640)) + list(range(1542, 1702)) + list(range(1280, 1286))
    c3 = list(range(640, 1280))
    nope = [h * 96 + d for h in range(6) for d in range(64)]
    peA = [h * 96 + 64 + d for h in range(3) for d in range(32)]
    peB = [h * 96 + 64 + d for h in range(3, 6) for d in range(32)]
    swA = [h * 96 + 64 + (d + 16) % 32 for h in range(3) for d in range(32)]
    swB = [h * 96 + 64 + (d + 16) % 32 for h in range(3, 6) for d in range(32)]
    uqcols = nope + peA + peB + swA + swB
    for l in range(depth):
        img[l, 0] = _pad_piece(kmaj(w_in[l][:, c1]))
        img[l, 1] = _pad_piece(kmaj(w_in[l][:, c2]))
        img[l, 2] = _pad_piece(kmaj(w_in[l][:, c3]))
        for jm in range(4):
            blk = np.zeros((128, 11, 256), np.float32)
            mc = slice(jm * 256, (jm + 1) * 256)
            for kc in range(5):
                blk[:, kc, :] = w_out[l][kc * 128:(kc + 1) * 128, mc]
            for h in range(6):
                blk[0:64, 5 + h, :] = w_out[l][640 + h * 64:640 + (h + 1) * 64, mc]
            img[l, 3 + jm] = _pad_piece(blk)
        for i in range(8):
            img[l, 7 + 2 * i] = _pad_piece(kmaj(w_up[l][:, 512 * i:512 * (i + 1)]))
            img[l, 8 + 2 * i] = _pad_piece(kmaj(w_down[l][512 * i:512 * (i + 1), :]))
        sm = np.zeros((128, 2560), np.float32)
        sm[:, W_UQ:W_UQ + 1536] = kmaj(w_uq[l][:, uqcols]).reshape(128, 1536)
        wk = w_uk[l].reshape(128, 3, 2, 64).transpose(2, 3, 1, 0).reshape(128, 3 * 128)
        sm[:, W_UK:W_UK + 384] = wk
        sm[:, W_UV:W_UV + 384] = w_uv[l].reshape(128, 384)
        pw = np.zeros((128, 2, 128), np.float32)
        for c in range(2):
            for e in range(2):
                pw[e * 64:(e + 1) * 64, c, e * 64:(e + 1) * 64] = pool_w[l][2 * c + e]
        sm[:, W_PW:W_PW + 256] = pw.reshape(128, 256)
        img[l, 23] = _pad_piece(sm)
    return img.reshape(depth * NP * 128, SLOT)


def make_cf(inp, depth):
    CG = cf_globals(depth)
    cf = np.zeros((128, CG["NCF"]), np.float32)
    p = np.arange(128)

    def col8(v):
        return _f32(v).reshape(-1, 128).T

    for l in range(depth):
        o = l * CF_PER_LAYER
        cf[:, o + O_GMIX:o + O_GMIX + 8] = col8(inp["norm_mix_g"][l])
        cf[:, o + O_GFFN:o + O_GFFN + 8] = col8(inp["norm_ffn_g"][l])
        cf[:, o + O_PSC:o + O_PSC + 2] = col8(inp["pool_scale"][l])
        cw = _f32(inp["conv_w"][l])
        for j in range(5):
            for k in range(4):
                cf[:, o + O_CW + j * 4 + k] = cw[k, j * 128:(j + 1) * 128]
        cf[:, o + O_CB:o + O_CB + 5] = col8(inp["conv_b"][l])
        ds = _f32(inp["d_skip"][l])
        for j in range(3):
            cf[:, o + O_DSK + j] = ds[2 * j + p // 64]
        cf[:, o + O_SSDG:o + O_SSDG + 3] = col8(inp["ssd_norm_g"][l])
        cf[:, o + O_QG:o + O_QG + 2] = col8(inp["q_norm_g"][l])
        mg = _f32(inp["mla_out_g"][l]).reshape(6, 64)
        cf[0:64, o + O_MLAG:o + O_MLAG + 6] = mg.T
        cf[:, o + O_KVG:o + O_KVG + 128] = _f32(inp["kv_norm_g"][l])[None, :]
        cf[:, o + O_DTB:o + O_DTB + 6] = _f32(inp["dt_bias"][l])[None, :]
        cf[:, o + O_ALOG:o + O_ALOG + 6] = _f32(inp["a_log"][l])[None, :]
    cf[:, CG["FING"]:CG["FING"] + 8] = col8(inp["final_norm_g"])
    for c in range(2):
        w = np.array([POOL_W[2 * c + q // 64] for q in range(128)], np.float32)
        cf[:, CG["INVW"] + c] = 1.0 / w
        for t in range(15):
            cf[:, CG["INVC"] + c * 15 + t] = 1.0 / np.minimum(w, t + 1)
    r = np.arange(128)[:, None]
    c_ = np.arange(128)[None, :]
    cf[:, CG["U"]:CG["U"] + 128] = (r <= c_)
    cf[:, CG["LS"]:CG["LS"] + 128] = (r > c_)
    same = (r // 8 == c_ // 8)
    cf[:, CG["UB"]:CG["UB"] + 128] = (r <= c_) & same
    cf[:, CG["LSB"]:CG["LSB"] + 128] = (r > c_) & same
    cf[:, CG["ONES"]:CG["ONES"] + 128] = 1.0
    cf[:, CG["SEQM"]:CG["SEQM"] + 16] = (np.arange(128)[:, None] // 8 == np.arange(16)[None, :])
    sidx = np.arange(128)[:, None]
    cf[:, CG["PARS"]:CG["PARS"] + 2] = ((sidx // 8) % 2 == np.arange(2)[None, :])
    cf[:, CG["PAIRM"]:CG["PAIRM"] + 8] = (sidx // 16 == np.arange(8)[None, :])
    cf[:, CG["LASTM"]:CG["LASTM"] + 128] = (np.arange(128)[None, :] == 8 * (sidx // 8) + 7)
    return cf


def make_cb():
    cb = np.zeros((128, NCB), np.float32)
    r = np.arange(128)[:, None]
    c_ = np.arange(128)[None, :]
    cb[:, CB_ID:CB_ID + 128] = np.eye(128)
    cb[:, CB_ONES:CB_ONES + 128] = 1.0
    cb[:, CB_U:CB_U + 128] = (r <= c_)
    p = np.arange(128)
    col = np.arange(96)
    cb[:, CB_PMASK:CB_PMASK + 96] = ((p[:, None] // 64) == ((col[None, :] % 16) // 8))
    nm = np.zeros((128, 8, 96), np.float32)
    bq, iq = p // 8, p % 8
    for q in range(8):
        b2c, ic = (col % 16) // 8, col % 8
        nm[:, q, :] = (bq[:, None] == 2 * q + b2c[None, :]) & (iq[:, None] <= ic[None, :])
    cb[:, CB_NMASK:CB_NMASK + 768] = nm.reshape(128, 768)
    for g in range(2):
        dup = np.zeros((128, 128), np.float32)
        for b2 in range(2):
            for n in range(64):
                dup[64 * g + n, b2 * 64 + n] = 1.0
        off = CB_DUP0 if g == 0 else CB_DUP1
        cb[:, off:off + 128] = dup
    cb[:, CB_PARL:CB_PARL + 128] = ((c_ // 8) % 2 == (r // 64))
    return cb.astype(ml_dtypes.bfloat16)


def sample_inputs(inp, c, depth):
    m = {}
    ckv = np.asarray(inp["cache_kv_latent"])
    ckr = np.asarray(inp["cache_k_rope"])
    n_phys = ckv.shape[1]
    for l in range(depth):
        m["ckvc%d" % l] = ckv[l].reshape(n_phys, 16384)
        m["ckrc%d" % l] = ckr[l].reshape(n_phys, 4096)
    pt = np.asarray(inp["page_table"])[16 * c:16 * c + 16].astype(np.int32)
    m["pt"] = np.ascontiguousarray(pt.reshape(8, 2, 64).transpose(1, 2, 0).reshape(128, 8))
    sp = _f32(inp["state_pool"])[:depth, 16 * c:16 * c + 16]
    m["spool"] = np.ascontiguousarray(sp.transpose(0, 3, 1, 2))
    sc = _f32(inp["state_conv"])[:depth, 16 * c:16 * c + 16]
    m["sconv"] = np.ascontiguousarray(sc.transpose(0, 3, 1, 2))
    ss = _f32(inp["state_ssm"])[:depth, 16 * c:16 * c + 16]
    m["sssm"] = np.ascontiguousarray(ss.transpose(0, 1, 2, 4, 3).reshape(depth, 8, 2, 6, 64, 64))
    return m


def sample_outputs(res, depth):
    s_kv = np.concatenate([res[c]["o_skv"].reshape(depth, 16, 8, 128) for c in range(8)], axis=1)
    s_kr = np.concatenate([res[c]["o_skr"].reshape(depth, 16, 8, 32) for c in range(8)], axis=1)
    s_pool = np.concatenate([res[c]["o_spool"].transpose(0, 2, 3, 1) for c in range(8)], axis=1)
    s_conv = np.concatenate([res[c]["o_sconv"].transpose(0, 2, 3, 1) for c in range(8)], axis=1)
    s_ssm = np.concatenate([res[c]["o_sssm"].reshape(depth, 16, 6, 64, 64).transpose(0, 1, 2, 4, 3) for c in range(8)], axis=1)
    return [s_kv, s_kr, s_pool, s_conv, s_ssm]


def make_rope(S, past_len):
    NTOK = S + 128
    NTILE = S // 128 + 1
    inv = (np.float32(10000.0) ** (-np.arange(16, dtype=np.float32) * np.float32(2.0 / 32))).astype(np.float32)
    pos = np.concatenate([np.arange(S), past_len + (np.arange(128) % 8)]).astype(np.float32)
    ang = (pos[:, None] * inv[None, :]).astype(np.float32)
    cos = np.cos(ang).astype(np.float32)
    sin = np.sin(ang).astype(np.float32)
    ropeq = np.zeros((192, NTOK), np.float32)
    for hh in range(3):
        for d in range(32):
            ropeq[hh * 32 + d] = cos[:, d % 16]
            ropeq[96 + hh * 32 + d] = sin[:, d % 16] * (-1.0 if d < 16 else 1.0)
    ropek = np.zeros((128, 2, NTILE, 16), np.float32)
    ropek[:, 0] = cos.reshape(NTILE, 128, 16).transpose(1, 0, 2)
    ropek[:, 1] = sin.reshape(NTILE, 128, 16).transpose(1, 0, 2)
    return ropeq, ropek.reshape(128, 2 * NTILE * 16)


_PROG_CACHE = {}


def kernel(**inp):
    x_prompt = _f32(inp["x_prompt"])
    x_sample = _f32(inp["x_sample"])
    S = x_prompt.shape[1]
    depth = inp["w_in"].shape[0]
    page_table = np.asarray(inp["page_table"])
    n_pages = page_table.shape[1]
    n_phys = inp["cache_kv_latent"].shape[1]
    past_len = n_pages * 128
    key = (S, depth, n_pages, n_phys, WITH_SAMPLE)
    if key not in _PROG_CACHE:
        _PROG_CACHE[key] = build_program(S, depth, n_pages, n_phys, with_sample=WITH_SAMPLE)
    nc, _ = _PROG_CACHE[key]
    wimg = make_wimg(inp, depth)
    cf = make_cf(inp, depth)
    cb = make_cb()
    ropeq, ropek = make_rope(S, past_len)
    in_maps = []
    for c in range(8):
        xs = x_sample[16 * c:16 * c + 16].reshape(128, D)
        xT_in = np.ascontiguousarray(np.concatenate([x_prompt[c % 4], xs], axis=0).T)
        m = dict(xT_in=xT_in, wimg=wimg, cf=cf, cb=cb, ropeq=ropeq, ropek=ropek)
        if WITH_SAMPLE:
            m.update(sample_inputs(inp, c, depth))
        in_maps.append(m)
    res = run_bass_kernel_spmd(nc, in_maps, core_ids=list(range(8))).results
    B = x_prompt.shape[0]
    y_prompt = np.stack([res[s]["yT"][:, :S].T for s in range(B)])
    y_sample = np.concatenate([res[c]["yT"][:, S:].T.reshape(16, 8, D) for c in range(8)], axis=0)
    p_kv = np.stack([np.stack([res[s]["o_pkv"][l] for s in range(B)]) for l in range(depth)])
    p_kr = np.stack([np.stack([res[s]["o_pkr"][l] for s in range(B)]) for l in range(depth)])
    p_pool = np.stack([np.stack([res[s]["o_ppool"][l].T for s in range(B)]) for l in range(depth)])
    p_conv = np.stack([np.stack([res[s]["o_pconv"][l].T for s in range(B)]) for l in range(depth)])
    p_ssm = np.stack([np.stack([res[s]["o_pssm"][l].transpose(0, 2, 1) for s in range(B)]) for l in range(depth)])
    outs = [y_prompt, y_sample, p_kv, p_kr, p_pool, p_conv, p_ssm]
    if WITH_SAMPLE:
        outs += sample_outputs(res, depth)
    return tuple(np.ascontiguousarray(o, dtype=np.float32) for o in outs)
```
